# Optimizing a Trainium2 kernel written in Bass

```python
import math
import jax, jax.numpy as jnp
from jax import lax
import numpy as np

D_MODEL = 1024
BATCH = 8
SEQ = 2048
DEPTH = 2
DEC_BATCH = 32
DEC_SEQ = 1
PAST_LEN = 8192
PAGE_SIZE = 128

N_MIXERS = 2
N_META = 16
GLA_HEADS = 4
GLA_DK = D_MODEL // 2 // GLA_HEADS
GLA_DV = D_MODEL // GLA_HEADS
GLA_RANK = 16
GLA_TAU = 16.0
GLA_CHUNK = 64
GLA_IN = 2 * GLA_HEADS * GLA_DK + 2 * GLA_HEADS * GLA_DV + GLA_RANK
DIFF_HEADS = 8
DIFF_DH = D_MODEL // (2 * DIFF_HEADS)
DIFF_DV = 2 * DIFF_DH
DIFF_IN = 2 * (DIFF_HEADS * 2 * DIFF_DH) + DIFF_HEADS * DIFF_DV
ROT_DIM = DIFF_DH // 4
ROPE_THETA = 500000.0
Q_BLOCK = 128
D_FF = 4 * D_MODEL
LN_EPS = 1e-5
DEEPNORM_ALPHA = (2 * DEPTH) ** 0.25
DEEPNORM_BETA = (8 * DEPTH) ** -0.25
N_GLA_LAYERS = (DEPTH + 1) // 2
N_DIFF_LAYERS = DEPTH // 2

kernel_name = "gla_diffattn_hybrid_step"

F32 = jnp.float32


def layer_norm(x, g, b):
    xf = x.astype(F32)
    mu = jnp.mean(xf, axis=-1, keepdims=True)
    xc = xf - mu
    var = jnp.mean(xc * xc, axis=-1, keepdims=True)
    return (xc * lax.rsqrt(var + LN_EPS) * g.astype(F32) + b.astype(F32)).astype(x.dtype)


def rms_norm(x, g):
    xf = x.astype(F32)
    return xf * lax.rsqrt(jnp.mean(xf * xf, axis=-1, keepdims=True) + LN_EPS) * g.astype(F32)


def post_norm(x, f, g, b):
    return layer_norm(DEEPNORM_ALPHA * x + f, g, b)


def sq_relu_mlp(x, w1, w2):
    h = jax.nn.relu(x @ w1)
    return (h * h) @ w2


def gla_project(x, w_in, w_gate, b_gate):
    B, T, _ = x.shape
    h = x @ w_in
    qk = GLA_HEADS * GLA_DK
    vv = GLA_HEADS * GLA_DV
    q, k, v, r, g_low = jnp.split(h, [qk, 2 * qk, 2 * qk + vv, 2 * qk + 2 * vv], axis=-1)
    q = q.reshape(B, T, GLA_HEADS, GLA_DK).astype(F32) * (GLA_DK ** -0.5)
    k = k.reshape(B, T, GLA_HEADS, GLA_DK).astype(F32)
    v = v.reshape(B, T, GLA_HEADS, GLA_DV).astype(F32)
    lg = jax.nn.log_sigmoid((g_low @ w_gate + b_gate).astype(F32)) / GLA_TAU
    lg = lg.reshape(B, T, GLA_HEADS, GLA_DK)
    return q, k, v, lg, r


def gla_chunk(state, q, k, v, lg):
    L = q.shape[1]
    b = jnp.cumsum(lg, axis=1)
    o_inter = jnp.einsum('blhd,bhde->blhe', q * jnp.exp(b), state)
    causal = jnp.tril(jnp.ones((L, L), bool))[None, :, :, None, None]
    diff = b[:, :, None] - b[:, None, :]
    decay = jnp.where(causal, jnp.exp(jnp.where(causal, diff, 0.0)), 0.0)
    scores = jnp.einsum('bihd,bjhd,bijhd->bhij', q, k, decay)
    o_intra = jnp.einsum('bhij,bjhe->bihe', scores, v)
    b_last = b[:, -1]
    new_state = jnp.exp(b_last)[..., None] * state + jnp.einsum(
        'blhd,blhe->bhde', k * jnp.exp(b_last[:, None] - b), v)
    return new_state, o_inter + o_intra


def gla_output(o, r, g_norm, w_out, dtype):
    B, T = o.shape[:2]
    o = rms_norm(o, g_norm).reshape(B, T, GLA_HEADS * GLA_DV)
    o = o * jax.nn.silu(r.astype(F32))
    return o.astype(dtype) @ w_out


def gla_prompt(x, w_in, w_gate, b_gate, g_norm, w_out):
    B, T, _ = x.shape
    q, k, v, lg, r = gla_project(x, w_in, w_gate, b_gate)
    s0 = jnp.zeros((B, GLA_HEADS, GLA_DK, GLA_DV), F32)
    s_meta, o_meta = gla_chunk(s0, q[:, :N_META], k[:, :N_META], v[:, :N_META], lg[:, :N_META])
    n_chunks = (T - N_META) // GLA_CHUNK

    def to_chunks(a):
        return a[:, N_META:].reshape(B, n_chunks, GLA_CHUNK, *a.shape[2:]).swapaxes(0, 1)

    def step(s, c):
        return gla_chunk(s, *c)

    s_fin, o_rest = lax.scan(step, s_meta, (to_chunks(q), to_chunks(k), to_chunks(v), to_chunks(lg)))
    o_rest = o_rest.swapaxes(0, 1).reshape(B, T - N_META, GLA_HEADS, GLA_DV)
    o = jnp.concatenate([o_meta, o_rest], axis=1)
    return gla_output(o, r, g_norm, w_out, x.dtype), s_fin


def gla_sample(x, state, w_in, w_gate, b_gate, g_norm, w_out):
    q, k, v, lg, r = gla_project(x, w_in, w_gate, b_gate)
    s_new, o = gla_chunk(state.astype(F32), q, k, v, lg)
    return gla_output(o, r, g_norm, w_out, x.dtype), s_new


def rope_partial(x, pos):
    inv = ROPE_THETA ** (-jnp.arange(0, ROT_DIM, 2, dtype=F32) / ROT_DIM)
    ang = pos.astype(F32)[:, None] * inv[None, :]
    cos = jnp.cos(ang)[None, :, None, None, :]
    sin = jnp.sin(ang)[None, :, None, None, :]
    xr = x[..., :ROT_DIM].astype(F32)
    x1, x2 = xr[..., :ROT_DIM // 2], xr[..., ROT_DIM // 2:]
    rot = jnp.concatenate([x1 * cos - x2 * sin, x2 * cos + x1 * sin], axis=-1)
    return jnp.concatenate([rot.astype(x.dtype), x[..., ROT_DIM:]], axis=-1)


def diff_project(x, w_in, pos):
    B, T, _ = x.shape
    nq = DIFF_HEADS * 2 * DIFF_DH
    q, k, v = jnp.split(x @ w_in, [nq, 2 * nq], axis=-1)
    q = rope_partial(q.reshape(B, T, DIFF_HEADS, 2, DIFF_DH), pos)
    k = rope_partial(k.reshape(B, T, DIFF_HEADS, 2, DIFF_DH), pos)
    v = v.reshape(B, T, DIFF_HEADS, DIFF_DV)
    return q, k, v


def diff_core(q, k, v, q_pos, k_pos, lam):
    s = jnp.einsum('bqhmd,bkhmd->bhmqk', q, k).astype(F32) * (DIFF_DH ** -0.5)
    mask = k_pos[None, :] <= q_pos[:, None]
    s = jnp.where(mask, s, -jnp.inf)
    p = jax.nn.softmax(s, axis=-1)
    a = p[:, :, 0] - lam * p[:, :, 1]
    return jnp.einsum('bhqk,bkhe->bqhe', a, v.astype(F32))


def diff_output(o, g_sub, lam_init, w_out, dtype):
    B, T = o.shape[:2]
    o = rms_norm(o, g_sub) * (1.0 - lam_init)
    return o.reshape(B, T, DIFF_HEADS * DIFF_DV).astype(dtype) @ w_out


def diff_lambda_value(lam_params, lam_init):
    lp = lam_params.astype(F32)
    return jnp.exp(jnp.sum(lp[0] * lp[1])) - jnp.exp(jnp.sum(lp[2] * lp[3])) + lam_init


def pad_time(a, n):
    return jnp.pad(a, [(0, 0), (0, n)] + [(0, 0)] * (a.ndim - 2))


def diff_prompt(x, w_in, lam, lam_init, g_sub, w_out):
    B, T, _ = x.shape
    pos = jnp.arange(T)
    q, k, v = diff_project(x, w_in, pos)
    n_blocks = -(-T // Q_BLOCK)
    t_pad = n_blocks * Q_BLOCK
    qp, kp, vp = pad_time(q, t_pad - T), pad_time(k, t_pad - T), pad_time(v, t_pad - T)
    all_pos = jnp.arange(t_pad)
    q_blocks = qp.reshape(B, n_blocks, Q_BLOCK, DIFF_HEADS, 2, DIFF_DH).swapaxes(0, 1)
    pos_blocks = all_pos.reshape(n_blocks, Q_BLOCK)
    o = lax.map(lambda c: diff_core(c[0], kp, vp, c[1], all_pos, lam), (q_blocks, pos_blocks))
    o = o.swapaxes(0, 1).reshape(B, t_pad, DIFF_HEADS, DIFF_DV)[:, :T]
    out = diff_output(o, g_sub, lam_init, w_out, x.dtype)
    return out, k.reshape(B, T, DIFF_HEADS, 2 * DIFF_DH), v


def diff_sample(x, cache_k, cache_v, page_table, w_in, lam, lam_init, g_sub, w_out):
    B, T, _ = x.shape
    past = page_table.shape[1] * PAGE_SIZE
    pos = past + jnp.arange(T)
    q, k, v = diff_project(x, w_in, pos)
    k_past = cache_k[page_table].reshape(B, past, DIFF_HEADS, 2, DIFF_DH).astype(k.dtype)
    v_past = cache_v[page_table].reshape(B, past, DIFF_HEADS, DIFF_DV).astype(v.dtype)
    k_all = jnp.concatenate([k_past, k], axis=1)
    v_all = jnp.concatenate([v_past, v], axis=1)
    o = diff_core(q, k_all, v_all, pos, jnp.arange(past + T), lam)
    out = diff_output(o, g_sub, lam_init, w_out, x.dtype)
    return out, k.reshape(B, T, DIFF_HEADS, 2 * DIFF_DH), v


def setup_inputs(seed: int = 0) -> dict:
    key = jax.random.key(seed)
    ks = jax.random.split(key, 24)
    n_pages = PAST_LEN // PAGE_SIZE
    n_used = DEC_BATCH * n_pages
    n_pool = (n_used * 5) // 4
    nrm = jax.random.normal
    x_prompt = nrm(ks[0], (BATCH, SEQ, D_MODEL), F32)
    x_sample = nrm(ks[1], (DEC_BATCH, DEC_SEQ, D_MODEL), F32)
    state_gla = nrm(ks[2], (N_GLA_LAYERS, DEC_BATCH, GLA_HEADS, GLA_DK, GLA_DV), F32)
    cache_k = nrm(ks[3], (N_DIFF_LAYERS, n_pool, PAGE_SIZE, DIFF_HEADS, 2 * DIFF_DH), F32)
    cache_v = nrm(ks[4], (N_DIFF_LAYERS, n_pool, PAGE_SIZE, DIFF_HEADS, DIFF_DV), F32)
    page_table = jax.random.permutation(ks[5], n_pool)[:n_used].reshape(DEC_BATCH, n_pages).astype(jnp.int32)
    meta_tokens = nrm(ks[6], (N_META, D_MODEL), F32)
    gla_w_in = nrm(ks[7], (N_GLA_LAYERS, D_MODEL, GLA_IN), F32) * D_MODEL ** -0.5
    gla_w_gate = nrm(ks[8], (N_GLA_LAYERS, GLA_RANK, GLA_HEADS * GLA_DK), F32) * GLA_RANK ** -0.5
    gla_b_gate = 0.1 * nrm(ks[9], (N_GLA_LAYERS, GLA_HEADS * GLA_DK), F32)
    gla_norm = 1.0 + 0.01 * nrm(ks[10], (N_GLA_LAYERS, GLA_DV), F32)
    gla_w_out = nrm(ks[11], (N_GLA_LAYERS, GLA_HEADS * GLA_DV, D_MODEL), F32) * (GLA_HEADS * GLA_DV) ** -0.5 * DEEPNORM_BETA
    diff_w_in = nrm(ks[12], (N_DIFF_LAYERS, D_MODEL, DIFF_IN), F32) * D_MODEL ** -0.5
    diff_lambda = 0.1 * nrm(ks[13], (N_DIFF_LAYERS, 4, DIFF_DH), F32)
    diff_norm = 1.0 + 0.01 * nrm(ks[14], (N_DIFF_LAYERS, DIFF_DV), F32)
    diff_w_out = nrm(ks[15], (N_DIFF_LAYERS, DIFF_HEADS * DIFF_DV, D_MODEL), F32) * (DIFF_HEADS * DIFF_DV) ** -0.5 * DEEPNORM_BETA
    mlp_w1 = nrm(ks[16], (DEPTH, D_MODEL, D_FF), F32) * D_MODEL ** -0.5
    mlp_w2 = nrm(ks[17], (DEPTH, D_FF, D_MODEL), F32) * D_FF ** -0.5 * DEEPNORM_BETA
    ln_mix_g = 1.0 + 0.01 * nrm(ks[18], (DEPTH, D_MODEL), F32)
    ln_mix_b = 0.01 * nrm(ks[19], (DEPTH, D_MODEL), F32)
    ln_mlp_g = 1.0 + 0.01 * nrm(ks[20], (DEPTH, D_MODEL), F32)
    ln_mlp_b = 0.01 * nrm(ks[21], (DEPTH, D_MODEL), F32)
    return {"x_prompt": x_prompt, "x_sample": x_sample, "state_gla": state_gla,
            "cache_k": cache_k, "cache_v": cache_v, "page_table": page_table,
            "meta_tokens": meta_tokens, "gla_w_in": gla_w_in, "gla_w_gate": gla_w_gate,
            "gla_b_gate": gla_b_gate, "gla_norm": gla_norm, "gla_w_out": gla_w_out,
            "diff_w_in": diff_w_in, "diff_lambda": diff_lambda, "diff_norm": diff_norm,
            "diff_w_out": diff_w_out, "mlp_w1": mlp_w1, "mlp_w2": mlp_w2,
            "ln_mix_g": ln_mix_g, "ln_mix_b": ln_mix_b, "ln_mlp_g": ln_mlp_g, "ln_mlp_b": ln_mlp_b}


def reference(x_prompt, x_sample, state_gla, cache_k, cache_v, page_table, meta_tokens,
              gla_w_in, gla_w_gate, gla_b_gate, gla_norm, gla_w_out,
              diff_w_in, diff_lambda, diff_norm, diff_w_out,
              mlp_w1, mlp_w2, ln_mix_g, ln_mix_b, ln_mlp_g, ln_mlp_b):
    B = x_prompt.shape[0]
    meta = jnp.broadcast_to(meta_tokens[None].astype(x_prompt.dtype), (B, N_META, D_MODEL))
    x_p = jnp.concatenate([meta, x_prompt], axis=1)
    x_s = x_sample
    gla_sp, gla_ss, k_p, v_p, k_s, v_s = [], [], [], [], [], []
    for layer in range(DEPTH):
        j = layer // N_MIXERS
        if layer % N_MIXERS == 0:
            f_p, st_p = gla_prompt(x_p, gla_w_in[j], gla_w_gate[j], gla_b_gate[j], gla_norm[j], gla_w_out[j])
            f_s, st_s = gla_sample(x_s, state_gla[j], gla_w_in[j], gla_w_gate[j], gla_b_gate[j], gla_norm[j], gla_w_out[j])
            gla_sp.append(st_p)
            gla_ss.append(st_s)
        else:
            lam_init = 0.8 - 0.6 * math.exp(-0.3 * layer)
            lam = diff_lambda_value(diff_lambda[j], lam_init)
            f_p, kp_rows, vp_rows = diff_prompt(x_p, diff_w_in[j], lam, lam_init, diff_norm[j], diff_w_out[j])
            f_s, ks_rows, vs_rows = diff_sample(x_s, cache_k[j], cache_v[j], page_table, diff_w_in[j],
                                                lam, lam_init, diff_norm[j], diff_w_out[j])
            k_p.append(kp_rows)
            v_p.append(vp_rows)
            k_s.append(ks_rows)
            v_s.append(vs_rows)
        x_p = post_norm(x_p, f_p, ln_mix_g[layer], ln_mix_b[layer])
        x_s = post_norm(x_s, f_s, ln_mix_g[layer], ln_mix_b[layer])
        x_p = post_norm(x_p, sq_relu_mlp(x_p, mlp_w1[layer], mlp_w2[layer]), ln_mlp_g[layer], ln_mlp_b[layer])
        x_s = post_norm(x_s, sq_relu_mlp(x_s, mlp_w1[layer], mlp_w2[layer]), ln_mlp_g[layer], ln_mlp_b[layer])
    y_prompt = x_p[:, N_META:]
    return (y_prompt, x_s, jnp.stack(gla_sp), jnp.stack(gla_ss),
            jnp.stack(k_p), jnp.stack(v_p), jnp.stack(k_s), jnp.stack(v_s))
```

```python
import math
import os
from contextlib import ExitStack
import numpy as np
import concourse.bass as bass
import concourse.mybir as mybir
from concourse.bass_utils import run_bass_kernel_spmd

F32 = mybir.dt.float32
BF16 = mybir.dt.bfloat16
I32 = mybir.dt.int32
AF = mybir.ActivationFunctionType
ALU = mybir.AluOpType

D = 1024
NKC = 8
NMETA = 16
NSLOT = 32
C0P = NMETA + NSLOT
SEQ = 2048
NT = C0P + SEQ
NCORES = 8
DEPTH = 2
ALPHA = (2 * DEPTH) ** 0.25
EPS = 1e-5
PAGE = 128
NPAGES = 64
NPOOL = 2560
EPOCH = 4000
TILES = [(0, C0P)] + [(C0P + 512 * i, 512) for i in range(SEQ // 512)]
GT = 256
GTILES = [(0, C0P)] + [(C0P + GT * i, GT) for i in range(SEQ // GT)]
LAM_INIT = 0.8 - 0.6 * math.exp(-0.3 * 1)
DEBUG = False


class Buf:
    __slots__ = ("name", "w", "r")

    def __init__(self, name):
        self.name = name
        self.w = None
        self.r = {}


class SemRec:
    def __init__(self, sem):
        self.sem = sem
        self.val = 0


class Sched:
    def __init__(self, nc, stack):
        self.nc = nc
        self.stack = stack
        self.eng = {"pe": nc.tensor, "act": nc.scalar, "dve": nc.vector, "pool": nc.gpsimd, "sp": nc.sync}
        self.cnt = {k: 0 for k in self.eng}
        self.sems = {k: [] for k in self.eng}
        self.waited = {k: {} for k in self.eng}
        self.recs = []
        self.nsem = 0

    def new_sem(self, name):
        self.nsem += 1
        return self.stack.enter_context(self.nc.semaphore(name))

    def rec(self, name):
        r = SemRec(self.new_sem("d_" + name))
        self.recs.append(r)
        return r

    def esem(self, e, i):
        ep = (i - 1) // EPOCH
        while len(self.sems[e]) <= ep:
            self.sems[e].append(self.new_sem(f"e_{e}_{len(self.sems[e])}"))
        return self.sems[e][ep], (i - 1) % EPOCH + 1

    def _wait(self, e, dep):
        if dep[0] == "e":
            _, e2, i = dep
            if e2 == e and e == "pe":
                return
            if self.waited[e].get(e2, 0) >= i:
                return
            sem, val = self.esem(e2, i)
            self.eng[e].wait_ge(sem, val)
            self.waited[e][e2] = i
        else:
            _, rec, val = dep
            if self.waited[e].get(id(rec), 0) >= val:
                return
            self.eng[e].wait_ge(rec.sem, val)
            self.waited[e][id(rec)] = val

    def _deps(self, e, r, w):
        for b in r:
            if b.w is not None:
                self._wait(e, b.w)
        for b in w:
            if b.w is not None:
                self._wait(e, b.w)
            for k, d in b.r.items():
                if k != e:
                    self._wait(e, d)

    def op(self, e, fn, r=(), w=(), sig=True):
        self._deps(e, r, w)
        ins = fn(self.eng[e])
        nid = self.cnt[e] + 1
        if sig:
            self.cnt[e] = nid
            sem, _ = self.esem(e, nid)
            ins.then_inc(sem, 1)
        me = ("e", e, nid)
        for b in r:
            b.r[e] = me
        for b in w:
            b.w = me
            b.r = {}
        return ins

    def dma(self, q, out, in_, rec, r=(), w=(), **kw):
        self._deps(q, r, w)
        ins = self.eng[q].dma_start(out=out, in_=in_, **kw)
        rec.val += 16
        ins.then_inc(rec.sem, 16)
        me = ("d", rec, rec.val)
        for b in r:
            b.r["d%d" % id(rec)] = me
        for b in w:
            b.w = me
            b.r = {}
        return ins

    def barrier(self):
        for e in self.eng:
            for e2 in self.eng:
                if e2 != e and self.cnt[e2] > 0:
                    self._wait(e, ("e", e2, self.cnt[e2]))
            for rec in self.recs:
                if rec.val > 0:
                    self._wait(e, ("d", rec, rec.val))


class T:
    def __init__(self, t, name):
        self.t = t
        self.b = Buf(name)

    def __getitem__(self, k):
        return self.t[k]


def build(ncores, npool, with_sample, tail=False):
    nc = bass.Bass("TRN2", target_bir_lowering=False)
    dram = {}
    ntot = 32 if tail else NT
    tiles = [(0, 32)] if tail else TILES

    def din(name, shape, dt=F32):
        dram[name] = nc.dram_tensor(name, list(shape), dt, kind="ExternalInput").ap()
        return dram[name]

    def dout(name, shape, dt=F32):
        dram[name] = nc.dram_tensor(name, list(shape), dt, kind="ExternalOutput").ap()
        return dram[name]

    xT = din("xT", [128, NKC, ntot])
    gla_w_gate = din("gla_w_gate", [16, 512])
    diff_w_out = din("diff_w_out", [D, D])
    mlp_w1 = din("mlp_w1", [2, D, 4 * D])
    mlp_w2 = din("mlp_w2", [2, 4 * D, D])
    pp_d = din("pp", [128, 80])
    cf_d = din("cf", [128, 10, 128])
    smask_d = din("smask", [128, GT + C0P])
    yT = dout("yT", [128, NKC, ntot])
    og_all_d = din("og_all", [128, 8, 32]) if tail else None
    if not tail:
        state_own = din("state_all", [32, 4, 128, 256])
        gla_w_in = din("gla_w_in", [D, 3088])
        gla_w_out = din("gla_w_out", [D, D])
        diff_w_in = din("diff_w_in", [D, 3072])
        lam_d = din("lam", [1, 256])
        cos_d = din("rcos", [128, NT])
        sin_d = din("rsin", [128, NT])
        wqkv_c = din("wqkv_c", [D, 3, 128])
        cache_d = din("cache", [npool * 128, 256])
        pt_d = din("ptab", [1, NSLOT * NPAGES], I32)
        gsp = dout("gla_sp", [4, 128, 256])
        gss = dout("gla_ss", [32, 4, 128, 256])
        xmid_o = dout("xs_mid", [128, NKC, 32])
        ogs_o = dout("ogs", [128, 32])
        kTo = dout("kT_rows", [128, 8, NT])
        vro = dout("v_rows", [NT, D])
    dbg = [dout(f"dbg{i}", [128, NKC, NT]) for i in range(2)] if (DEBUG and not tail) else []

    with ExitStack() as top:
        S = Sched(nc, top)
        uid = [0]

        def sb(st, name, shape, dt=F32):
            uid[0] += 1
            return T(st.enter_context(nc.sbuf_tensor("s%d_" % uid[0] + name, list(shape), dt)), name)

        def pst(st, name, shape=(128, 512), dt=F32):
            uid[0] += 1
            return T(st.enter_context(nc.psum_tensor("p%d_" % uid[0] + name, list(shape), dt)), name)

        def bufs(ts):
            return [t.b for t in ts]

        def mm(out, lhsT, rhs, r, w, start=True, stop=True, sig=True):
            return S.op("pe", lambda e: e.matmul(out, lhsT=lhsT, rhs=rhs, start=start, stop=stop),
                        r=bufs(r), w=bufs(w), sig=sig)

        def act(out, in_, func, r, w, bias=None, scale=None):
            kw = {}
            if bias is not None:
                kw["bias"] = bias
            if scale is not None:
                kw["scale"] = scale
            return S.op("act", lambda e: e.activation(out=out, in_=in_, func=func, **kw), r=bufs(r), w=bufs(w))

        def tt(eng, out, a, b, op, r, w):
            return S.op(eng, lambda e: e.tensor_tensor(out=out, in0=a, in1=b, op=op), r=bufs(r), w=bufs(w))

        def stt(out, a, scalar, b, op0, op1, r, w):
            return S.op("dve", lambda e: e.scalar_tensor_tensor(out=out, in0=a, scalar=scalar, in1=b, op0=op0, op1=op1),
                        r=bufs(r), w=bufs(w))

        def ts(eng, out, a, s1, s2, op0, op1, r, w):
            return S.op(eng, lambda e: e.tensor_scalar(out=out, in0=a, scalar1=s1, scalar2=s2, op0=op0, op1=op1),
                        r=bufs(r), w=bufs(w))

        def cp(eng, out, in_, r, w):
            if eng == "act":
                return S.op("act", lambda e: e.copy(out=out, in_=in_), r=bufs(r), w=bufs(w))
            return S.op(eng, lambda e: e.tensor_copy(out=out, in_=in_), r=bufs(r), w=bufs(w))

        X = sb(top, "X", [128, NKC, ntot])
        WB = sb(top, "WB", [128, 33280], BF16)
        pp = sb(top, "pp", [128, 80])
        cf = sb(top, "cf", [128, 10, 128])
        cb = sb(top, "cb", [128, 10, 128], BF16)
        smask = sb(top, "smask", [128, GT + C0P])
        wgate = sb(top, "wgate", [16, 512])
        negb = sb(top, "negb", [128, 4])
        epsb = sb(top, "epsb", [128, 1])
        r_ld = S.rec("ld")
        S.dma("sp", X[:], xT, r_ld, w=[X.b])
        S.dma("sp", pp[:], pp_d, r_ld, w=[pp.b])
        S.dma("sp", cf[:], cf_d, r_ld, w=[cf.b])
        S.dma("sp", smask[:], smask_d, r_ld, w=[smask.b])
        S.dma("sp", wgate[:], gla_w_gate, r_ld, w=[wgate.b])
        for t_ in (X, pp, cf, smask, wgate):
            t_.b.w = ("d", r_ld, r_ld.val)
        cp("dve", cb[:], cf[:], [cf], [cb])
        ts("dve", negb[:], pp[:, 64:68], -1.0, None, ALU.mult, ALU.bypass, [pp], [negb])
        S.op("dve", lambda e: e.memset(epsb[:], EPS), w=[epsb.b])
        IDENT = cb[:, 0, :]
        ONESB = cb[:, 1, :]
        TRIB = cb[:, 2, :]
        PERM = cf[:, 3, :]
        O1024 = cb[:, 4, :]
        O256 = cb[:, 5, :]
        O128 = cb[:, 6, :]

        PS = [pst(top, f"ps{i}") for i in range(7)]
        PSB = pst(top, "psb", (128, 1024), BF16)
        wrecs = [S.rec(f"w{i}") for i in range(6)]
        orec = {}

        def out_dma(out, in_, r):
            key = r[0].b.name
            if key not in orec:
                orec[key] = S.rec("o_" + key)
            S.dma("sp", out, in_, orec[key], r=bufs(r))

        def wload(dst_ap, src_ap, buf, i):
            S.dma("pool", dst_ap, src_ap, wrecs[i], w=[buf])

        def wview(lo, k, n):
            return WB[:, lo:lo + k * n].rearrange("p (k n) -> p k n", k=k)

        def layer_norm(c0, n, gcol, bcol, zb, zsq, tmp, psA, psB):
            m2, var, n1, n2 = tmp
            xs = X[:, :, c0:c0 + n]
            cp("pool", zb[:, :, 0:n], xs, [X], [zb])
            act(zsq[:, :, 0:n], xs, AF.Square, [X], [zsq])
            for kc in range(NKC):
                mm(psA[:, 0:n], O1024, zb[:, kc, 0:n], [cb, zb], [psA], start=(kc == 0), stop=(kc == NKC - 1), sig=(kc == NKC - 1))
            for kc in range(NKC):
                mm(psB[:, 0:n], O1024, zsq[:, kc, 0:n], [cb, zsq], [psB], start=(kc == 0), stop=(kc == NKC - 1), sig=(kc == NKC - 1))
            act(m2[:, 0:n], psA[:, 0:n], AF.Square, [psA], [m2])
            tt("dve", var[:, 0:n], psB[:, 0:n], m2[:, 0:n], ALU.subtract, [psB, m2], [var])
            act(var[:, 0:n], var[:, 0:n], AF.Ln, [var], [var], bias=epsb[:, 0:1])
            act(var[:, 0:n], var[:, 0:n], AF.Exp, [var], [var], scale=-0.5)
            cp("act", m2[:, 0:n], psA[:, 0:n], [psA], [m2])
            for kc in range(NKC):
                tt("dve", n1[:, 0:n], X[:, kc, c0:c0 + n], m2[:, 0:n], ALU.subtract, [X, m2], [n1])
                tt("pool", n2[:, 0:n], n1[:, 0:n], var[:, 0:n], ALU.mult, [n1, var], [n2])
                act(X[:, kc, c0:c0 + n], n2[:, 0:n], AF.Identity, [n2, pp], [X],
                    scale=pp[:, gcol + kc:gcol + kc + 1], bias=pp[:, bcol + kc:bcol + kc + 1])

        def mlp(layer):
            with ExitStack() as st:
                Xb = sb(st, "Xb", [128, NKC, ntot], BF16)
                hT = [sb(st, f"hT{i}", [128, 8, 512], BF16) for i in range(2)]
                rl = [sb(st, f"rl{i}", [128, 512]) for i in range(2)]
                tmp = [sb(st, f"lt{i}", [128, 512]) for i in range(4)]
                wb = [Buf("w1a"), Buf("w2a"), Buf("w1b"), Buf("w2b")]
                for (c0, n) in tiles:
                    for kc in range(NKC):
                        eng = ["dve", "act", "pool"][kc % 3]
                        cp(eng, Xb[:, kc, c0:c0 + n], X[:, kc, c0:c0 + n], [X], [Xb])
                w1d = mlp_w1[layer].rearrange("(k p) n -> p k n", p=128)
                w2d = mlp_w2[layer].rearrange("(k p) n -> p k n", p=128)
                hi = 0
                for q in range(4):
                    par = q % 2
                    w1v = wview(par * 16384, 8, 1024)
                    w2v = wview(par * 16384 + 8192, 8, 1024)
                    b1, b2 = wb[par * 2], wb[par * 2 + 1]
                    wload(w1v, w1d[:, :, q * 1024:(q + 1) * 1024], b1, par * 2)
                    wload(w2v, w2d[:, q * 8:(q + 1) * 8, :], b2, par * 2 + 1)
                    W1 = T(None, "w1")
                    W1.b = b1
                    W2 = T(None, "w2")
                    W2.b = b2
                    for (c0, n) in tiles:
                        h = hT[hi % 2]
                        hi += 1
                        for fc in range(8):
                            ps = PS[fc % 2]
                            for kc in range(NKC):
                                mm(ps[:, 0:n], w1v[:, kc, fc * 128:(fc + 1) * 128], Xb[:, kc, c0:c0 + n], [W1, Xb], [ps],
                                   start=(kc == 0), stop=(kc == NKC - 1), sig=(kc == NKC - 1))
                            r_ = rl[fc % 2]
                            act(r_[:, 0:n], ps[:, 0:n], AF.Relu, [ps], [r_])
                            tt("pool" if fc % 2 else "dve", h[:, fc, 0:n], r_[:, 0:n], r_[:, 0:n], ALU.mult, [r_], [h])
                        for dc in range(NKC):
                            ps = PS[2 + dc % 2]
                            for fc in range(8):
                                mm(ps[:, 0:n], w2v[:, fc, dc * 128:(dc + 1) * 128], h[:, fc, 0:n], [W2, h], [ps],
                                   start=(fc == 0), stop=(fc == 7), sig=(fc == 7))
                            stt(X[:, dc, c0:c0 + n], X[:, dc, c0:c0 + n], ALPHA if q == 0 else 1.0, ps[:, 0:n],
                                ALU.mult, ALU.add, [X, ps], [X])
                gi = (layer * 4 + 2) * 8
                for (c0, n) in tiles:
                    layer_norm(c0, n, gi, gi + 8, hT[0], hT[1], tmp, PS[4], PS[5])
                S.barrier()

        def gla():
            with ExitStack() as st:
                wq = wview(0, 8, 512)
                wk = wview(4096, 8, 512)
                wv = wview(8192, 8, 1024)
                wr = wview(16384, 8, 1024)
                wg = wview(24576, 8, 16)
                wo = wview(24704, 8, 1024)
                Wq, Wk, Wv, Wr, Wg, Wo = [T(None, n_) for n_ in ("gwq", "gwk", "gwv", "gwr", "gwg", "gwo")]
                wd = gla_w_in.rearrange("(k p) n -> p k n", p=128)
                S.dma("pool", wq, wd[:, :, 0:512], wrecs[0], w=[Wq.b])
                S.dma("pool", wk, wd[:, :, 512:1024], wrecs[1], w=[Wk.b])
                S.dma("pool", wg, wd[:, :, 3072:3088], wrecs[4], w=[Wg.b])
                S.dma("pool", wv, wd[:, :, 1024:2048], wrecs[2], w=[Wv.b])
                S.dma("pool", wr, wd[:, :, 2048:3072], wrecs[3], w=[Wr.b])
                S.dma("pool", wo, gla_w_out.rearrange("(k p) n -> p k n", p=128), wrecs[5], w=[Wo.b])
                xb = sb(st, "xb", [128, NKC, GT], BF16)
                glow = sb(st, "glow", [16, GT])
                e1 = sb(st, "e1", [128, GT])
                bc = sb(st, "bc", [128, GT])
                enb = sb(st, "enb", [128, GT])
                eb = [sb(st, f"eb{h}", [128, GT]) for h in range(4)]
                qt = [sb(st, f"qt{h}", [128, GT], BF16) for h in range(4)]
                kt = [sb(st, f"kt{h}", [128, GT], BF16) for h in range(4)]
                kh = sb(st, "kh", [128, GT], BF16)
                khtok = [sb(st, f"khtok{i}", [64, 4, 128], BF16) for i in range(2)]
                vtok = [sb(st, f"vtok{i}", [64, 1024], BF16) for i in range(2)]
                AT = [sb(st, f"AT{i}", [64, 4, 64], BF16) for i in range(2)]
                OT = sb(st, "OT", [128, 8, GT])
                sq = sb(st, "sq", [128, 2, GT], BF16)
                rstd = sb(st, "rstd", [128, GT])
                sr = sb(st, "sr", [128, GT])
                tq = sb(st, "tq", [128, GT])
                og = sb(st, "og", [128, 8, GT], BF16)
                Sst = sb(st, "Sst", [128, 4, 256])
                Sbf = sb(st, "Sbf", [128, 4, 256], BF16)
                S0 = sb(st, "S0", [128, 4, 256])
                S0bf = sb(st, "S0bf", [128, 4, 256], BF16)
                srec = S.rec("st")
                S.op("dve", lambda e: e.memset(Sst[:], 0.0), w=[Sst.b])
                S.op("pool", lambda e: e.memset(Sbf[:], 0.0), w=[Sbf.b])
                cctr = [0]

                def chunk(cl, L, St, Sb):
                    i2 = cctr[0] % 2
                    cctr[0] += 1
                    vt, kk, at = vtok[i2], khtok[i2], AT[i2]
                    for half in range(2):
                        ps = PS[half]
                        for kc in range(NKC):
                            mm(ps[0:L, :], xb[:, kc, cl:cl + L], wv[:, kc, half * 512:(half + 1) * 512], [xb, Wv], [ps],
                               start=(kc == 0), stop=(kc == NKC - 1), sig=(kc == NKC - 1))
                        cp("act", vt[0:L, half * 512:(half + 1) * 512], ps[0:L, :], [ps], [vt])
                    for h in range(4):
                        S.op("pe", lambda e, h=h: e.transpose(PSB[0:L, h * 128:(h + 1) * 128], khT4[h][:, cl:cl + L], IDENT),
                             r=[khT4[h].b, cb.b], w=[PSB.b], sig=(h == 3))
                    cp("dve", kk[0:L, :, :], PSB[0:L, 0:512].rearrange("p (h d) -> p h d", h=4), [PSB], [kk])
                    psS = PS[2]
                    for h in range(4):
                        mm(psS[0:L, h * 64:h * 64 + L], kt[h][:, cl:cl + L], qt[h][:, cl:cl + L], [kt[h], qt[h]], [psS], sig=(h == 3))
                    tt("dve", at[0:L, :, 0:L], psS[0:L, 0:256].rearrange("p (h i) -> p h i", h=4)[:, :, 0:L],
                       TRIB[0:L, 0:L].unsqueeze(1).to_broadcast([L, 4, L]), ALU.mult, [psS, cb], [at])
                    psO = PS[3]
                    for h in range(4):
                        for ec in range(2):
                            j = h * 2 + ec
                            mm(psO[:, j * 64:j * 64 + L], Sb[:, h, ec * 128:(ec + 1) * 128], qt[h][:, cl:cl + L], [Sb, qt[h]], [psO],
                               start=True, stop=False, sig=False)
                            mm(psO[:, j * 64:j * 64 + L], vt[0:L, h * 256 + ec * 128:h * 256 + (ec + 1) * 128], at[0:L, h, 0:L],
                               [vt, at], [psO], start=False, stop=True, sig=(j == 7))
                    cp("act", OT[:, :, cl:cl + L], psO[:, :].rearrange("p (j i) -> p j i", j=8)[:, :, 0:L], [psO], [OT])
                    for h in range(4):
                        ps = PS[4 + h // 2]
                        mm(ps[:, (h % 2) * 256:(h % 2 + 1) * 256], kk[0:L, h, :], vt[0:L, h * 256:(h + 1) * 256], [kk, vt], [ps],
                           sig=(h % 2 == 1))
                    for h in range(4):
                        ps = PS[4 + h // 2]
                        stt(St[:, h, :], St[:, h, :], eb[h][:, cl + L - 1:cl + L], ps[:, (h % 2) * 256:(h % 2 + 1) * 256],
                            ALU.mult, ALU.add, [St, eb[h], ps], [St])
                    cp("pool", Sb[:], St[:], [St], [Sb])

                khT4 = [sb(st, f"khT{h}", [128, GT], BF16) for h in range(4)]

                for ti, (c0, n) in enumerate(GTILES):
                    thin = (ti == 0)
                    for kc in range(NKC):
                        cp(["dve", "pool"][kc % 2], xb[:, kc, 0:n], X[:, kc, c0:c0 + n], [X], [xb])
                    for kc in range(NKC):
                        mm(PS[6][0:16, 0:n], wg[:, kc, :], xb[:, kc, 0:n], [Wg, xb], [PS[6]], start=(kc == 0), stop=(kc == NKC - 1),
                           sig=(kc == NKC - 1))
                    cp("act", glow[:, 0:n], PS[6][0:16, 0:n], [PS[6]], [glow])
                    mk = smask[:, GT:GT + n] if thin else smask[:, 0:n]
                    for h in range(4):
                        mm(PS[6][:, 0:n], wgate[:, h * 128:(h + 1) * 128], glow[:, 0:n], [wgate, glow], [PS[6]])
                        act(e1[:, 0:n], PS[6][:, 0:n], AF.Exp, [PS[6], negb], [e1], scale=-1.0, bias=negb[:, h:h + 1])
                        act(e1[:, 0:n], e1[:, 0:n], AF.Ln, [e1], [e1], bias=1.0)
                        S.op("dve", lambda e: e.tensor_tensor_scan(out=bc[:, 0:n], data0=mk, data1=e1[:, 0:n], initial=0.0,
                                                                    op0=ALU.mult, op1=ALU.add), r=[smask.b, e1.b], w=[bc.b])
                        act(eb[h][:, 0:n], bc[:, 0:n], AF.Exp, [bc], [eb[h]], scale=-1.0 / 16.0)
                        act(enb[:, 0:n], bc[:, 0:n], AF.Exp, [bc], [enb], scale=1.0 / 16.0)
                        for kc in range(NKC):
                            mm(PS[0][:, 0:n], wq[:, kc, h * 128:(h + 1) * 128], xb[:, kc, 0:n], [Wq, xb], [PS[0]], start=(kc == 0),
                               stop=(kc == NKC - 1), sig=(kc == NKC - 1))
                        stt(qt[h][:, 0:n], PS[0][:, 0:n], 128.0 ** -0.5, eb[h][:, 0:n], ALU.mult, ALU.mult, [PS[0], eb[h]], [qt[h]])
                        for kc in range(NKC):
                            mm(PS[1][:, 0:n], wk[:, kc, h * 128:(h + 1) * 128], xb[:, kc, 0:n], [Wk, xb], [PS[1]], start=(kc == 0),
                               stop=(kc == NKC - 1), sig=(kc == NKC - 1))
                        tt("dve", kt[h][:, 0:n], PS[1][:, 0:n], enb[:, 0:n], ALU.mult, [PS[1], enb], [kt[h]])
                        if thin:
                            tt("pool", khT4[h][:, 0:16], kt[h][:, 0:16], eb[h][:, 15:16].to_broadcast([128, 16]), ALU.mult,
                               [kt[h], eb[h]], [khT4[h]])
                            tt("pool", khT4[h][:, 16:n], kt[h][:, 16:n], eb[h][:, 16:n], ALU.mult, [kt[h], eb[h]], [khT4[h]])
                        else:
                            nch = n // 64
                            tt("pool", khT4[h][:, 0:n].rearrange("p (c l) -> p c l", c=nch),
                               kt[h][:, 0:n].rearrange("p (c l) -> p c l", c=nch),
                               eb[h][:, 0:n].rearrange("p (c l) -> p c l", c=nch)[:, :, 63:64].to_broadcast([128, nch, 64]),
                               ALU.mult, [kt[h], eb[h]], [khT4[h]])
                    if thin:
                        S.op("dve", lambda e: e.memset(OT[:, :, 0:n], 0.0), w=[OT.b])
                        chunk(0, 16, Sst, Sbf)
                        for i in range(32):
                            S.dma("sp", S0[:], state_own[i].rearrange("h p e -> p h e"), srec, w=[S0.b])
                            cp("pool", S0bf[:], S0[:], [S0], [S0bf])
                            chunk(16 + i, 1, S0, S0bf)
                            out_dma(gss[i].rearrange("h p e -> p h e"), S0[:], [S0])
                    else:
                        for ci in range(n // 64):
                            chunk(ci * 64, 64, Sst, Sbf)
                    for h in range(4):
                        act(sq[:, :, 0:n], OT[:, 2 * h:2 * h + 2, 0:n], AF.Square, [OT], [sq])
                        for ec in range(2):
                            mm(PS[6][:, 0:n], O256, sq[:, ec, 0:n], [cb, sq], [PS[6]], start=(ec == 0), stop=(ec == 1), sig=(ec == 1))
                        act(rstd[:, 0:n], PS[6][:, 0:n], AF.Ln, [PS[6]], [rstd], bias=epsb[:, 0:1])
                        act(rstd[:, 0:n], rstd[:, 0:n], AF.Exp, [rstd], [rstd], scale=-0.5)
                        for ec in range(2):
                            j = 2 * h + ec
                            ps = PS[ec]
                            for kc in range(NKC):
                                mm(ps[:, 0:n], wr[:, kc, j * 128:(j + 1) * 128], xb[:, kc, 0:n], [Wr, xb], [ps], start=(kc == 0),
                                   stop=(kc == NKC - 1), sig=(kc == NKC - 1))
                            act(sr[:, 0:n], ps[:, 0:n], AF.Silu, [ps], [sr])
                            stt(tq[:, 0:n], OT[:, j, 0:n], pp[:, 68 + ec:69 + ec], rstd[:, 0:n], ALU.mult, ALU.mult, [OT, pp, rstd], [tq])
                            tt("pool", og[:, j, 0:n], tq[:, 0:n], sr[:, 0:n], ALU.mult, [tq, sr], [og])
                    for dc in range(NKC):
                        ps = PS[dc % 2]
                        for j in range(8):
                            mm(ps[:, 0:n], wo[:, j, dc * 128:(dc + 1) * 128], og[:, j, 0:n], [Wo, og], [ps], start=(j == 0), stop=(j == 7),
                               sig=(j == 7))
                        stt(X[:, dc, c0:c0 + n], X[:, dc, c0:c0 + n], ALPHA, ps[:, 0:n], ALU.mult, ALU.add, [X, ps], [X])
                    layer_norm(c0, n, 0, 8, xb, og, [e1, bc, sr, tq], PS[2], PS[3])
                out_dma(gsp.rearrange("h p e -> p h e"), Sst[:], [Sst])
                S.barrier()


        def diff():
            with ExitStack() as st2:
                VT = sb(st2, "VT", [128, 17, 1024], BF16)
                KTv = WB[:, 16384:16384 + 8 * NT].rearrange("p (h n) -> p h n", h=8)
                KT = T(None, "KT")
                R0, R1 = T(None, "wr0"), T(None, "wr1")
                w0v = wview(0, 8, 1024)
                w1v = wview(8192, 8, 1024)
                neglam = sb(st2, "neglam", [128, 1])
                gsub = sb(st2, "gsub", [128, 1])
                wd = diff_w_in.rearrange("(k p) n -> p k n", p=128)
                crec = S.rec("cs")
                crec2 = S.rec("sn")
                with ExitStack() as st:
                    lp = sb(st, "lp", [1, 256])
                    pr = sb(st, "pr", [1, 128])
                    s2 = sb(st, "s2", [1, 2])
                    l1 = sb(st, "l1", [1, 1])
                    S.dma("sp", lp[:], lam_d, crec, w=[lp.b])
                    tt("dve", pr[:].rearrange("p (a d) -> p a d", a=2), lp[:].rearrange("p (a b d) -> p a b d", a=2, b=2)[:, :, 0, :],
                       lp[:].rearrange("p (a b d) -> p a b d", a=2, b=2)[:, :, 1, :], ALU.mult, [lp], [pr])
                    S.op("dve", lambda e: e.tensor_reduce(out=s2[:], in_=pr[:].rearrange("p (a d) -> p a d", a=2),
                                                          axis=mybir.AxisListType.X, op=ALU.add), r=[pr.b], w=[s2.b])
                    act(s2[:], s2[:], AF.Exp, [s2], [s2])
                    tt("dve", l1[:], s2[:, 1:2], s2[:, 0:1], ALU.subtract, [s2], [l1])
                    ts("dve", l1[:], l1[:], -LAM_INIT, None, ALU.add, ALU.bypass, [l1], [l1])
                    mm(PS[6][:, 0:1], cf[0:1, 1, :], l1[0:1, 0:1], [cf, l1], [PS[6]])
                    cp("act", neglam[:], PS[6][:, 0:1], [PS[6]], [neglam])
                    ts("dve", gsub[:], pp[:, 70:71], 1.0 - LAM_INIT, None, ALU.mult, ALU.bypass, [pp], [gsub])
                    S.barrier()

                with ExitStack() as st:
                    S.dma("pool", w0v, wd[:, :, 1024:2048], wrecs[0], w=[R0.b])
                    S.dma("pool", w1v, wd[:, :, 2048:3072], wrecs[1], w=[R1.b])
                    xb = sb(st, "xbA", [128, NKC, 512], BF16)
                    cs = sb(st, "cosA", [128, 512])
                    sn = sb(st, "sinA", [128, 512])
                    kf = sb(st, "kf", [128, 512])
                    t1 = sb(st, "t1A", [128, 512])
                    t2 = sb(st, "t2A", [128, 512])
                    kst = [sb(st, f"kst{i}", [128, 512]) for i in range(2)]
                    vst = [sb(st, f"vst{i}", [128, 1024]) for i in range(2)]
                    vi = 0
                    order = list(range(1, len(TILES))) + [0]
                    for ti in order:
                        c0, n = TILES[ti]
                        for kc in range(NKC):
                            cp(["dve", "pool"][kc % 2], xb[:, kc, 0:n], X[:, kc, c0:c0 + n], [X], [xb])
                        S.dma("sp", cs[:, 0:n], cos_d[:, c0:c0 + n], crec, w=[cs.b])
                        S.dma("sp", sn[:, 0:n], sin_d[:, c0:c0 + n], crec2, w=[sn.b])
                        for h in range(8):
                            ps = PS[h % 2]
                            for kc in range(NKC):
                                mm(ps[:, 0:n], w0v[:, kc, h * 128:(h + 1) * 128], xb[:, kc, 0:n], [R0, xb], [ps], start=(kc == 0),
                                   stop=(kc == NKC - 1), sig=(kc == NKC - 1))
                            cp("act", kf[:, 0:n], ps[:, 0:n], [ps], [kf])
                            ps2 = PS[2 + h % 2]
                            mm(ps2[:, 0:n], PERM, kf[:, 0:n], [cf, kf], [ps2])
                            tt("pool", t1[:, 0:n], kf[:, 0:n], cs[:, 0:n], ALU.mult, [kf, cs], [t1])
                            tt("dve", t2[:, 0:n], ps2[:, 0:n], sn[:, 0:n], ALU.mult, [ps2, sn], [t2])
                            ks_ = kst[h % 2]
                            tt("dve", ks_[:, 0:n], t1[:, 0:n], t2[:, 0:n], ALU.add, [t1, t2], [ks_])
                            cp("act", KTv[:, h, c0:c0 + n], ks_[:, 0:n], [ks_], [KT])
                            out_dma(kTo[:, h, c0:c0 + n], ks_[:, 0:n], [ks_])
                        nblk = max(1, n // 128)
                        for blk in range(nblk):
                            rows = min(128, n)
                            kt_i = 0 if ti == 0 else 1 + (ti - 1) * 4 + blk
                            v_ = vst[vi % 2]
                            vi += 1
                            for half in range(2):
                                ps = PS[4 + half]
                                for kc in range(NKC):
                                    mm(ps[0:rows, :], xb[:, kc, blk * 128:blk * 128 + rows], w1v[:, kc, half * 512:(half + 1) * 512],
                                       [xb, R1], [ps], start=(kc == 0), stop=(kc == NKC - 1), sig=(kc == NKC - 1))
                                cp("act" if half else "dve", v_[0:rows, half * 512:(half + 1) * 512], ps[0:rows, :], [ps], [v_])
                            cp("pool", VT[0:rows, kt_i, :], v_[0:rows, :], [v_], [VT])
                            out_dma(vro[c0 + blk * 128:c0 + blk * 128 + rows, :], v_[0:rows, :], [v_])
                    S.barrier()


                if with_sample:
                    with ExitStack() as st:
                        wc = sb(st, "wc", [128, NKC, 3, 128], BF16)
                        S.dma("pool", wc[:], wqkv_c.rearrange("(k p) j d -> p k j d", p=128), wrecs[2], w=[wc.b])
                        xbs = sb(st, "xbs", [128, NKC, 32], BF16)
                        cS = sb(st, "cS", [128, 32])
                        sS = sb(st, "sS", [128, 32])
                        S.dma("sp", cS[:], cos_d[:, 16:48], crec, w=[cS.b])
                        S.dma("sp", sS[:], sin_d[:, 16:48], crec2, w=[sS.b])
                        cp("dve", xbs[:], X[:, :, 16:48], [X], [xbs])
                        f3 = [sb(st, f"f3_{j}", [128, 32]) for j in range(3)]
                        ta = sb(st, "ta", [128, 32])
                        tb = sb(st, "tb", [128, 32])
                        for j in range(3):
                            ps = PS[4 + j % 2]
                            for kc in range(NKC):
                                mm(ps[:, 0:32], wc[:, kc, j, :], xbs[:, kc, :], [wc, xbs], [ps], start=(kc == 0), stop=(kc == NKC - 1),
                                   sig=(kc == NKC - 1))
                            cp("act", f3[j][:], ps[:, 0:32], [ps], [f3[j]])
                            if j < 2:
                                sc_ = 0.125 if j == 0 else 1.0
                                mm(PS[6][:, 0:32], PERM, f3[j][:], [cf, f3[j]], [PS[6]])
                                stt(ta[:], f3[j][:], sc_, cS[:], ALU.mult, ALU.mult, [f3[j], cS], [ta])
                                stt(tb[:], PS[6][:, 0:32], sc_, sS[:], ALU.mult, ALU.mult, [PS[6], sS], [tb])
                                tt("dve", f3[j][:], ta[:], tb[:], ALU.add, [ta, tb], [f3[j]])
                        qs, ks, vs = f3
                        Qb = sb(st, "Qb", [128, 32, 2], BF16)
                        S.op("dve", lambda e: e.memset(Qb[:], 0.0), w=[Qb.b])
                        cp("dve", Qb[0:64, :, 0], qs[0:64, :], [qs], [Qb])
                        cp("dve", Qb[64:128, :, 1], qs[64:128, :], [qs], [Qb])
                        STG = int(os.environ.get('PS_STAGE', '9'))
                        NPT = NSLOT * NPAGES
                        if STG >= 1:
                            idx = sb(st, "idx", [128, NPT], I32)
                            io = sb(st, "io", [128, 1])
                            ptrec = S.rec("pt")
                            S.dma("sp", idx[:], pt_d.to_broadcast([128, NPT]), ptrec, w=[idx.b])
                            SUB = int(os.environ.get('PS_SUB', '9'))
                            if SUB >= 1:
                                S.op("pool", lambda e: e.iota(io[:], pattern=[[0, 1]], base=0, channel_multiplier=1,
                                                              allow_small_or_imprecise_dtypes=True), w=[io.b])
                            if SUB >= 2:
                                ts("pool", idx[:], idx[:], 128.0, io[:, 0:1], ALU.mult, ALU.add, [idx, io], [idx])
                        NSL = 16
                        pg = [sb(st, f"pg{i}", [128, 256], BF16) for i in range(NSL)]
                        prec = [S.rec(f"pg{i}") for i in range(NSL)]
                        OTs = sb(st, "OTs", [128, 32, 2])
                        Ls = sb(st, "Ls", [128, 32, 2])
                        Pb = [sb(st, f"Pb{i}", [128, 16], BF16) for i in range(2)]
                        gi_ = 0
                        for s_ in range(32 if STG >= 3 else (1 if STG == 2 else 0)):
                            for g in range(8):
                                slots = []
                                for j in range(8):
                                    pj = s_ * 64 + g * 8 + j
                                    sl = (gi_ * 8 + j) % NSL
                                    slots.append(sl)
                                    S._deps("pool", [idx.b], [pg[sl].b])
                                    ins = nc.gpsimd.indirect_dma_start(
                                        out=pg[sl][:], out_offset=None, in_=cache_d,
                                        in_offset=bass.IndirectOffsetOnAxis(ap=idx[:, pj:pj + 1], axis=0))
                                    rec = prec[sl]
                                    rec.val += 16
                                    ins.then_inc(rec.sem, 16)
                                    pg[sl].b.w = ("d", rec, rec.val)
                                    pg[sl].b.r = {}
                                    idx.b.r["d%d" % id(rec)] = ("d", rec, rec.val)
                                pS = PS[4 + gi_ % 2]
                                for j in range(8):
                                    mm(pS[:, 2 * j:2 * j + 2], pg[slots[j]][:, 0:128], Qb[:, s_, :], [pg[slots[j]], Qb], [pS], sig=(j == 7))
                                pb = Pb[gi_ % 2]
                                act(pb[:], pS[:, 0:16], AF.Exp, [pS], [pb])
                                for j in range(8):
                                    mm(PS[0][:, 0:2], pg[slots[j]][:, 128:256], pb[:, 2 * j:2 * j + 2], [pg[slots[j]], pb], [PS[0]],
                                       start=(g == 0 and j == 0), stop=(g == 7 and j == 7), sig=(j == 7))
                                mm(PS[1][:, 0:16], ONESB, pb[:], [cb, pb], [PS[1]], start=(g == 0), stop=(g == 7))
                                gi_ += 1
                            cp("act", OTs[:, s_, :], PS[0][:, 0:2], [PS[0]], [OTs])
                            S.op("dve", lambda e, s_=s_: e.tensor_reduce(out=Ls[:, s_, :],
                                                                        in_=PS[1][:, 0:16].rearrange("p (j m) -> p m j", m=2),
                                                                        axis=mybir.AxisListType.X, op=ALU.add), r=[PS[1].b], w=[Ls.b])
                        if STG >= 4:
                            tt("dve", ta[:], qs[:], ks[:], ALU.mult, [qs, ks], [ta])
                            for m in range(2):
                                mm(PS[6][:, m * 32:(m + 1) * 32], cf[:, 7 + m, :], ta[:], [cf, ta], [PS[6]],
                                   sig=(m == 1))
                            pself = sb(st, "pself", [128, 2, 32])
                            num = sb(st, "num", [128, 2, 32])
                            den = sb(st, "den", [128, 2, 32])
                            act(pself[:], PS[6][:, 0:64].rearrange("p (m s) -> p m s", m=2), AF.Exp, [PS[6]], [pself])
                            for m in range(2):
                                tt("dve", num[:, m, :], pself[:, m, :], vs[:], ALU.mult, [pself, vs], [num])
                                tt("dve", num[:, m, :], num[:, m, :], OTs[:, :, m], ALU.add, [num, OTs], [num])
                                tt("dve", den[:, m, :], pself[:, m, :], Ls[:, :, m], ALU.add, [pself, Ls], [den])
                            act(den[:], den[:], AF.Ln, [den], [den])
                            act(den[:], den[:], AF.Exp, [den], [den], scale=-1.0)
                            tt("dve", num[:], num[:], den[:], ALU.mult, [num, den], [num])
                            stt(ta[:], num[:, 1, :], neglam[:, 0:1], num[:, 0, :], ALU.mult, ALU.add, [num, neglam], [ta])
                            sqs = sb(st, "sqs", [128, 32], BF16)
                            act(sqs[:], ta[:], AF.Square, [ta], [sqs])
                            mm(PS[6][:, 0:32], O128, sqs[:], [cb, sqs], [PS[6]])
                            act(tb[:], PS[6][:, 0:32], AF.Ln, [PS[6]], [tb], bias=epsb[:, 0:1])
                            act(tb[:], tb[:], AF.Exp, [tb], [tb], scale=-0.5)
                            stt(ta[:], ta[:], gsub[:, 0:1], tb[:], ALU.mult, ALU.mult, [ta, gsub, tb], [ta])
                            out_dma(ogs_o, ta[:], [ta])
                        S.barrier()

                with ExitStack() as st:
                    S.dma("pool", w0v, wd[:, :, 0:1024], wrecs[0], w=[R0.b])
                    S.dma("pool", w1v, diff_w_out.rearrange("(k p) n -> p k n", p=128), wrecs[1], w=[R1.b])
                    xb = sb(st, "xbB", [128, NKC, 512], BF16)
                    cs = sb(st, "cosB", [128, 512])
                    sn = sb(st, "sinB", [128, 512])
                    qf = sb(st, "qf", [128, 512])
                    t1 = sb(st, "t1B", [128, 512])
                    t2 = sb(st, "t2B", [128, 512])
                    QT = sb(st, "QT", [128, 8, 512], BF16)
                    og = xb
                    pt = [sb(st, f"pt{i}", [128, 512], BF16) for i in range(3)]
                    rl = [cs, sn]
                    sqb = sb(st, "sqB", [128, 512], BF16)
                    pti = 0
                    for ti, (c0, n) in enumerate(TILES):
                        nq = 16 if ti == 0 else n
                        for kc in range(NKC):
                            cp(["dve", "pool"][kc % 2], xb[:, kc, 0:n], X[:, kc, c0:c0 + n], [X], [xb])
                        S.dma("sp", cs[:, 0:n], cos_d[:, c0:c0 + n], crec, w=[cs.b])
                        S.dma("sp", sn[:, 0:n], sin_d[:, c0:c0 + n], crec2, w=[sn.b])
                        for h in range(8):
                            ps = PS[4 + h % 2]
                            for kc in range(NKC):
                                mm(ps[:, 0:nq], w0v[:, kc, h * 128:(h + 1) * 128], xb[:, kc, 0:nq], [R0, xb], [ps], start=(kc == 0),
                                   stop=(kc == NKC - 1), sig=(kc == NKC - 1))
                            cp("act", qf[:, 0:nq], ps[:, 0:nq], [ps], [qf])
                            mm(PS[6][:, 0:nq], PERM, qf[:, 0:nq], [cf, qf], [PS[6]])
                            stt(t1[:, 0:nq], qf[:, 0:nq], 0.125, cs[:, 0:nq], ALU.mult, ALU.mult, [qf, cs], [t1])
                            stt(t2[:, 0:nq], PS[6][:, 0:nq], 0.125, sn[:, 0:nq], ALU.mult, ALU.mult, [PS[6], sn], [t2])
                            tt("pool", QT[:, h, 0:nq], t1[:, 0:nq], t2[:, 0:nq], ALU.add, [t1, t2], [QT])
                        if ti == 0:
                            kl = [(0, 16, 0, 0, True)]
                        else:
                            i = ti - 1
                            kl = [(0, 16, 0, 0, False)]
                            for r in (3, 2, 1, 0):
                                j = 4 * i + r
                                kl.append((C0P + 128 * j, 128, 1 + j, 128 * r, True))
                            for j in range(4 * i):
                                kl.append((C0P + 128 * j, 128, 1 + j, 0, False))
                        for h in range(8):
                            for m in range(2):
                                pO, pL = PS[m], PS[2 + m]
                                for ki, (kc0, nk, vti, qlo, diag) in enumerate(kl):
                                    first, last = (ki == 0), (ki == len(kl) - 1)
                                    w_ = nq - qlo
                                    pS = PS[4 + pti % 2]
                                    p_ = pt[pti % 3]
                                    pti += 1
                                    mm(pS[0:nk, 0:w_], KTv[m * 64:(m + 1) * 64, h, kc0:kc0 + nk], QT[m * 64:(m + 1) * 64, h, qlo:nq],
                                       [KT, QT], [pS])
                                    act(p_[0:nk, 0:w_], pS[0:nk, 0:w_], AF.Exp, [pS], [p_])
                                    if diag:
                                        dw = min(128, w_)
                                        tt("pool", p_[0:nk, 0:dw], p_[0:nk, 0:dw], TRIB[0:nk, 0:dw], ALU.mult, [p_, cb], [p_])
                                    mm(pO[:, qlo:nq], VT[0:nk, vti, h * 128:(h + 1) * 128], p_[0:nk, 0:w_], [VT, p_], [pO],
                                       start=first, stop=last)
                                    mm(pL[:, qlo:nq], ONESB[0:nk, :], p_[0:nk, 0:w_], [cb, p_], [pL], start=first, stop=last)
                                act(rl[m][:, 0:nq], pL[:, 0:nq], AF.Ln, [pL], [rl[m]])
                                act(rl[m][:, 0:nq], rl[m][:, 0:nq], AF.Exp, [rl[m]], [rl[m]], scale=-1.0)
                            tt("dve", t1[:, 0:nq], PS[0][:, 0:nq], rl[0][:, 0:nq], ALU.mult, [PS[0], rl[0]], [t1])
                            tt("dve", t2[:, 0:nq], PS[1][:, 0:nq], rl[1][:, 0:nq], ALU.mult, [PS[1], rl[1]], [t2])
                            stt(t1[:, 0:nq], t2[:, 0:nq], neglam[:, 0:1], t1[:, 0:nq], ALU.mult, ALU.add, [t2, neglam, t1], [t1])
                            act(sqb[:, 0:nq], t1[:, 0:nq], AF.Square, [t1], [sqb])
                            mm(PS[6][:, 0:nq], O128, sqb[:, 0:nq], [cb, sqb], [PS[6]])
                            act(qf[:, 0:nq], PS[6][:, 0:nq], AF.Ln, [PS[6]], [qf], bias=epsb[:, 0:1])
                            act(qf[:, 0:nq], qf[:, 0:nq], AF.Exp, [qf], [qf], scale=-0.5)
                            stt(og[:, h, 0:nq], t1[:, 0:nq], gsub[:, 0:1], qf[:, 0:nq], ALU.mult, ALU.mult, [t1, gsub, qf], [og])
                        if ti == 0:
                            sample_og(og)
                        for dc in range(NKC):
                            ps = PS[4 + dc % 2]
                            for h in range(8):
                                mm(ps[:, 0:n], w1v[:, h, dc * 128:(dc + 1) * 128], og[:, h, 0:n], [R1, og], [ps], start=(h == 0), stop=(h == 7),
                                   sig=(h == 7))
                            stt(X[:, dc, c0:c0 + n], X[:, dc, c0:c0 + n], ALPHA, ps[:, 0:n], ALU.mult, ALU.add, [X, ps], [X])
                        layer_norm(c0, n, 32, 40, xb, QT, [qf, t1, t2, rl[0]], PS[4], PS[5])
                    S.barrier()

        def sample_og(og):
            S.op("dve", lambda e: e.memset(og[:, :, 16:48], 0.0), w=[og.b])


        def tail_prog():
            with ExitStack() as st:
                ogf = sb(st, "ogf", [128, 8, 32])
                ogb = sb(st, "ogb", [128, 8, 32], BF16)
                trec = S.rec("tl")
                S.dma("sp", ogf[:], og_all_d, trec, w=[ogf.b])
                cp("dve", ogb[:], ogf[:], [ogf], [ogb])
                w1v = wview(8192, 8, 1024)
                R1 = T(None, "r1")
                S.dma("pool", w1v, diff_w_out.rearrange("(k p) n -> p k n", p=128), wrecs[1], w=[R1.b])
                for dc in range(NKC):
                    ps = PS[4 + dc % 2]
                    for h in range(8):
                        mm(ps[:, 0:32], w1v[:, h, dc * 128:(dc + 1) * 128], ogb[:, h, :], [R1, ogb], [ps], start=(h == 0), stop=(h == 7),
                           sig=(h == 7))
                    stt(X[:, dc, 0:32], X[:, dc, 0:32], ALPHA, ps[:, 0:32], ALU.mult, ALU.add, [X, ps], [X])
                zb = sb(st, "zbt", [128, 8, 32], BF16)
                zq = sb(st, "zqt", [128, 8, 32], BF16)
                tmp = [sb(st, f"tt{i}", [128, 32]) for i in range(4)]
                layer_norm(0, 32, 32, 40, zb, zq, tmp, PS[4], PS[5])
                S.barrier()

        if tail:
            tail_prog()
            mlp(1)
            out_dma(yT, X[:], [X])
            S.barrier()
            return nc
        gla()
        if DEBUG:
            out_dma(dbg[0], X[:], [X])
        mlp(0)
        if DEBUG:
            out_dma(dbg[1], X[:], [X])
        out_dma(xmid_o, X[:, :, 16:48], [X])
        diff()
        mlp(1)
        out_dma(yT, X[:], [X])
        S.barrier()
    return nc


def _consts():
    cf = np.zeros((128, 10, 128), np.float32)
    cf[0:64, 7, :] = 1.0
    cf[64:128, 8, :] = 1.0
    cf[:, 6, :] = 1.0 / 128
    cf[:, 0, :] = np.eye(128)
    cf[:, 1, :] = 1.0
    cf[:, 2, :] = np.triu(np.ones((128, 128)))
    for p in range(128):
        d = p % 64
        if d < 8:
            cf[p + 8, 3, p] = -1.0
        elif d < 16:
            cf[p - 8, 3, p] = 1.0
    cf[:, 4, :] = 1.0 / 1024
    cf[:, 5, :] = 1.0 / 256
    sm = np.ones((128, GT + C0P), np.float32)
    sm[:, 0:GT:64] = 0.0
    sm[:, GT] = 0.0
    sm[:, GT + 16:] = 0.0
    pos = np.zeros(NT, np.float32)
    pos[0:16] = np.arange(16)
    pos[16:48] = 8192
    pos[48:] = 16 + np.arange(SEQ)
    inv = (np.float32(500000.0) ** (-(np.arange(0, 16, 2, dtype=np.float32)) / np.float32(16))).astype(np.float32)
    rc = np.ones((128, NT), np.float32)
    rs = np.zeros((128, NT), np.float32)
    for p in range(128):
        d = p % 64
        if d < 16:
            ang = (pos * inv[d % 8]).astype(np.float32)
            rc[p] = np.cos(ang)
            rs[p] = np.sin(ang)
    return cf, sm, rc, rs


def kernel(x_prompt, x_sample, state_gla, cache_k, cache_v, page_table, meta_tokens,
           gla_w_in, gla_w_gate, gla_b_gate, gla_norm, gla_w_out,
           diff_w_in, diff_lambda, diff_norm, diff_w_out,
           mlp_w1, mlp_w2, ln_mix_g, ln_mix_b, ln_mlp_g, ln_mlp_b, _ncores=NCORES, _dev=False, _npool=None):
    f = lambda a: np.ascontiguousarray(np.asarray(a, dtype=np.float32))
    x_prompt, x_sample, state_gla, meta_tokens = f(x_prompt), f(x_sample), f(state_gla), f(meta_tokens)
    ncores = _ncores
    npool = 8 if _dev else (_npool or NPOOL)
    cf, sm, rc, rs = _consts()
    pp = np.zeros((128, 80), np.float32)
    lnp = [ln_mix_g, ln_mix_b, ln_mlp_g, ln_mlp_b]
    for layer in range(2):
        for j in range(4):
            pp[:, (layer * 4 + j) * 8:(layer * 4 + j + 1) * 8] = f(lnp[j])[layer].reshape(8, 128).T
    pp[:, 64:68] = f(gla_b_gate)[0].reshape(4, 128).T
    pp[:, 68:70] = f(gla_norm)[0].reshape(2, 128).T
    pp[:, 70] = f(diff_norm)[0]
    tail_shared = {
        "gla_w_gate": f(gla_w_gate)[0], "diff_w_out": f(diff_w_out)[0], "mlp_w1": f(mlp_w1), "mlp_w2": f(mlp_w2),
        "pp": pp, "cf": cf, "smask": sm,
    }
    shared = dict(tail_shared)
    shared.update({
        "gla_w_in": f(gla_w_in)[0], "gla_w_out": f(gla_w_out)[0], "diff_w_in": f(diff_w_in)[0],
        "lam": f(diff_lambda)[0].reshape(1, 256), "rcos": rc, "rsin": rs,
        "ptab": np.ascontiguousarray(np.asarray(page_table, dtype=np.int32).reshape(1, -1)),
        "state_all": np.ascontiguousarray(state_gla[0]),
    })
    dwi = shared["diff_w_in"]
    in_maps = []
    for c in range(ncores):
        xt = np.zeros((D, NT), np.float32)
        xt[:, 0:16] = meta_tokens.T
        xt[:, 16:48] = x_sample[:, 0, :].T
        xt[:, C0P:] = x_prompt[c].T
        m = dict(shared)
        m["xT"] = np.ascontiguousarray(xt.reshape(8, 128, NT).transpose(1, 0, 2))
        m["wqkv_c"] = np.ascontiguousarray(
            np.stack([dwi[:, j * 1024 + c * 128:j * 1024 + (c + 1) * 128] for j in range(3)], axis=1))
        if _dev:
            m["cache"] = np.zeros((npool * 128, 256), np.float32)
        else:
            kc_ = np.asarray(cache_k)[0, :npool, :, c, :]
            vc_ = np.asarray(cache_v)[0, :npool, :, c, :]
            m["cache"] = np.ascontiguousarray(
                np.concatenate([kc_.transpose(0, 2, 1), vc_], axis=2).reshape(npool * 128, 256).astype(np.float32))
        in_maps.append(m)
    nc = build(ncores, npool, not _dev)
    res = run_bass_kernel_spmd(nc, in_maps, core_ids=list(range(ncores))).results
    B = ncores
    y_prompt = np.zeros((B, SEQ, D), np.float32)
    gla_sp = np.zeros((1, B, 4, 128, 256), np.float32)
    k_p = np.zeros((1, B, SEQ + 16, 8, 128), np.float32)
    v_p = np.zeros((1, B, SEQ + 16, 8, 128), np.float32)
    for c in range(ncores):
        r = res[c]
        yt = r["yT"].transpose(1, 0, 2).reshape(D, NT)
        y_prompt[c] = yt[:, C0P:].T
        gla_sp[0, c] = r["gla_sp"]
        kt_ = r["kT_rows"]
        kcols = np.concatenate([kt_[:, :, 0:16], kt_[:, :, C0P:]], axis=2)
        k_p[0, c] = kcols.transpose(2, 1, 0)
        vr = r["v_rows"]
        v_p[0, c] = np.concatenate([vr[0:16], vr[C0P:]], axis=0).reshape(SEQ + 16, 8, 128)
    r0 = res[0]
    gla_ss = np.ascontiguousarray(r0["gla_ss"]).reshape(1, 32, 4, 128, 256)
    k_s = np.ascontiguousarray(r0["kT_rows"][:, :, 16:48].transpose(2, 1, 0)).reshape(1, 32, 1, 8, 128)
    v_s = np.ascontiguousarray(r0["v_rows"][16:48]).reshape(1, 32, 1, 8, 128)
    og_all = np.zeros((128, 8, 32), np.float32)
    for c in range(ncores):
        og_all[:, c, :] = res[c]["ogs"]
    m2 = dict(tail_shared)
    m2["xT"] = np.ascontiguousarray(r0["xs_mid"])
    m2["og_all"] = og_all
    nc2 = build(1, npool, False, tail=True)
    res2 = run_bass_kernel_spmd(nc2, [m2], core_ids=[0]).results[0]
    y_sample = np.ascontiguousarray(res2["yT"].transpose(1, 0, 2).reshape(D, 32).T).reshape(32, 1, D)
    return (y_prompt, y_sample, gla_sp, gla_ss, k_p, v_p, k_s, v_s)
```

```python
import math
import os
from contextlib import ExitStack
import numpy as np
import concourse.bass as bass
import concourse.mybir as mybir
from concourse.bass_utils import run_bass_kernel_spmd

F32 = mybir.dt.float32
BF16 = mybir.dt.bfloat16
I32 = mybir.dt.int32
AF = mybir.ActivationFunctionType
ALU = mybir.AluOpType

D = 1024
NKC = 8
NMETA = 16
NSLOT = 32
C0P = NMETA + NSLOT
SEQ = 2048
NT = C0P + SEQ
NCORES = 8
DEPTH = 2
ALPHA = (2 * DEPTH) ** 0.25
EPS = 1e-5
PAGE = 128
NPAGES = 64
NPOOL = 2560
EPOCH = 4000
TILES = [(0, C0P)] + [(C0P + 512 * i, 512) for i in range(SEQ // 512)]
GT = 256
GTILES = [(0, C0P)] + [(C0P + GT * i, GT) for i in range(SEQ // GT)]
LAM_INIT = 0.8 - 0.6 * math.exp(-0.3 * 1)
DEBUG = False


class Buf:
    __slots__ = ("name", "w", "r")

    def __init__(self, name):
        self.name = name
        self.w = None
        self.r = {}


class SemRec:
    def __init__(self, sem):
        self.sem = sem
        self.val = 0


class Sched:
    def __init__(self, nc, stack):
        self.nc = nc
        self.stack = stack
        self.eng = {"pe": nc.tensor, "act": nc.scalar, "dve": nc.vector, "pool": nc.gpsimd, "sp": nc.sync}
        self.cnt = {k: 0 for k in self.eng}
        self.sems = {k: [] for k in self.eng}
        self.waited = {k: {} for k in self.eng}
        self.recs = []
        self.nsem = 0

    def new_sem(self, name):
        self.nsem += 1
        return self.stack.enter_context(self.nc.semaphore(name))

    def rec(self, name):
        r = SemRec(self.new_sem("d_" + name))
        self.recs.append(r)
        return r

    def esem(self, e, i):
        ep = (i - 1) // EPOCH
        while len(self.sems[e]) <= ep:
            self.sems[e].append(self.new_sem(f"e_{e}_{len(self.sems[e])}"))
        return self.sems[e][ep], (i - 1) % EPOCH + 1

    def _wait(self, e, dep):
        if dep[0] == "e":
            _, e2, i = dep
            if e2 == e and e == "pe":
                return
            if self.waited[e].get(e2, 0) >= i:
                return
            sem, val = self.esem(e2, i)
            self.eng[e].wait_ge(sem, val)
            self.waited[e][e2] = i
        else:
            _, rec, val = dep
            if self.waited[e].get(id(rec), 0) >= val:
                return
            self.eng[e].wait_ge(rec.sem, val)
            self.waited[e][id(rec)] = val

    def _deps(self, e, r, w):
        for b in r:
            if b.w is not None:
                self._wait(e, b.w)
        for b in w:
            if b.w is not None:
                self._wait(e, b.w)
            for k, d in b.r.items():
                if k != e:
                    self._wait(e, d)

    def op(self, e, fn, r=(), w=(), sig=True):
        self._deps(e, r, w)
        ins = fn(self.eng[e])
        nid = self.cnt[e] + 1
        if sig:
            self.cnt[e] = nid
            sem, _ = self.esem(e, nid)
            ins.then_inc(sem, 1)
        me = ("e", e, nid)
        for b in r:
            b.r[e] = me
        for b in w:
            b.w = me
            b.r = {}
        return ins

    def dma(self, q, out, in_, rec, r=(), w=(), **kw):
        self._deps(q, r, w)
        ins = self.eng[q].dma_start(out=out, in_=in_, **kw)
        rec.val += 16
        ins.then_inc(rec.sem, 16)
        me = ("d", rec, rec.val)
        for b in r:
            b.r["d%d" % id(rec)] = me
        for b in w:
            b.w = me
            b.r = {}
        return ins

    def barrier(self):
        for e in self.eng:
            for e2 in self.eng:
                if e2 != e and self.cnt[e2] > 0:
                    self._wait(e, ("e", e2, self.cnt[e2]))
            for rec in self.recs:
                if rec.val > 0:
                    self._wait(e, ("d", rec, rec.val))


class T:
    def __init__(self, t, name):
        self.t = t
        self.b = Buf(name)

    def __getitem__(self, k):
        return self.t[k]


def build(ncores, npool, with_sample, tail=False):
    nc = bass.Bass("TRN2", target_bir_lowering=False)
    dram = {}
    ntot = 32 if tail else NT
    tiles = [(0, 32)] if tail else TILES

    def din(name, shape, dt=F32):
        dram[name] = nc.dram_tensor(name, list(shape), dt, kind="ExternalInput").ap()
        return dram[name]

    def dout(name, shape, dt=F32):
        dram[name] = nc.dram_tensor(name, list(shape), dt, kind="ExternalOutput").ap()
        return dram[name]

    xT = din("xT", [128, NKC, ntot])
    gla_w_gate = din("gla_w_gate", [16, 512])
    diff_w_out = din("diff_w_out", [D, D])
    mlp_w1 = din("mlp_w1", [2, D, 4 * D])
    mlp_w2 = din("mlp_w2", [2, 4 * D, D])
    pp_d = din("pp", [128, 80])
    cf_d = din("cf", [128, 10, 128])
    smask_d = din("smask", [128, GT + C0P])
    yT = dout("yT", [128, NKC, ntot])
    og_all_d = din("og_all", [128, 8, 32]) if tail else None
    if not tail:
        state_own = din("state_all", [32, 4, 128, 256])
        gla_w_in = din("gla_w_in", [D, 3088])
        gla_w_out = din("gla_w_out", [D, D])
        diff_w_in = din("diff_w_in", [D, 3072])
        lam_d = din("lam", [1, 256])
        cos_d = din("rcos", [128, NT])
        sin_d = din("rsin", [128, NT])
        wqkv_c = din("wqkv_c", [D, 3, 128])
        cache_d = din("cache", [npool * 128, 256])
        pt_d = din("ptab", [1, NSLOT * NPAGES], I32)
        gsp = dout("gla_sp", [4, 128, 256])
        gss = dout("gla_ss", [32, 4, 128, 256])
        xmid_o = dout("xs_mid", [128, NKC, 32])
        ogs_o = dout("ogs", [128, 32])
        kTo = dout("kT_rows", [128, 8, NT])
        vro = dout("v_rows", [NT, D])
    dbg = [dout(f"dbg{i}", [128, NKC, NT]) for i in range(2)] if (DEBUG and not tail) else []

    with ExitStack() as top:
        S = Sched(nc, top)
        uid = [0]

        def sb(st, name, shape, dt=F32):
            uid[0] += 1
            return T(st.enter_context(nc.sbuf_tensor("s%d_" % uid[0] + name, list(shape), dt)), name)

        def pst(st, name, shape=(128, 512), dt=F32):
            uid[0] += 1
            return T(st.enter_context(nc.psum_tensor("p%d_" % uid[0] + name, list(shape), dt)), name)

        def bufs(ts):
            return [t.b for t in ts]

        def mm(out, lhsT, rhs, r, w, start=True, stop=True, sig=True):
            return S.op("pe", lambda e: e.matmul(out, lhsT=lhsT, rhs=rhs, start=start, stop=stop),
                        r=bufs(r), w=bufs(w), sig=sig)

        def act(out, in_, func, r, w, bias=None, scale=None):
            kw = {}
            if bias is not None:
                kw["bias"] = bias
            if scale is not None:
                kw["scale"] = scale
            return S.op("act", lambda e: e.activation(out=out, in_=in_, func=func, **kw), r=bufs(r), w=bufs(w))

        def tt(eng, out, a, b, op, r, w):
            return S.op(eng, lambda e: e.tensor_tensor(out=out, in0=a, in1=b, op=op), r=bufs(r), w=bufs(w))

        def stt(out, a, scalar, b, op0, op1, r, w):
            return S.op("dve", lambda e: e.scalar_tensor_tensor(out=out, in0=a, scalar=scalar, in1=b, op0=op0, op1=op1),
                        r=bufs(r), w=bufs(w))

        def ts(eng, out, a, s1, s2, op0, op1, r, w):
            return S.op(eng, lambda e: e.tensor_scalar(out=out, in0=a, scalar1=s1, scalar2=s2, op0=op0, op1=op1),
                        r=bufs(r), w=bufs(w))

        def cp(eng, out, in_, r, w):
            if eng == "act":
                return S.op("act", lambda e: e.copy(out=out, in_=in_), r=bufs(r), w=bufs(w))
            return S.op(eng, lambda e: e.tensor_copy(out=out, in_=in_), r=bufs(r), w=bufs(w))

        X = sb(top, "X", [128, NKC, ntot])
        WB = sb(top, "WB", [128, 33280], BF16)
        pp = sb(top, "pp", [128, 80])
        cf = sb(top, "cf", [128, 10, 128])
        cb = sb(top, "cb", [128, 10, 128], BF16)
        smask = sb(top, "smask", [128, GT + C0P])
        wgate = sb(top, "wgate", [16, 512])
        negb = sb(top, "negb", [128, 4])
        epsb = sb(top, "epsb", [128, 1])
        r_ld = S.rec("ld")
        S.dma("sp", X[:], xT, r_ld, w=[X.b])
        S.dma("sp", pp[:], pp_d, r_ld, w=[pp.b])
        S.dma("sp", cf[:], cf_d, r_ld, w=[cf.b])
        S.dma("sp", smask[:], smask_d, r_ld, w=[smask.b])
        S.dma("sp", wgate[:], gla_w_gate, r_ld, w=[wgate.b])
        for t_ in (X, pp, cf, smask, wgate):
            t_.b.w = ("d", r_ld, r_ld.val)
        cp("dve", cb[:], cf[:], [cf], [cb])
        ts("dve", negb[:], pp[:, 64:68], -1.0, None, ALU.mult, ALU.bypass, [pp], [negb])
        S.op("dve", lambda e: e.memset(epsb[:], EPS), w=[epsb.b])
        IDENT = cb[:, 0, :]
        ONESB = cb[:, 1, :]
        TRIB = cb[:, 2, :]
        PERM = cf[:, 3, :]
        O1024 = cb[:, 4, :]
        O256 = cb[:, 5, :]
        O128 = cb[:, 6, :]

        PS = [pst(top, f"ps{i}") for i in range(7)]
        PSB = pst(top, "psb", (128, 1024), BF16)
        wrecs = [S.rec(f"w{i}") for i in range(6)]
        orec = {}

        def out_dma(out, in_, r):
            key = r[0].b.name
            if key not in orec:
                orec[key] = S.rec("o_" + key)
            S.dma("sp", out, in_, orec[key], r=bufs(r))

        def wload(dst_ap, src_ap, buf, i):
            S.dma("pool", dst_ap, src_ap, wrecs[i], w=[buf])

        def wview(lo, k, n):
            return WB[:, lo:lo + k * n].rearrange("p (k n) -> p k n", k=k)

        def layer_norm(c0, n, gcol, bcol, zb, zsq, tmp, psA, psB):
            m2, var, n1, n2 = tmp
            xs = X[:, :, c0:c0 + n]
            cp("pool", zb[:, :, 0:n], xs, [X], [zb])
            act(zsq[:, :, 0:n], xs, AF.Square, [X], [zsq])
            for kc in range(NKC):
                mm(psA[:, 0:n], O1024, zb[:, kc, 0:n], [cb, zb], [psA], start=(kc == 0), stop=(kc == NKC - 1), sig=(kc == NKC - 1))
            for kc in range(NKC):
                mm(psB[:, 0:n], O1024, zsq[:, kc, 0:n], [cb, zsq], [psB], start=(kc == 0), stop=(kc == NKC - 1), sig=(kc == NKC - 1))
            act(m2[:, 0:n], psA[:, 0:n], AF.Square, [psA], [m2])
            tt("dve", var[:, 0:n], psB[:, 0:n], m2[:, 0:n], ALU.subtract, [psB, m2], [var])
            act(var[:, 0:n], var[:, 0:n], AF.Ln, [var], [var], bias=epsb[:, 0:1])
            act(var[:, 0:n], var[:, 0:n], AF.Exp, [var], [var], scale=-0.5)
            cp("act", m2[:, 0:n], psA[:, 0:n], [psA], [m2])
            for kc in range(NKC):
                tt("dve", n1[:, 0:n], X[:, kc, c0:c0 + n], m2[:, 0:n], ALU.subtract, [X, m2], [n1])
                tt("pool", n2[:, 0:n], n1[:, 0:n], var[:, 0:n], ALU.mult, [n1, var], [n2])
                act(X[:, kc, c0:c0 + n], n2[:, 0:n], AF.Identity, [n2, pp], [X],
                    scale=pp[:, gcol + kc:gcol + kc + 1], bias=pp[:, bcol + kc:bcol + kc + 1])

        def mlp(layer):
            with ExitStack() as st:
                Xb = sb(st, "Xb", [128, NKC, ntot], BF16)
                hT = [sb(st, f"hT{i}", [128, 8, 512], BF16) for i in range(2)]
                rl = [sb(st, f"rl{i}", [128, 512]) for i in range(2)]
                tmp = [sb(st, f"lt{i}", [128, 512]) for i in range(4)]
                wb = [Buf("w1a"), Buf("w2a"), Buf("w1b"), Buf("w2b")]
                for (c0, n) in tiles:
                    for kc in range(NKC):
                        eng = ["dve", "act", "pool"][kc % 3]
                        cp(eng, Xb[:, kc, c0:c0 + n], X[:, kc, c0:c0 + n], [X], [Xb])
                w1d = mlp_w1[layer].rearrange("(k p) n -> p k n", p=128)
                w2d = mlp_w2[layer].rearrange("(k p) n -> p k n", p=128)
                hi = 0
                for q in range(4):
                    par = q % 2
                    w1v = wview(par * 16384, 8, 1024)
                    w2v = wview(par * 16384 + 8192, 8, 1024)
                    b1, b2 = wb[par * 2], wb[par * 2 + 1]
                    wload(w1v, w1d[:, :, q * 1024:(q + 1) * 1024], b1, par * 2)
                    wload(w2v, w2d[:, q * 8:(q + 1) * 8, :], b2, par * 2 + 1)
                    W1 = T(None, "w1")
                    W1.b = b1
                    W2 = T(None, "w2")
                    W2.b = b2
                    for (c0, n) in tiles:
                        h = hT[hi % 2]
                        hi += 1
                        for fc in range(8):
                            ps = PS[fc % 2]
                            for kc in range(NKC):
                                mm(ps[:, 0:n], w1v[:, kc, fc * 128:(fc + 1) * 128], Xb[:, kc, c0:c0 + n], [W1, Xb], [ps],
                                   start=(kc == 0), stop=(kc == NKC - 1), sig=(kc == NKC - 1))
                            r_ = rl[fc % 2]
                            act(r_[:, 0:n], ps[:, 0:n], AF.Relu, [ps], [r_])
                            tt("pool" if fc % 2 else "dve", h[:, fc, 0:n], r_[:, 0:n], r_[:, 0:n], ALU.mult, [r_], [h])
                        for dc in range(NKC):
                            ps = PS[2 + dc % 2]
                            for fc in range(8):
                                mm(ps[:, 0:n], w2v[:, fc, dc * 128:(dc + 1) * 128], h[:, fc, 0:n], [W2, h], [ps],
                                   start=(fc == 0), stop=(fc == 7), sig=(fc == 7))
                            stt(X[:, dc, c0:c0 + n], X[:, dc, c0:c0 + n], ALPHA if q == 0 else 1.0, ps[:, 0:n],
                                ALU.mult, ALU.add, [X, ps], [X])
                gi = (layer * 4 + 2) * 8
                for (c0, n) in tiles:
                    layer_norm(c0, n, gi, gi + 8, hT[0], hT[1], tmp, PS[4], PS[5])
                S.barrier()

        def gla():
            with ExitStack() as st:
                wq = wview(0, 8, 512)
                wk = wview(4096, 8, 512)
                wv = wview(8192, 8, 1024)
                wr = wview(16384, 8, 1024)
                wg = wview(24576, 8, 16)
                wo = wview(24704, 8, 1024)
                Wq, Wk, Wv, Wr, Wg, Wo = [T(None, n_) for n_ in ("gwq", "gwk", "gwv", "gwr", "gwg", "gwo")]
                wd = gla_w_in.rearrange("(k p) n -> p k n", p=128)
                S.dma("pool", wq, wd[:, :, 0:512], wrecs[0], w=[Wq.b])
                S.dma("pool", wk, wd[:, :, 512:1024], wrecs[1], w=[Wk.b])
                S.dma("pool", wg, wd[:, :, 3072:3088], wrecs[4], w=[Wg.b])
                S.dma("pool", wv, wd[:, :, 1024:2048], wrecs[2], w=[Wv.b])
                S.dma("pool", wr, wd[:, :, 2048:3072], wrecs[3], w=[Wr.b])
                S.dma("pool", wo, gla_w_out.rearrange("(k p) n -> p k n", p=128), wrecs[5], w=[Wo.b])
                xb = sb(st, "xb", [128, NKC, GT], BF16)
                glow = sb(st, "glow", [16, GT])
                e1 = sb(st, "e1", [128, GT])
                bc = sb(st, "bc", [128, GT])
                enb = sb(st, "enb", [128, GT])
                eb = [sb(st, f"eb{h}", [128, GT]) for h in range(4)]
                qt = [sb(st, f"qt{h}", [128, GT], BF16) for h in range(4)]
                kt = [sb(st, f"kt{h}", [128, GT], BF16) for h in range(4)]
                kh = sb(st, "kh", [128, GT], BF16)
                khtok = [sb(st, f"khtok{i}", [64, 4, 128], BF16) for i in range(2)]
                vtok = [sb(st, f"vtok{i}", [64, 1024], BF16) for i in range(2)]
                AT = [sb(st, f"AT{i}", [64, 4, 64], BF16) for i in range(2)]
                OT = sb(st, "OT", [128, 8, GT])
                sq = sb(st, "sq", [128, 2, GT], BF16)
                rstd = sb(st, "rstd", [128, GT])
                sr = sb(st, "sr", [128, GT])
                tq = sb(st, "tq", [128, GT])
                og = sb(st, "og", [128, 8, GT], BF16)
                Sst = sb(st, "Sst", [128, 4, 256])
                Sbf = sb(st, "Sbf", [128, 4, 256], BF16)
                S0s = [sb(st, f"S0_{i}", [128, 4, 256]) for i in range(2)]
                S0bfs = [sb(st, f"S0bf_{i}", [128, 4, 256], BF16) for i in range(2)]
                srecs = [S.rec("st0"), S.rec("st1")]
                S.op("dve", lambda e: e.memset(Sst[:], 0.0), w=[Sst.b])
                S.op("pool", lambda e: e.memset(Sbf[:], 0.0), w=[Sbf.b])
                cctr = [0]

                def chunk(cl, L, St, Sb):
                    i2 = cctr[0] % 2
                    cctr[0] += 1
                    vt, kk, at = vtok[i2], khtok[i2], AT[i2]
                    for half in range(2):
                        ps = PS[half]
                        for kc in range(NKC):
                            mm(ps[0:L, :], xb[:, kc, cl:cl + L], wv[:, kc, half * 512:(half + 1) * 512], [xb, Wv], [ps],
                               start=(kc == 0), stop=(kc == NKC - 1), sig=(kc == NKC - 1))
                        cp("act", vt[0:L, half * 512:(half + 1) * 512], ps[0:L, :], [ps], [vt])
                    for h in range(4):
                        S.op("pe", lambda e, h=h: e.transpose(PSB[0:L, h * 128:(h + 1) * 128], khT4[h][:, cl:cl + L], IDENT),
                             r=[khT4[h].b, cb.b], w=[PSB.b], sig=(h == 3))
                    cp("dve", kk[0:L, :, :], PSB[0:L, 0:512].rearrange("p (h d) -> p h d", h=4), [PSB], [kk])
                    psS = PS[2]
                    for h in range(4):
                        mm(psS[0:L, h * 64:h * 64 + L], kt[h][:, cl:cl + L], qt[h][:, cl:cl + L], [kt[h], qt[h]], [psS], sig=(h == 3))
                    tt("dve", at[0:L, :, 0:L], psS[0:L, 0:256].rearrange("p (h i) -> p h i", h=4)[:, :, 0:L],
                       TRIB[0:L, 0:L].unsqueeze(1).to_broadcast([L, 4, L]), ALU.mult, [psS, cb], [at])
                    psO = PS[3]
                    for h in range(4):
                        for ec in range(2):
                            j = h * 2 + ec
                            mm(psO[:, j * 64:j * 64 + L], Sb[:, h, ec * 128:(ec + 1) * 128], qt[h][:, cl:cl + L], [Sb, qt[h]], [psO],
                               start=True, stop=False, sig=False)
                            mm(psO[:, j * 64:j * 64 + L], vt[0:L, h * 256 + ec * 128:h * 256 + (ec + 1) * 128], at[0:L, h, 0:L],
                               [vt, at], [psO], start=False, stop=True, sig=(j == 7))
                    cp("act", OT[:, :, cl:cl + L], psO[:, :].rearrange("p (j i) -> p j i", j=8)[:, :, 0:L], [psO], [OT])
                    for h in range(4):
                        ps = PS[4 + h // 2]
                        mm(ps[:, (h % 2) * 256:(h % 2 + 1) * 256], kk[0:L, h, :], vt[0:L, h * 256:(h + 1) * 256], [kk, vt], [ps],
                           sig=(h % 2 == 1))
                    for h in range(4):
                        ps = PS[4 + h // 2]
                        stt(St[:, h, :], St[:, h, :], eb[h][:, cl + L - 1:cl + L], ps[:, (h % 2) * 256:(h % 2 + 1) * 256],
                            ALU.mult, ALU.add, [St, eb[h], ps], [St])
                    cp("pool", Sb[:], St[:], [St], [Sb])

                khT4 = [sb(st, f"khT{h}", [128, GT], BF16) for h in range(4)]

                for ti, (c0, n) in enumerate(GTILES):
                    thin = (ti == 0)
                    for kc in range(NKC):
                        cp(["dve", "pool"][kc % 2], xb[:, kc, 0:n], X[:, kc, c0:c0 + n], [X], [xb])
                    for kc in range(NKC):
                        mm(PS[6][0:16, 0:n], wg[:, kc, :], xb[:, kc, 0:n], [Wg, xb], [PS[6]], start=(kc == 0), stop=(kc == NKC - 1),
                           sig=(kc == NKC - 1))
                    cp("act", glow[:, 0:n], PS[6][0:16, 0:n], [PS[6]], [glow])
                    mk = smask[:, GT:GT + n] if thin else smask[:, 0:n]
                    for h in range(4):
                        mm(PS[6][:, 0:n], wgate[:, h * 128:(h + 1) * 128], glow[:, 0:n], [wgate, glow], [PS[6]])
                        act(e1[:, 0:n], PS[6][:, 0:n], AF.Exp, [PS[6], negb], [e1], scale=-1.0, bias=negb[:, h:h + 1])
                        act(e1[:, 0:n], e1[:, 0:n], AF.Ln, [e1], [e1], bias=1.0)
                        S.op("dve", lambda e: e.tensor_tensor_scan(out=bc[:, 0:n], data0=mk, data1=e1[:, 0:n], initial=0.0,
                                                                    op0=ALU.mult, op1=ALU.add), r=[smask.b, e1.b], w=[bc.b])
                        act(eb[h][:, 0:n], bc[:, 0:n], AF.Exp, [bc], [eb[h]], scale=-1.0 / 16.0)
                        act(enb[:, 0:n], bc[:, 0:n], AF.Exp, [bc], [enb], scale=1.0 / 16.0)
                        for kc in range(NKC):
                            mm(PS[0][:, 0:n], wq[:, kc, h * 128:(h + 1) * 128], xb[:, kc, 0:n], [Wq, xb], [PS[0]], start=(kc == 0),
                               stop=(kc == NKC - 1), sig=(kc == NKC - 1))
                        stt(qt[h][:, 0:n], PS[0][:, 0:n], 128.0 ** -0.5, eb[h][:, 0:n], ALU.mult, ALU.mult, [PS[0], eb[h]], [qt[h]])
                        for kc in range(NKC):
                            mm(PS[1][:, 0:n], wk[:, kc, h * 128:(h + 1) * 128], xb[:, kc, 0:n], [Wk, xb], [PS[1]], start=(kc == 0),
                               stop=(kc == NKC - 1), sig=(kc == NKC - 1))
                        tt("dve", kt[h][:, 0:n], PS[1][:, 0:n], enb[:, 0:n], ALU.mult, [PS[1], enb], [kt[h]])
                        if thin:
                            tt("pool", khT4[h][:, 0:16], kt[h][:, 0:16], eb[h][:, 15:16].to_broadcast([128, 16]), ALU.mult,
                               [kt[h], eb[h]], [khT4[h]])
                            tt("pool", khT4[h][:, 16:n], kt[h][:, 16:n], eb[h][:, 16:n], ALU.mult, [kt[h], eb[h]], [khT4[h]])
                        else:
                            nch = n // 64
                            tt("pool", khT4[h][:, 0:n].rearrange("p (c l) -> p c l", c=nch),
                               kt[h][:, 0:n].rearrange("p (c l) -> p c l", c=nch),
                               eb[h][:, 0:n].rearrange("p (c l) -> p c l", c=nch)[:, :, 63:64].to_broadcast([128, nch, 64]),
                               ALU.mult, [kt[h], eb[h]], [khT4[h]])
                    if thin:
                        S.op("dve", lambda e: e.memset(OT[:, :, 0:n], 0.0), w=[OT.b])
                        chunk(0, 16, Sst, Sbf)
                        for i in range(32):
                            S0, S0bf = S0s[i % 2], S0bfs[i % 2]
                            S.dma("sp", S0[:], state_own[i].rearrange("h p e -> p h e"), srecs[i % 2], w=[S0.b])
                            cp("pool", S0bf[:], S0[:], [S0], [S0bf])
                            chunk(16 + i, 1, S0, S0bf)
                            out_dma(gss[i].rearrange("h p e -> p h e"), S0[:], [S0])
                    else:
                        for ci in range(n // 64):
                            chunk(ci * 64, 64, Sst, Sbf)
                    for h in range(4):
                        act(sq[:, :, 0:n], OT[:, 2 * h:2 * h + 2, 0:n], AF.Square, [OT], [sq])
                        for ec in range(2):
                            mm(PS[6][:, 0:n], O256, sq[:, ec, 0:n], [cb, sq], [PS[6]], start=(ec == 0), stop=(ec == 1), sig=(ec == 1))
                        act(rstd[:, 0:n], PS[6][:, 0:n], AF.Ln, [PS[6]], [rstd], bias=epsb[:, 0:1])
                        act(rstd[:, 0:n], rstd[:, 0:n], AF.Exp, [rstd], [rstd], scale=-0.5)
                        for ec in range(2):
                            j = 2 * h + ec
                            ps = PS[ec]
                            for kc in range(NKC):
                                mm(ps[:, 0:n], wr[:, kc, j * 128:(j + 1) * 128], xb[:, kc, 0:n], [Wr, xb], [ps], start=(kc == 0),
                                   stop=(kc == NKC - 1), sig=(kc == NKC - 1))
                            act(sr[:, 0:n], ps[:, 0:n], AF.Silu, [ps], [sr])
                            stt(tq[:, 0:n], OT[:, j, 0:n], pp[:, 68 + ec:69 + ec], rstd[:, 0:n], ALU.mult, ALU.mult, [OT, pp, rstd], [tq])
                            tt("pool", og[:, j, 0:n], tq[:, 0:n], sr[:, 0:n], ALU.mult, [tq, sr], [og])
                    for dc in range(NKC):
                        ps = PS[dc % 2]
                        for j in range(8):
                            mm(ps[:, 0:n], wo[:, j, dc * 128:(dc + 1) * 128], og[:, j, 0:n], [Wo, og], [ps], start=(j == 0), stop=(j == 7),
                               sig=(j == 7))
                        stt(X[:, dc, c0:c0 + n], X[:, dc, c0:c0 + n], ALPHA, ps[:, 0:n], ALU.mult, ALU.add, [X, ps], [X])
                    layer_norm(c0, n, 0, 8, xb, og, [e1, bc, sr, tq], PS[2], PS[3])
                out_dma(gsp.rearrange("h p e -> p h e"), Sst[:], [Sst])
                S.barrier()


        def diff():
            with ExitStack() as st2:
                VT = sb(st2, "VT", [128, 17, 1024], BF16)
                KTv = WB[:, 16384:16384 + 8 * NT].rearrange("p (h n) -> p h n", h=8)
                KT = T(None, "KT")
                R0, R1 = T(None, "wr0"), T(None, "wr1")
                w0v = wview(0, 8, 1024)
                w1v = wview(8192, 8, 1024)
                neglam = sb(st2, "neglam", [128, 1])
                gsub = sb(st2, "gsub", [128, 1])
                wd = diff_w_in.rearrange("(k p) n -> p k n", p=128)
                crec = S.rec("cs")
                crec2 = S.rec("sn")
                with ExitStack() as st:
                    lp = sb(st, "lp", [1, 256])
                    pr = sb(st, "pr", [1, 128])
                    s2 = sb(st, "s2", [1, 2])
                    l1 = sb(st, "l1", [1, 1])
                    S.dma("sp", lp[:], lam_d, crec, w=[lp.b])
                    tt("dve", pr[:].rearrange("p (a d) -> p a d", a=2), lp[:].rearrange("p (a b d) -> p a b d", a=2, b=2)[:, :, 0, :],
                       lp[:].rearrange("p (a b d) -> p a b d", a=2, b=2)[:, :, 1, :], ALU.mult, [lp], [pr])
                    S.op("dve", lambda e: e.tensor_reduce(out=s2[:], in_=pr[:].rearrange("p (a d) -> p a d", a=2),
                                                          axis=mybir.AxisListType.X, op=ALU.add), r=[pr.b], w=[s2.b])
                    act(s2[:], s2[:], AF.Exp, [s2], [s2])
                    tt("dve", l1[:], s2[:, 1:2], s2[:, 0:1], ALU.subtract, [s2], [l1])
                    ts("dve", l1[:], l1[:], -LAM_INIT, None, ALU.add, ALU.bypass, [l1], [l1])
                    mm(PS[6][:, 0:1], cf[0:1, 1, :], l1[0:1, 0:1], [cf, l1], [PS[6]])
                    cp("act", neglam[:], PS[6][:, 0:1], [PS[6]], [neglam])
                    ts("dve", gsub[:], pp[:, 70:71], 1.0 - LAM_INIT, None, ALU.mult, ALU.bypass, [pp], [gsub])
                    S.barrier()

                with ExitStack() as st:
                    S.dma("pool", w0v, wd[:, :, 1024:2048], wrecs[0], w=[R0.b])
                    S.dma("pool", w1v, wd[:, :, 2048:3072], wrecs[1], w=[R1.b])
                    xb = sb(st, "xbA", [128, NKC, 512], BF16)
                    cs = sb(st, "cosA", [128, 512])
                    sn = sb(st, "sinA", [128, 512])
                    kf = sb(st, "kf", [128, 512])
                    t1 = sb(st, "t1A", [128, 512])
                    t2 = sb(st, "t2A", [128, 512])
                    kst = [sb(st, f"kst{i}", [128, 512]) for i in range(2)]
                    vst = [sb(st, f"vst{i}", [128, 1024]) for i in range(2)]
                    vi = 0
                    order = list(range(1, len(TILES))) + [0]
                    for ti in order:
                        c0, n = TILES[ti]
                        for kc in range(NKC):
                            cp(["dve", "pool"][kc % 2], xb[:, kc, 0:n], X[:, kc, c0:c0 + n], [X], [xb])
                        S.dma("sp", cs[:, 0:n], cos_d[:, c0:c0 + n], crec, w=[cs.b])
                        S.dma("sp", sn[:, 0:n], sin_d[:, c0:c0 + n], crec2, w=[sn.b])
                        for h in range(8):
                            ps = PS[h % 2]
                            for kc in range(NKC):
                                mm(ps[:, 0:n], w0v[:, kc, h * 128:(h + 1) * 128], xb[:, kc, 0:n], [R0, xb], [ps], start=(kc == 0),
                                   stop=(kc == NKC - 1), sig=(kc == NKC - 1))
                            cp("act", kf[:, 0:n], ps[:, 0:n], [ps], [kf])
                            ps2 = PS[2 + h % 2]
                            mm(ps2[:, 0:n], PERM, kf[:, 0:n], [cf, kf], [ps2])
                            tt("pool", t1[:, 0:n], kf[:, 0:n], cs[:, 0:n], ALU.mult, [kf, cs], [t1])
                            tt("dve", t2[:, 0:n], ps2[:, 0:n], sn[:, 0:n], ALU.mult, [ps2, sn], [t2])
                            ks_ = kst[h % 2]
                            tt("dve", ks_[:, 0:n], t1[:, 0:n], t2[:, 0:n], ALU.add, [t1, t2], [ks_])
                            cp("act", KTv[:, h, c0:c0 + n], ks_[:, 0:n], [ks_], [KT])
                            out_dma(kTo[:, h, c0:c0 + n], ks_[:, 0:n], [ks_])
                        nblk = max(1, n // 128)
                        for blk in range(nblk):
                            rows = min(128, n)
                            kt_i = 0 if ti == 0 else 1 + (ti - 1) * 4 + blk
                            v_ = vst[vi % 2]
                            vi += 1
                            for half in range(2):
                                ps = PS[4 + half]
                                for kc in range(NKC):
                                    mm(ps[0:rows, :], xb[:, kc, blk * 128:blk * 128 + rows], w1v[:, kc, half * 512:(half + 1) * 512],
                                       [xb, R1], [ps], start=(kc == 0), stop=(kc == NKC - 1), sig=(kc == NKC - 1))
                                cp("act" if half else "dve", v_[0:rows, half * 512:(half + 1) * 512], ps[0:rows, :], [ps], [v_])
                            cp("pool", VT[0:rows, kt_i, :], v_[0:rows, :], [v_], [VT])
                            out_dma(vro[c0 + blk * 128:c0 + blk * 128 + rows, :], v_[0:rows, :], [v_])
                    S.barrier()


                if int(os.environ.get('K_STOP', '9')) == 2:
                    return
                if with_sample:
                    with ExitStack() as st:
                        wc = sb(st, "wc", [128, NKC, 3, 128], BF16)
                        S.dma("pool", wc[:], wqkv_c.rearrange("(k p) j d -> p k j d", p=128), wrecs[2], w=[wc.b])
                        xbs = sb(st, "xbs", [128, NKC, 32], BF16)
                        cS = sb(st, "cS", [128, 32])
                        sS = sb(st, "sS", [128, 32])
                        S.dma("sp", cS[:], cos_d[:, 16:48], crec, w=[cS.b])
                        S.dma("sp", sS[:], sin_d[:, 16:48], crec2, w=[sS.b])
                        cp("dve", xbs[:], X[:, :, 16:48], [X], [xbs])
                        f3 = [sb(st, f"f3_{j}", [128, 32]) for j in range(3)]
                        ta = sb(st, "ta", [128, 32])
                        tb = sb(st, "tb", [128, 32])
                        for j in range(3):
                            ps = PS[4 + j % 2]
                            for kc in range(NKC):
                                mm(ps[:, 0:32], wc[:, kc, j, :], xbs[:, kc, :], [wc, xbs], [ps], start=(kc == 0), stop=(kc == NKC - 1),
                                   sig=(kc == NKC - 1))
                            cp("act", f3[j][:], ps[:, 0:32], [ps], [f3[j]])
                            if j < 2:
                                sc_ = 0.125 if j == 0 else 1.0
                                mm(PS[6][:, 0:32], PERM, f3[j][:], [cf, f3[j]], [PS[6]])
                                stt(ta[:], f3[j][:], sc_, cS[:], ALU.mult, ALU.mult, [f3[j], cS], [ta])
                                stt(tb[:], PS[6][:, 0:32], sc_, sS[:], ALU.mult, ALU.mult, [PS[6], sS], [tb])
                                tt("dve", f3[j][:], ta[:], tb[:], ALU.add, [ta, tb], [f3[j]])
                        qs, ks, vs = f3
                        Qb = sb(st, "Qb", [128, 32, 2], BF16)
                        S.op("dve", lambda e: e.memset(Qb[:], 0.0), w=[Qb.b])
                        cp("dve", Qb[0:64, :, 0], qs[0:64, :], [qs], [Qb])
                        cp("dve", Qb[64:128, :, 1], qs[64:128, :], [qs], [Qb])
                        STG = int(os.environ.get('PS_STAGE', '9'))
                        NPT = NSLOT * NPAGES
                        if STG >= 1:
                            idx = sb(st, "idx", [128, NPT], I32)
                            io = sb(st, "io", [128, 1])
                            ptrec = S.rec("pt")
                            S.dma("sp", idx[:], pt_d.to_broadcast([128, NPT]), ptrec, w=[idx.b])
                            SUB = int(os.environ.get('PS_SUB', '9'))
                            if SUB >= 1:
                                S.op("pool", lambda e: e.iota(io[:], pattern=[[0, 1]], base=0, channel_multiplier=1,
                                                              allow_small_or_imprecise_dtypes=True), w=[io.b])
                            if SUB >= 2:
                                ts("pool", idx[:], idx[:], 128.0, io[:, 0:1], ALU.mult, ALU.add, [idx, io], [idx])
                        NSL = 16
                        pg = [sb(st, f"pg{i}", [128, 256], BF16) for i in range(NSL)]
                        prec = [S.rec(f"pg{i}") for i in range(NSL)]
                        OTs = sb(st, "OTs", [128, 32, 2])
                        Ls = sb(st, "Ls", [128, 32, 2])
                        Pb = [sb(st, f"Pb{i}", [128, 16], BF16) for i in range(2)]
                        gi_ = 0
                        for s_ in range(32 if STG >= 3 else (1 if STG == 2 else 0)):
                            for g in range(8):
                                slots = []
                                for j in range(8):
                                    pj = s_ * 64 + g * 8 + j
                                    sl = (gi_ * 8 + j) % NSL
                                    slots.append(sl)
                                    S._deps("pool", [idx.b], [pg[sl].b])
                                    ins = nc.gpsimd.indirect_dma_start(
                                        out=pg[sl][:], out_offset=None, in_=cache_d,
                                        in_offset=bass.IndirectOffsetOnAxis(ap=idx[:, pj:pj + 1], axis=0))
                                    rec = prec[sl]
                                    rec.val += 16
                                    ins.then_inc(rec.sem, 16)
                                    pg[sl].b.w = ("d", rec, rec.val)
                                    pg[sl].b.r = {}
                                    idx.b.r["d%d" % id(rec)] = ("d", rec, rec.val)
                                pS = PS[4 + gi_ % 2]
                                for j in range(8):
                                    mm(pS[:, 2 * j:2 * j + 2], pg[slots[j]][:, 0:128], Qb[:, s_, :], [pg[slots[j]], Qb], [pS], sig=(j == 7))
                                pb = Pb[gi_ % 2]
                                act(pb[:], pS[:, 0:16], AF.Exp, [pS], [pb])
                                for j in range(8):
                                    mm(PS[0][:, 0:2], pg[slots[j]][:, 128:256], pb[:, 2 * j:2 * j + 2], [pg[slots[j]], pb], [PS[0]],
                                       start=(g == 0 and j == 0), stop=(g == 7 and j == 7), sig=(j == 7))
                                mm(PS[1][:, 0:16], ONESB, pb[:], [cb, pb], [PS[1]], start=(g == 0), stop=(g == 7))
                                gi_ += 1
                            cp("act", OTs[:, s_, :], PS[0][:, 0:2], [PS[0]], [OTs])
                            S.op("dve", lambda e, s_=s_: e.tensor_reduce(out=Ls[:, s_, :],
                                                                        in_=PS[1][:, 0:16].rearrange("p (j m) -> p m j", m=2),
                                                                        axis=mybir.AxisListType.X, op=ALU.add), r=[PS[1].b], w=[Ls.b])
                        if STG >= 4:
                            tt("dve", ta[:], qs[:], ks[:], ALU.mult, [qs, ks], [ta])
                            for m in range(2):
                                mm(PS[6][:, m * 32:(m + 1) * 32], cf[:, 7 + m, :], ta[:], [cf, ta], [PS[6]],
                                   sig=(m == 1))
                            pself = sb(st, "pself", [128, 2, 32])
                            num = sb(st, "num", [128, 2, 32])
                            den = sb(st, "den", [128, 2, 32])
                            act(pself[:], PS[6][:, 0:64].rearrange("p (m s) -> p m s", m=2), AF.Exp, [PS[6]], [pself])
                            for m in range(2):
                                tt("dve", num[:, m, :], pself[:, m, :], vs[:], ALU.mult, [pself, vs], [num])
                                tt("dve", num[:, m, :], num[:, m, :], OTs[:, :, m], ALU.add, [num, OTs], [num])
                                tt("dve", den[:, m, :], pself[:, m, :], Ls[:, :, m], ALU.add, [pself, Ls], [den])
                            act(den[:], den[:], AF.Ln, [den], [den])
                            act(den[:], den[:], AF.Exp, [den], [den], scale=-1.0)
                            tt("dve", num[:], num[:], den[:], ALU.mult, [num, den], [num])
                            stt(ta[:], num[:, 1, :], neglam[:, 0:1], num[:, 0, :], ALU.mult, ALU.add, [num, neglam], [ta])
                            sqs = sb(st, "sqs", [128, 32], BF16)
                            act(sqs[:], ta[:], AF.Square, [ta], [sqs])
                            mm(PS[6][:, 0:32], O128, sqs[:], [cb, sqs], [PS[6]])
                            act(tb[:], PS[6][:, 0:32], AF.Ln, [PS[6]], [tb], bias=epsb[:, 0:1])
                            act(tb[:], tb[:], AF.Exp, [tb], [tb], scale=-0.5)
                            stt(ta[:], ta[:], gsub[:, 0:1], tb[:], ALU.mult, ALU.mult, [ta, gsub, tb], [ta])
                            out_dma(ogs_o, ta[:], [ta])
                        S.barrier()

                with ExitStack() as st:
                    S.dma("pool", w0v, wd[:, :, 0:1024], wrecs[0], w=[R0.b])
                    S.dma("pool", w1v, diff_w_out.rearrange("(k p) n -> p k n", p=128), wrecs[1], w=[R1.b])
                    xb = sb(st, "xbB", [128, NKC, 512], BF16)
                    cs = sb(st, "cosB", [128, 512])
                    sn = sb(st, "sinB", [128, 512])
                    qf = sb(st, "qf", [128, 512])
                    t1 = sb(st, "t1B", [128, 512])
                    t2 = sb(st, "t2B", [128, 512])
                    QT = sb(st, "QT", [128, 8, 512], BF16)
                    og = xb
                    pt = [sb(st, f"pt{i}", [128, 512], BF16) for i in range(3)]
                    rl = [cs, sn]
                    sqb = sb(st, "sqB", [128, 512], BF16)
                    pti = 0
                    for ti, (c0, n) in enumerate(TILES):
                        nq = 16 if ti == 0 else n
                        for kc in range(NKC):
                            cp(["dve", "pool"][kc % 2], xb[:, kc, 0:n], X[:, kc, c0:c0 + n], [X], [xb])
                        S.dma("sp", cs[:, 0:n], cos_d[:, c0:c0 + n], crec, w=[cs.b])
                        S.dma("sp", sn[:, 0:n], sin_d[:, c0:c0 + n], crec2, w=[sn.b])
                        for h in range(8):
                            ps = PS[4 + h % 2]
                            for kc in range(NKC):
                                mm(ps[:, 0:nq], w0v[:, kc, h * 128:(h + 1) * 128], xb[:, kc, 0:nq], [R0, xb], [ps], start=(kc == 0),
                                   stop=(kc == NKC - 1), sig=(kc == NKC - 1))
                            cp("act", qf[:, 0:nq], ps[:, 0:nq], [ps], [qf])
                            mm(PS[6][:, 0:nq], PERM, qf[:, 0:nq], [cf, qf], [PS[6]])
                            stt(t1[:, 0:nq], qf[:, 0:nq], 0.125, cs[:, 0:nq], ALU.mult, ALU.mult, [qf, cs], [t1])
                            stt(t2[:, 0:nq], PS[6][:, 0:nq], 0.125, sn[:, 0:nq], ALU.mult, ALU.mult, [PS[6], sn], [t2])
                            tt("pool", QT[:, h, 0:nq], t1[:, 0:nq], t2[:, 0:nq], ALU.add, [t1, t2], [QT])
                        if ti == 0:
                            kl = [(0, 16, 0, 0, True)]
                        else:
                            i = ti - 1
                            kl = [(0, 16, 0, 0, False)]
                            for r in (3, 2, 1, 0):
                                j = 4 * i + r
                                kl.append((C0P + 128 * j, 128, 1 + j, 128 * r, True))
                            for j in range(4 * i):
                                kl.append((C0P + 128 * j, 128, 1 + j, 0, False))
                        for h in range(8):
                            units = [(m, ki) for m in range(2) for ki in range(len(kl))]
                            pend = None
                            for u in units + [None]:
                                cur = None
                                if u is not None:
                                    m, ki = u
                                    kc0, nk, vti, qlo, diag = kl[ki]
                                    w_ = nq - qlo
                                    pS = PS[4 + pti % 2]
                                    p_ = pt[pti % 3]
                                    pti += 1
                                    mm(pS[0:nk, 0:w_], KTv[m * 64:(m + 1) * 64, h, kc0:kc0 + nk], QT[m * 64:(m + 1) * 64, h, qlo:nq],
                                       [KT, QT], [pS])
                                    act(p_[0:nk, 0:w_], pS[0:nk, 0:w_], AF.Exp, [pS], [p_])
                                    if diag:
                                        dw = min(128, w_)
                                        tt("pool", p_[0:nk, 0:dw], p_[0:nk, 0:dw], TRIB[0:nk, 0:dw], ALU.mult, [p_, cb], [p_])
                                    cur = (m, ki, p_)
                                if pend is not None:
                                    m, ki, p_ = pend
                                    kc0, nk, vti, qlo, diag = kl[ki]
                                    w_ = nq - qlo
                                    first, last = (ki == 0), (ki == len(kl) - 1)
                                    pO, pL = PS[m], PS[2 + m]
                                    mm(pO[:, qlo:nq], VT[0:nk, vti, h * 128:(h + 1) * 128], p_[0:nk, 0:w_], [VT, p_], [pO],
                                       start=first, stop=last)
                                    mm(pL[:, qlo:nq], ONESB[0:nk, :], p_[0:nk, 0:w_], [cb, p_], [pL], start=first, stop=last)
                                    if last:
                                        act(rl[m][:, 0:nq], pL[:, 0:nq], AF.Ln, [pL], [rl[m]])
                                        act(rl[m][:, 0:nq], rl[m][:, 0:nq], AF.Exp, [rl[m]], [rl[m]], scale=-1.0)
                                pend = cur
                            tt("dve", t1[:, 0:nq], PS[0][:, 0:nq], rl[0][:, 0:nq], ALU.mult, [PS[0], rl[0]], [t1])
                            tt("dve", t2[:, 0:nq], PS[1][:, 0:nq], rl[1][:, 0:nq], ALU.mult, [PS[1], rl[1]], [t2])
                            stt(t1[:, 0:nq], t2[:, 0:nq], neglam[:, 0:1], t1[:, 0:nq], ALU.mult, ALU.add, [t2, neglam, t1], [t1])
                            act(sqb[:, 0:nq], t1[:, 0:nq], AF.Square, [t1], [sqb])
                            mm(PS[6][:, 0:nq], O128, sqb[:, 0:nq], [cb, sqb], [PS[6]])
                            act(qf[:, 0:nq], PS[6][:, 0:nq], AF.Ln, [PS[6]], [qf], bias=epsb[:, 0:1])
                            act(qf[:, 0:nq], qf[:, 0:nq], AF.Exp, [qf], [qf], scale=-0.5)
                            stt(og[:, h, 0:nq], t1[:, 0:nq], gsub[:, 0:1], qf[:, 0:nq], ALU.mult, ALU.mult, [t1, gsub, qf], [og])
                        if ti == 0:
                            sample_og(og)
                        for dc in range(NKC):
                            ps = PS[4 + dc % 2]
                            for h in range(8):
                                mm(ps[:, 0:n], w1v[:, h, dc * 128:(dc + 1) * 128], og[:, h, 0:n], [R1, og], [ps], start=(h == 0), stop=(h == 7),
                                   sig=(h == 7))
                            stt(X[:, dc, c0:c0 + n], X[:, dc, c0:c0 + n], ALPHA, ps[:, 0:n], ALU.mult, ALU.add, [X, ps], [X])
                        layer_norm(c0, n, 32, 40, xb, QT, [qf, t1, t2, rl[0]], PS[4], PS[5])
                    S.barrier()

        def sample_og(og):
            S.op("dve", lambda e: e.memset(og[:, :, 16:48], 0.0), w=[og.b])


        def tail_prog():
            with ExitStack() as st:
                ogf = sb(st, "ogf", [128, 8, 32])
                ogb = sb(st, "ogb", [128, 8, 32], BF16)
                trec = S.rec("tl")
                S.dma("sp", ogf[:], og_all_d, trec, w=[ogf.b])
                cp("dve", ogb[:], ogf[:], [ogf], [ogb])
                w1v = wview(8192, 8, 1024)
                R1 = T(None, "r1")
                S.dma("pool", w1v, diff_w_out.rearrange("(k p) n -> p k n", p=128), wrecs[1], w=[R1.b])
                for dc in range(NKC):
                    ps = PS[4 + dc % 2]
                    for h in range(8):
                        mm(ps[:, 0:32], w1v[:, h, dc * 128:(dc + 1) * 128], ogb[:, h, :], [R1, ogb], [ps], start=(h == 0), stop=(h == 7),
                           sig=(h == 7))
                    stt(X[:, dc, 0:32], X[:, dc, 0:32], ALPHA, ps[:, 0:32], ALU.mult, ALU.add, [X, ps], [X])
                zb = sb(st, "zbt", [128, 8, 32], BF16)
                zq = sb(st, "zqt", [128, 8, 32], BF16)
                tmp = [sb(st, f"tt{i}", [128, 32]) for i in range(4)]
                layer_norm(0, 32, 32, 40, zb, zq, tmp, PS[4], PS[5])
                S.barrier()

        if tail:
            tail_prog()
            mlp(1)
            out_dma(yT, X[:], [X])
            S.barrier()
            return nc
        KSTOP = int(os.environ.get('K_STOP', '9'))
        gla()
        if KSTOP >= 1:
            mlp(0)
        out_dma(xmid_o, X[:, :, 16:48], [X])
        if KSTOP >= 2:
            diff()
        if KSTOP >= 4:
            mlp(1)
        out_dma(yT, X[:], [X])
        S.barrier()
    return nc


def _consts():
    cf = np.zeros((128, 10, 128), np.float32)
    cf[0:64, 7, :] = 1.0
    cf[64:128, 8, :] = 1.0
    cf[:, 6, :] = 1.0 / 128
    cf[:, 0, :] = np.eye(128)
    cf[:, 1, :] = 1.0
    cf[:, 2, :] = np.triu(np.ones((128, 128)))
    for p in range(128):
        d = p % 64
        if d < 8:
            cf[p + 8, 3, p] = -1.0
        elif d < 16:
            cf[p - 8, 3, p] = 1.0
    cf[:, 4, :] = 1.0 / 1024
    cf[:, 5, :] = 1.0 / 256
    sm = np.ones((128, GT + C0P), np.float32)
    sm[:, 0:GT:64] = 0.0
    sm[:, GT] = 0.0
    sm[:, GT + 16:] = 0.0
    pos = np.zeros(NT, np.float32)
    pos[0:16] = np.arange(16)
    pos[16:48] = 8192
    pos[48:] = 16 + np.arange(SEQ)
    inv = (np.float32(500000.0) ** (-(np.arange(0, 16, 2, dtype=np.float32)) / np.float32(16))).astype(np.float32)
    rc = np.ones((128, NT), np.float32)
    rs = np.zeros((128, NT), np.float32)
    for p in range(128):
        d = p % 64
        if d < 16:
            ang = (pos * inv[d % 8]).astype(np.float32)
            rc[p] = np.cos(ang)
            rs[p] = np.sin(ang)
    return cf, sm, rc, rs


def kernel(x_prompt, x_sample, state_gla, cache_k, cache_v, page_table, meta_tokens,
           gla_w_in, gla_w_gate, gla_b_gate, gla_norm, gla_w_out,
           diff_w_in, diff_lambda, diff_norm, diff_w_out,
           mlp_w1, mlp_w2, ln_mix_g, ln_mix_b, ln_mlp_g, ln_mlp_b, _ncores=NCORES, _dev=False, _npool=None):
    f = lambda a: np.ascontiguousarray(np.asarray(a, dtype=np.float32))
    x_prompt, x_sample, state_gla, meta_tokens = f(x_prompt), f(x_sample), f(state_gla), f(meta_tokens)
    ncores = _ncores
    npool = 8 if _dev else (_npool or NPOOL)
    cf, sm, rc, rs = _consts()
    pp = np.zeros((128, 80), np.float32)
    lnp = [ln_mix_g, ln_mix_b, ln_mlp_g, ln_mlp_b]
    for layer in range(2):
        for j in range(4):
            pp[:, (layer * 4 + j) * 8:(layer * 4 + j + 1) * 8] = f(lnp[j])[layer].reshape(8, 128).T
    pp[:, 64:68] = f(gla_b_gate)[0].reshape(4, 128).T
    pp[:, 68:70] = f(gla_norm)[0].reshape(2, 128).T
    pp[:, 70] = f(diff_norm)[0]
    tail_shared = {
        "gla_w_gate": f(gla_w_gate)[0], "diff_w_out": f(diff_w_out)[0], "mlp_w1": f(mlp_w1), "mlp_w2": f(mlp_w2),
        "pp": pp, "cf": cf, "smask": sm,
    }
    shared = dict(tail_shared)
    shared.update({
        "gla_w_in": f(gla_w_in)[0], "gla_w_out": f(gla_w_out)[0], "diff_w_in": f(diff_w_in)[0],
        "lam": f(diff_lambda)[0].reshape(1, 256), "rcos": rc, "rsin": rs,
        "ptab": np.ascontiguousarray(np.asarray(page_table, dtype=np.int32).reshape(1, -1)),
        "state_all": np.ascontiguousarray(state_gla[0]),
    })
    dwi = shared["diff_w_in"]
    in_maps = []
    for c in range(ncores):
        xt = np.zeros((D, NT), np.float32)
        xt[:, 0:16] = meta_tokens.T
        xt[:, 16:48] = x_sample[:, 0, :].T
        xt[:, C0P:] = x_prompt[c].T
        m = dict(shared)
        m["xT"] = np.ascontiguousarray(xt.reshape(8, 128, NT).transpose(1, 0, 2))
        m["wqkv_c"] = np.ascontiguousarray(
            np.stack([dwi[:, j * 1024 + c * 128:j * 1024 + (c + 1) * 128] for j in range(3)], axis=1))
        if _dev:
            m["cache"] = np.zeros((npool * 128, 256), np.float32)
        else:
            kc_ = np.asarray(cache_k)[0, :npool, :, c, :]
            vc_ = np.asarray(cache_v)[0, :npool, :, c, :]
            m["cache"] = np.ascontiguousarray(
                np.concatenate([kc_.transpose(0, 2, 1), vc_], axis=2).reshape(npool * 128, 256).astype(np.float32))
        in_maps.append(m)
    nc = build(ncores, npool, not _dev)
    res = run_bass_kernel_spmd(nc, in_maps, core_ids=list(range(ncores))).results
    B = ncores
    y_prompt = np.zeros((B, SEQ, D), np.float32)
    gla_sp = np.zeros((1, B, 4, 128, 256), np.float32)
    k_p = np.zeros((1, B, SEQ + 16, 8, 128), np.float32)
    v_p = np.zeros((1, B, SEQ + 16, 8, 128), np.float32)
    for c in range(ncores):
        r = res[c]
        yt = r["yT"].transpose(1, 0, 2).reshape(D, NT)
        y_prompt[c] = yt[:, C0P:].T
        gla_sp[0, c] = r["gla_sp"]
        kt_ = r["kT_rows"]
        kcols = np.concatenate([kt_[:, :, 0:16], kt_[:, :, C0P:]], axis=2)
        k_p[0, c] = kcols.transpose(2, 1, 0)
        vr = r["v_rows"]
        v_p[0, c] = np.concatenate([vr[0:16], vr[C0P:]], axis=0).reshape(SEQ + 16, 8, 128)
    r0 = res[0]
    gla_ss = np.ascontiguousarray(r0["gla_ss"]).reshape(1, 32, 4, 128, 256)
    k_s = np.ascontiguousarray(r0["kT_rows"][:, :, 16:48].transpose(2, 1, 0)).reshape(1, 32, 1, 8, 128)
    v_s = np.ascontiguousarray(r0["v_rows"][16:48]).reshape(1, 32, 1, 8, 128)
    og_all = np.zeros((128, 8, 32), np.float32)
    for c in range(ncores):
        og_all[:, c, :] = res[c]["ogs"]
    m2 = dict(tail_shared)
    m2["xT"] = np.ascontiguousarray(r0["xs_mid"])
    m2["og_all"] = og_all
    nc2 = build(1, npool, False, tail=True)
    res2 = run_bass_kernel_spmd(nc2, [m2], core_ids=[0]).results[0]
    y_sample = np.ascontiguousarray(res2["yT"].transpose(1, 0, 2).reshape(D, 32).T).reshape(32, 1, D)
    return (y_prompt, y_sample, gla_sp, gla_ss, k_p, v_p, k_s, v_s)
```

```python
import math
import os
from contextlib import ExitStack
import numpy as np
import concourse.bass as bass
import concourse.mybir as mybir
from concourse.bass_utils import run_bass_kernel_spmd

F32 = mybir.dt.float32
BF16 = mybir.dt.bfloat16
I32 = mybir.dt.int32
AF = mybir.ActivationFunctionType
ALU = mybir.AluOpType

D = 1024
NKC = 8
NMETA = 16
NSLOT = 32
C0P = NMETA + NSLOT
SEQ = 2048
NT = C0P + SEQ
NCORES = 8
DEPTH = 2
ALPHA = (2 * DEPTH) ** 0.25
EPS = 1e-5
PAGE = 128
NPAGES = 64
NPOOL = 2560
EPOCH = 4000
TILES = [(0, C0P)] + [(C0P + 512 * i, 512) for i in range(SEQ // 512)]
GT = 256
GTILES = [(0, C0P)] + [(C0P + GT * i, GT) for i in range(SEQ // GT)]
LAM_INIT = 0.8 - 0.6 * math.exp(-0.3 * 1)
DEBUG = False


class Buf:
    __slots__ = ("name", "w", "r")

    def __init__(self, name):
        self.name = name
        self.w = None
        self.r = {}


class SemRec:
    def __init__(self, sem):
        self.sem = sem
        self.val = 0


class Sched:
    def __init__(self, nc, stack):
        self.nc = nc
        self.stack = stack
        self.eng = {"pe": nc.tensor, "act": nc.scalar, "dve": nc.vector, "pool": nc.gpsimd, "sp": nc.sync}
        self.cnt = {k: 0 for k in self.eng}
        self.sems = {k: [] for k in self.eng}
        self.waited = {k: {} for k in self.eng}
        self.recs = []
        self.nsem = 0

    def new_sem(self, name):
        self.nsem += 1
        return self.stack.enter_context(self.nc.semaphore(name))

    def rec(self, name):
        r = SemRec(self.new_sem("d_" + name))
        self.recs.append(r)
        return r

    def esem(self, e, i):
        ep = (i - 1) // EPOCH
        while len(self.sems[e]) <= ep:
            self.sems[e].append(self.new_sem(f"e_{e}_{len(self.sems[e])}"))
        return self.sems[e][ep], (i - 1) % EPOCH + 1

    def _wait(self, e, dep):
        if dep[0] == "e":
            _, e2, i = dep
            if e2 == e and e == "pe":
                return
            if self.waited[e].get(e2, 0) >= i:
                return
            sem, val = self.esem(e2, i)
            self.eng[e].wait_ge(sem, val)
            self.waited[e][e2] = i
        else:
            _, rec, val = dep
            if self.waited[e].get(id(rec), 0) >= val:
                return
            self.eng[e].wait_ge(rec.sem, val)
            self.waited[e][id(rec)] = val

    def _deps(self, e, r, w):
        for b in r:
            if b.w is not None:
                self._wait(e, b.w)
        for b in w:
            if b.w is not None:
                self._wait(e, b.w)
            for k, d in b.r.items():
                if k != e:
                    self._wait(e, d)

    def op(self, e, fn, r=(), w=(), sig=True):
        self._deps(e, r, w)
        ins = fn(self.eng[e])
        nid = self.cnt[e] + 1
        if sig:
            self.cnt[e] = nid
            sem, _ = self.esem(e, nid)
            ins.then_inc(sem, 1)
        me = ("e", e, nid)
        for b in r:
            b.r[e] = me
        for b in w:
            b.w = me
            b.r = {}
        return ins

    def dma(self, q, out, in_, rec, r=(), w=(), **kw):
        self._deps(q, r, w)
        ins = self.eng[q].dma_start(out=out, in_=in_, **kw)
        rec.val += 16
        ins.then_inc(rec.sem, 16)
        me = ("d", rec, rec.val)
        for b in r:
            b.r["d%d" % id(rec)] = me
        for b in w:
            b.w = me
            b.r = {}
        return ins

    def barrier(self):
        for e in self.eng:
            for e2 in self.eng:
                if e2 != e and self.cnt[e2] > 0:
                    self._wait(e, ("e", e2, self.cnt[e2]))
            for rec in self.recs:
                if rec.val > 0:
                    self._wait(e, ("d", rec, rec.val))


class T:
    def __init__(self, t, name):
        self.t = t
        self.b = Buf(name)

    def __getitem__(self, k):
        return self.t[k]


def build(ncores, npool, with_sample, tail=False):
    nc = bass.Bass("TRN2", target_bir_lowering=False)
    dram = {}
    ntot = 32 if tail else NT
    tiles = [(0, 32)] if tail else TILES

    def din(name, shape, dt=F32):
        dram[name] = nc.dram_tensor(name, list(shape), dt, kind="ExternalInput").ap()
        return dram[name]

    def dout(name, shape, dt=F32):
        dram[name] = nc.dram_tensor(name, list(shape), dt, kind="ExternalOutput").ap()
        return dram[name]

    xT = din("xT", [128, NKC, ntot])
    gla_w_gate = din("gla_w_gate", [16, 512])
    diff_w_out = din("diff_w_out", [D, D])
    mlp_w1 = din("mlp_w1", [2, D, 4 * D])
    mlp_w2 = din("mlp_w2", [2, 4 * D, D])
    pp_d = din("pp", [128, 80])
    cf_d = din("cf", [128, 10, 128])
    smask_d = din("smask", [128, GT + C0P])
    yT = dout("yT", [128, NKC, ntot])
    og_all_d = din("og_all", [128, 8, 32]) if tail else None
    if not tail:
        state_own = din("state_all", [32, 4, 128, 256])
        gla_w_in = din("gla_w_in", [D, 3088])
        gla_w_out = din("gla_w_out", [D, D])
        diff_w_in = din("diff_w_in", [D, 3072])
        lam_d = din("lam", [1, 256])
        cos_d = din("rcos", [128, NT])
        sin_d = din("rsin", [128, NT])
        wqkv_c = din("wqkv_c", [D, 3, 256])
        cache_d = din("cache", [npool * 128, 512])
        pt_d = din("ptab", [1, 16 * NPAGES], I32)
        gsp = dout("gla_sp", [4, 128, 256])
        gss = dout("gla_ss", [32, 4, 128, 256])
        xmid_o = dout("xs_mid", [128, NKC, 32])
        ogs_o = dout("ogs", [128, 2, 16])
        kTo = dout("kT_rows", [128, 8, NT])
        vro = dout("v_rows", [NT, D])
    dbg = [dout(f"dbg{i}", [128, NKC, NT]) for i in range(2)] if (DEBUG and not tail) else []

    with ExitStack() as top:
        S = Sched(nc, top)
        uid = [0]

        def sb(st, name, shape, dt=F32):
            uid[0] += 1
            return T(st.enter_context(nc.sbuf_tensor("s%d_" % uid[0] + name, list(shape), dt)), name)

        def pst(st, name, shape=(128, 512), dt=F32):
            uid[0] += 1
            return T(st.enter_context(nc.psum_tensor("p%d_" % uid[0] + name, list(shape), dt)), name)

        def bufs(ts):
            return [t.b for t in ts]

        def mm(out, lhsT, rhs, r, w, start=True, stop=True, sig=True):
            return S.op("pe", lambda e: e.matmul(out, lhsT=lhsT, rhs=rhs, start=start, stop=stop),
                        r=bufs(r), w=bufs(w), sig=sig)

        def act(out, in_, func, r, w, bias=None, scale=None):
            kw = {}
            if bias is not None:
                kw["bias"] = bias
            if scale is not None:
                kw["scale"] = scale
            return S.op("act", lambda e: e.activation(out=out, in_=in_, func=func, **kw), r=bufs(r), w=bufs(w))

        def tt(eng, out, a, b, op, r, w):
            return S.op(eng, lambda e: e.tensor_tensor(out=out, in0=a, in1=b, op=op), r=bufs(r), w=bufs(w))

        def stt(out, a, scalar, b, op0, op1, r, w):
            return S.op("dve", lambda e: e.scalar_tensor_tensor(out=out, in0=a, scalar=scalar, in1=b, op0=op0, op1=op1),
                        r=bufs(r), w=bufs(w))

        def ts(eng, out, a, s1, s2, op0, op1, r, w):
            return S.op(eng, lambda e: e.tensor_scalar(out=out, in0=a, scalar1=s1, scalar2=s2, op0=op0, op1=op1),
                        r=bufs(r), w=bufs(w))

        def cp(eng, out, in_, r, w):
            if eng == "act":
                return S.op("act", lambda e: e.copy(out=out, in_=in_), r=bufs(r), w=bufs(w))
            return S.op(eng, lambda e: e.tensor_copy(out=out, in_=in_), r=bufs(r), w=bufs(w))

        X = sb(top, "X", [128, NKC, ntot])
        WB = sb(top, "WB", [128, 33280], BF16)
        pp = sb(top, "pp", [128, 80])
        cf = sb(top, "cf", [128, 10, 128])
        cb = sb(top, "cb", [128, 10, 128], BF16)
        smask = sb(top, "smask", [128, GT + C0P])
        wgate = sb(top, "wgate", [16, 512])
        negb = sb(top, "negb", [128, 4])
        epsb = sb(top, "epsb", [128, 1])
        r_ld = S.rec("ld")
        S.dma("sp", X[:], xT, r_ld, w=[X.b])
        S.dma("sp", pp[:], pp_d, r_ld, w=[pp.b])
        S.dma("sp", cf[:], cf_d, r_ld, w=[cf.b])
        S.dma("sp", smask[:], smask_d, r_ld, w=[smask.b])
        S.dma("sp", wgate[:], gla_w_gate, r_ld, w=[wgate.b])
        for t_ in (X, pp, cf, smask, wgate):
            t_.b.w = ("d", r_ld, r_ld.val)
        cp("dve", cb[:], cf[:], [cf], [cb])
        ts("dve", negb[:], pp[:, 64:68], -1.0, None, ALU.mult, ALU.bypass, [pp], [negb])
        S.op("dve", lambda e: e.memset(epsb[:], EPS), w=[epsb.b])
        IDENT = cb[:, 0, :]
        ONESB = cb[:, 1, :]
        TRIB = cb[:, 2, :]
        PERM = cf[:, 3, :]
        O1024 = cb[:, 4, :]
        O256 = cb[:, 5, :]
        O128 = cb[:, 6, :]

        PS = [pst(top, f"ps{i}") for i in range(7)]
        PSB = pst(top, "psb", (128, 1024), BF16)
        wrecs = [S.rec(f"w{i}") for i in range(6)]
        orec = {}

        def out_dma(out, in_, r):
            key = r[0].b.name
            if key not in orec:
                orec[key] = S.rec("o_" + key)
            S.dma("sp", out, in_, orec[key], r=bufs(r))

        def wload(dst_ap, src_ap, buf, i):
            S.dma("pool", dst_ap, src_ap, wrecs[i], w=[buf])

        def wview(lo, k, n):
            return WB[:, lo:lo + k * n].rearrange("p (k n) -> p k n", k=k)

        def layer_norm(c0, n, gcol, bcol, zb, zsq, tmp, psA, psB):
            m2, var, n1, n2 = tmp
            xs = X[:, :, c0:c0 + n]
            cp("pool", zb[:, :, 0:n], xs, [X], [zb])
            act(zsq[:, :, 0:n], xs, AF.Square, [X], [zsq])
            for kc in range(NKC):
                mm(psA[:, 0:n], O1024, zb[:, kc, 0:n], [cb, zb], [psA], start=(kc == 0), stop=(kc == NKC - 1), sig=(kc == NKC - 1))
            for kc in range(NKC):
                mm(psB[:, 0:n], O1024, zsq[:, kc, 0:n], [cb, zsq], [psB], start=(kc == 0), stop=(kc == NKC - 1), sig=(kc == NKC - 1))
            act(m2[:, 0:n], psA[:, 0:n], AF.Square, [psA], [m2])
            tt("dve", var[:, 0:n], psB[:, 0:n], m2[:, 0:n], ALU.subtract, [psB, m2], [var])
            act(var[:, 0:n], var[:, 0:n], AF.Ln, [var], [var], bias=epsb[:, 0:1])
            act(var[:, 0:n], var[:, 0:n], AF.Exp, [var], [var], scale=-0.5)
            cp("act", m2[:, 0:n], psA[:, 0:n], [psA], [m2])
            for kc in range(NKC):
                tt("dve", n1[:, 0:n], X[:, kc, c0:c0 + n], m2[:, 0:n], ALU.subtract, [X, m2], [n1])
                tt("pool", n2[:, 0:n], n1[:, 0:n], var[:, 0:n], ALU.mult, [n1, var], [n2])
                act(X[:, kc, c0:c0 + n], n2[:, 0:n], AF.Identity, [n2, pp], [X],
                    scale=pp[:, gcol + kc:gcol + kc + 1], bias=pp[:, bcol + kc:bcol + kc + 1])

        def mlp(layer):
            with ExitStack() as st:
                Xb = sb(st, "Xb", [128, NKC, ntot], BF16)
                hT = [sb(st, f"hT{i}", [128, 8, 512], BF16) for i in range(2)]
                rl = [sb(st, f"rl{i}", [128, 512]) for i in range(2)]
                tmp = [sb(st, f"lt{i}", [128, 512]) for i in range(4)]
                wb = [Buf("w1a"), Buf("w2a"), Buf("w1b"), Buf("w2b")]
                for (c0, n) in tiles:
                    for kc in range(NKC):
                        eng = ["dve", "act", "pool"][kc % 3]
                        cp(eng, Xb[:, kc, c0:c0 + n], X[:, kc, c0:c0 + n], [X], [Xb])
                w1d = mlp_w1[layer].rearrange("(k p) n -> p k n", p=128)
                w2d = mlp_w2[layer].rearrange("(k p) n -> p k n", p=128)
                hi = 0
                for q in range(4):
                    par = q % 2
                    w1v = wview(par * 16384, 8, 1024)
                    w2v = wview(par * 16384 + 8192, 8, 1024)
                    b1, b2 = wb[par * 2], wb[par * 2 + 1]
                    wload(w1v, w1d[:, :, q * 1024:(q + 1) * 1024], b1, par * 2)
                    wload(w2v, w2d[:, q * 8:(q + 1) * 8, :], b2, par * 2 + 1)
                    W1 = T(None, "w1")
                    W1.b = b1
                    W2 = T(None, "w2")
                    W2.b = b2
                    for (c0, n) in tiles:
                        h = hT[hi % 2]
                        hi += 1
                        for fc in range(8):
                            ps = PS[fc % 2]
                            for kc in range(NKC):
                                mm(ps[:, 0:n], w1v[:, kc, fc * 128:(fc + 1) * 128], Xb[:, kc, c0:c0 + n], [W1, Xb], [ps],
                                   start=(kc == 0), stop=(kc == NKC - 1), sig=(kc == NKC - 1))
                            r_ = rl[fc % 2]
                            act(r_[:, 0:n], ps[:, 0:n], AF.Relu, [ps], [r_])
                            tt("pool" if fc % 2 else "dve", h[:, fc, 0:n], r_[:, 0:n], r_[:, 0:n], ALU.mult, [r_], [h])
                        for dc in range(NKC):
                            ps = PS[2 + dc % 2]
                            for fc in range(8):
                                mm(ps[:, 0:n], w2v[:, fc, dc * 128:(dc + 1) * 128], h[:, fc, 0:n], [W2, h], [ps],
                                   start=(fc == 0), stop=(fc == 7), sig=(fc == 7))
                            stt(X[:, dc, c0:c0 + n], X[:, dc, c0:c0 + n], ALPHA if q == 0 else 1.0, ps[:, 0:n],
                                ALU.mult, ALU.add, [X, ps], [X])
                gi = (layer * 4 + 2) * 8
                for (c0, n) in tiles:
                    layer_norm(c0, n, gi, gi + 8, hT[0], hT[1], tmp, PS[4], PS[5])
                S.barrier()

        def gla():
            with ExitStack() as st:
                wq = wview(0, 8, 512)
                wk = wview(4096, 8, 512)
                wv = wview(8192, 8, 1024)
                wr = wview(16384, 8, 1024)
                wg = wview(24576, 8, 16)
                wo = wview(24704, 8, 1024)
                Wq, Wk, Wv, Wr, Wg, Wo = [T(None, n_) for n_ in ("gwq", "gwk", "gwv", "gwr", "gwg", "gwo")]
                wd = gla_w_in.rearrange("(k p) n -> p k n", p=128)
                S.dma("pool", wq, wd[:, :, 0:512], wrecs[0], w=[Wq.b])
                S.dma("pool", wk, wd[:, :, 512:1024], wrecs[1], w=[Wk.b])
                S.dma("pool", wg, wd[:, :, 3072:3088], wrecs[4], w=[Wg.b])
                S.dma("pool", wv, wd[:, :, 1024:2048], wrecs[2], w=[Wv.b])
                S.dma("pool", wr, wd[:, :, 2048:3072], wrecs[3], w=[Wr.b])
                S.dma("pool", wo, gla_w_out.rearrange("(k p) n -> p k n", p=128), wrecs[5], w=[Wo.b])
                xb = sb(st, "xb", [128, NKC, GT], BF16)
                glow = sb(st, "glow", [16, GT])
                e1 = sb(st, "e1", [128, GT])
                bc = sb(st, "bc", [128, GT])
                enb = sb(st, "enb", [128, GT])
                eb = [sb(st, f"eb{h}", [128, GT]) for h in range(4)]
                qt = [sb(st, f"qt{h}", [128, GT], BF16) for h in range(4)]
                kt = [sb(st, f"kt{h}", [128, GT], BF16) for h in range(4)]
                kh = sb(st, "kh", [128, GT], BF16)
                khtok = [sb(st, f"khtok{i}", [64, 4, 128], BF16) for i in range(2)]
                vtok = [sb(st, f"vtok{i}", [64, 1024], BF16) for i in range(2)]
                AT = [sb(st, f"AT{i}", [64, 4, 64], BF16) for i in range(2)]
                OT = sb(st, "OT", [128, 8, GT])
                sq = sb(st, "sq", [128, 2, GT], BF16)
                rstd = sb(st, "rstd", [128, GT])
                sr = sb(st, "sr", [128, GT])
                tq = sb(st, "tq", [128, GT])
                og = sb(st, "og", [128, 8, GT], BF16)
                Sst = sb(st, "Sst", [128, 4, 256])
                Sbf = sb(st, "Sbf", [128, 4, 256], BF16)
                S0s = [sb(st, f"S0_{i}", [128, 4, 256]) for i in range(2)]
                S0bfs = [sb(st, f"S0bf_{i}", [128, 4, 256], BF16) for i in range(2)]
                srecs = [S.rec("st0"), S.rec("st1")]
                S.op("dve", lambda e: e.memset(Sst[:], 0.0), w=[Sst.b])
                S.op("pool", lambda e: e.memset(Sbf[:], 0.0), w=[Sbf.b])
                cctr = [0]

                def chunk(cl, L, St, Sb):
                    i2 = cctr[0] % 2
                    cctr[0] += 1
                    vt, kk, at = vtok[i2], khtok[i2], AT[i2]
                    for half in range(2):
                        ps = PS[half]
                        for kc in range(NKC):
                            mm(ps[0:L, :], xb[:, kc, cl:cl + L], wv[:, kc, half * 512:(half + 1) * 512], [xb, Wv], [ps],
                               start=(kc == 0), stop=(kc == NKC - 1), sig=(kc == NKC - 1))
                        cp("act", vt[0:L, half * 512:(half + 1) * 512], ps[0:L, :], [ps], [vt])
                    for h in range(4):
                        S.op("pe", lambda e, h=h: e.transpose(PSB[0:L, h * 128:(h + 1) * 128], khT4[h][:, cl:cl + L], IDENT),
                             r=[khT4[h].b, cb.b], w=[PSB.b], sig=(h == 3))
                    cp("dve", kk[0:L, :, :], PSB[0:L, 0:512].rearrange("p (h d) -> p h d", h=4), [PSB], [kk])
                    psS = PS[2]
                    for h in range(4):
                        mm(psS[0:L, h * 64:h * 64 + L], kt[h][:, cl:cl + L], qt[h][:, cl:cl + L], [kt[h], qt[h]], [psS], sig=(h == 3))
                    tt("dve", at[0:L, :, 0:L], psS[0:L, 0:256].rearrange("p (h i) -> p h i", h=4)[:, :, 0:L],
                       TRIB[0:L, 0:L].unsqueeze(1).to_broadcast([L, 4, L]), ALU.mult, [psS, cb], [at])
                    psO = PS[3]
                    for h in range(4):
                        for ec in range(2):
                            j = h * 2 + ec
                            mm(psO[:, j * 64:j * 64 + L], Sb[:, h, ec * 128:(ec + 1) * 128], qt[h][:, cl:cl + L], [Sb, qt[h]], [psO],
                               start=True, stop=False, sig=False)
                            mm(psO[:, j * 64:j * 64 + L], vt[0:L, h * 256 + ec * 128:h * 256 + (ec + 1) * 128], at[0:L, h, 0:L],
                               [vt, at], [psO], start=False, stop=True, sig=(j == 7))
                    cp("act", OT[:, :, cl:cl + L], psO[:, :].rearrange("p (j i) -> p j i", j=8)[:, :, 0:L], [psO], [OT])
                    for h in range(4):
                        ps = PS[4 + h // 2]
                        mm(ps[:, (h % 2) * 256:(h % 2 + 1) * 256], kk[0:L, h, :], vt[0:L, h * 256:(h + 1) * 256], [kk, vt], [ps],
                           sig=(h % 2 == 1))
                    for h in range(4):
                        ps = PS[4 + h // 2]
                        stt(St[:, h, :], St[:, h, :], eb[h][:, cl + L - 1:cl + L], ps[:, (h % 2) * 256:(h % 2 + 1) * 256],
                            ALU.mult, ALU.add, [St, eb[h], ps], [St])
                    cp("pool", Sb[:], St[:], [St], [Sb])

                khT4 = [sb(st, f"khT{h}", [128, GT], BF16) for h in range(4)]

                for ti, (c0, n) in enumerate(GTILES):
                    thin = (ti == 0)
                    for kc in range(NKC):
                        cp(["dve", "pool"][kc % 2], xb[:, kc, 0:n], X[:, kc, c0:c0 + n], [X], [xb])
                    for kc in range(NKC):
                        mm(PS[6][0:16, 0:n], wg[:, kc, :], xb[:, kc, 0:n], [Wg, xb], [PS[6]], start=(kc == 0), stop=(kc == NKC - 1),
                           sig=(kc == NKC - 1))
                    cp("act", glow[:, 0:n], PS[6][0:16, 0:n], [PS[6]], [glow])
                    mk = smask[:, GT:GT + n] if thin else smask[:, 0:n]
                    for h in range(4):
                        mm(PS[6][:, 0:n], wgate[:, h * 128:(h + 1) * 128], glow[:, 0:n], [wgate, glow], [PS[6]])
                        act(e1[:, 0:n], PS[6][:, 0:n], AF.Exp, [PS[6], negb], [e1], scale=-1.0, bias=negb[:, h:h + 1])
                        act(e1[:, 0:n], e1[:, 0:n], AF.Ln, [e1], [e1], bias=1.0)
                        S.op("dve", lambda e: e.tensor_tensor_scan(out=bc[:, 0:n], data0=mk, data1=e1[:, 0:n], initial=0.0,
                                                                    op0=ALU.mult, op1=ALU.add), r=[smask.b, e1.b], w=[bc.b])
                        act(eb[h][:, 0:n], bc[:, 0:n], AF.Exp, [bc], [eb[h]], scale=-1.0 / 16.0)
                        act(enb[:, 0:n], bc[:, 0:n], AF.Exp, [bc], [enb], scale=1.0 / 16.0)
                        for kc in range(NKC):
                            mm(PS[0][:, 0:n], wq[:, kc, h * 128:(h + 1) * 128], xb[:, kc, 0:n], [Wq, xb], [PS[0]], start=(kc == 0),
                               stop=(kc == NKC - 1), sig=(kc == NKC - 1))
                        stt(qt[h][:, 0:n], PS[0][:, 0:n], 128.0 ** -0.5, eb[h][:, 0:n], ALU.mult, ALU.mult, [PS[0], eb[h]], [qt[h]])
                        for kc in range(NKC):
                            mm(PS[1][:, 0:n], wk[:, kc, h * 128:(h + 1) * 128], xb[:, kc, 0:n], [Wk, xb], [PS[1]], start=(kc == 0),
                               stop=(kc == NKC - 1), sig=(kc == NKC - 1))
                        tt("dve", kt[h][:, 0:n], PS[1][:, 0:n], enb[:, 0:n], ALU.mult, [PS[1], enb], [kt[h]])
                        if thin:
                            tt("pool", khT4[h][:, 0:16], kt[h][:, 0:16], eb[h][:, 15:16].to_broadcast([128, 16]), ALU.mult,
                               [kt[h], eb[h]], [khT4[h]])
                            tt("pool", khT4[h][:, 16:n], kt[h][:, 16:n], eb[h][:, 16:n], ALU.mult, [kt[h], eb[h]], [khT4[h]])
                        else:
                            nch = n // 64
                            tt("pool", khT4[h][:, 0:n].rearrange("p (c l) -> p c l", c=nch),
                               kt[h][:, 0:n].rearrange("p (c l) -> p c l", c=nch),
                               eb[h][:, 0:n].rearrange("p (c l) -> p c l", c=nch)[:, :, 63:64].to_broadcast([128, nch, 64]),
                               ALU.mult, [kt[h], eb[h]], [khT4[h]])
                    if thin:
                        S.op("dve", lambda e: e.memset(OT[:, :, 0:n], 0.0), w=[OT.b])
                        chunk(0, 16, Sst, Sbf)
                        for i in range(32):
                            S0, S0bf = S0s[i % 2], S0bfs[i % 2]
                            S.dma("sp", S0[:], state_own[i].rearrange("h p e -> p h e"), srecs[i % 2], w=[S0.b])
                            cp("pool", S0bf[:], S0[:], [S0], [S0bf])
                            chunk(16 + i, 1, S0, S0bf)
                            out_dma(gss[i].rearrange("h p e -> p h e"), S0[:], [S0])
                    else:
                        for ci in range(n // 64):
                            chunk(ci * 64, 64, Sst, Sbf)
                    for h in range(4):
                        act(sq[:, :, 0:n], OT[:, 2 * h:2 * h + 2, 0:n], AF.Square, [OT], [sq])
                        for ec in range(2):
                            mm(PS[6][:, 0:n], O256, sq[:, ec, 0:n], [cb, sq], [PS[6]], start=(ec == 0), stop=(ec == 1), sig=(ec == 1))
                        act(rstd[:, 0:n], PS[6][:, 0:n], AF.Ln, [PS[6]], [rstd], bias=epsb[:, 0:1])
                        act(rstd[:, 0:n], rstd[:, 0:n], AF.Exp, [rstd], [rstd], scale=-0.5)
                        for ec in range(2):
                            j = 2 * h + ec
                            ps = PS[ec]
                            for kc in range(NKC):
                                mm(ps[:, 0:n], wr[:, kc, j * 128:(j + 1) * 128], xb[:, kc, 0:n], [Wr, xb], [ps], start=(kc == 0),
                                   stop=(kc == NKC - 1), sig=(kc == NKC - 1))
                            act(sr[:, 0:n], ps[:, 0:n], AF.Silu, [ps], [sr])
                            stt(tq[:, 0:n], OT[:, j, 0:n], pp[:, 68 + ec:69 + ec], rstd[:, 0:n], ALU.mult, ALU.mult, [OT, pp, rstd], [tq])
                            tt("pool", og[:, j, 0:n], tq[:, 0:n], sr[:, 0:n], ALU.mult, [tq, sr], [og])
                    for dc in range(NKC):
                        ps = PS[dc % 2]
                        for j in range(8):
                            mm(ps[:, 0:n], wo[:, j, dc * 128:(dc + 1) * 128], og[:, j, 0:n], [Wo, og], [ps], start=(j == 0), stop=(j == 7),
                               sig=(j == 7))
                        stt(X[:, dc, c0:c0 + n], X[:, dc, c0:c0 + n], ALPHA, ps[:, 0:n], ALU.mult, ALU.add, [X, ps], [X])
                    layer_norm(c0, n, 0, 8, xb, og, [e1, bc, sr, tq], PS[2], PS[3])
                out_dma(gsp.rearrange("h p e -> p h e"), Sst[:], [Sst])
                S.barrier()


        def diff():
            with ExitStack() as st2:
                VT = sb(st2, "VT", [128, 17, 1024], BF16)
                KTv = WB[:, 16384:16384 + 8 * NT].rearrange("p (h n) -> p h n", h=8)
                KT = T(None, "KT")
                R0, R1 = T(None, "wr0"), T(None, "wr1")
                w0v = wview(0, 8, 1024)
                w1v = wview(8192, 8, 1024)
                neglam = sb(st2, "neglam", [128, 1])
                gsub = sb(st2, "gsub", [128, 1])
                wd = diff_w_in.rearrange("(k p) n -> p k n", p=128)
                crec = S.rec("cs")
                crec2 = S.rec("sn")
                with ExitStack() as st:
                    lp = sb(st, "lp", [1, 256])
                    pr = sb(st, "pr", [1, 128])
                    s2 = sb(st, "s2", [1, 2])
                    l1 = sb(st, "l1", [1, 1])
                    S.dma("sp", lp[:], lam_d, crec, w=[lp.b])
                    tt("dve", pr[:].rearrange("p (a d) -> p a d", a=2), lp[:].rearrange("p (a b d) -> p a b d", a=2, b=2)[:, :, 0, :],
                       lp[:].rearrange("p (a b d) -> p a b d", a=2, b=2)[:, :, 1, :], ALU.mult, [lp], [pr])
                    S.op("dve", lambda e: e.tensor_reduce(out=s2[:], in_=pr[:].rearrange("p (a d) -> p a d", a=2),
                                                          axis=mybir.AxisListType.X, op=ALU.add), r=[pr.b], w=[s2.b])
                    act(s2[:], s2[:], AF.Exp, [s2], [s2])
                    tt("dve", l1[:], s2[:, 1:2], s2[:, 0:1], ALU.subtract, [s2], [l1])
                    ts("dve", l1[:], l1[:], -LAM_INIT, None, ALU.add, ALU.bypass, [l1], [l1])
                    mm(PS[6][:, 0:1], cf[0:1, 1, :], l1[0:1, 0:1], [cf, l1], [PS[6]])
                    cp("act", neglam[:], PS[6][:, 0:1], [PS[6]], [neglam])
                    ts("dve", gsub[:], pp[:, 70:71], 1.0 - LAM_INIT, None, ALU.mult, ALU.bypass, [pp], [gsub])
                    S.barrier()

                with ExitStack() as st:
                    S.dma("pool", w0v, wd[:, :, 1024:2048], wrecs[0], w=[R0.b])
                    S.dma("pool", w1v, wd[:, :, 2048:3072], wrecs[1], w=[R1.b])
                    xb = sb(st, "xbA", [128, NKC, 512], BF16)
                    cs = sb(st, "cosA", [128, 512])
                    sn = sb(st, "sinA", [128, 512])
                    kf = sb(st, "kf", [128, 512])
                    t1 = sb(st, "t1A", [128, 512])
                    t2 = sb(st, "t2A", [128, 512])
                    kst = [sb(st, f"kst{i}", [128, 512]) for i in range(2)]
                    vst = [sb(st, f"vst{i}", [128, 1024]) for i in range(2)]
                    vi = 0
                    order = list(range(1, len(TILES))) + [0]
                    for ti in order:
                        c0, n = TILES[ti]
                        for kc in range(NKC):
                            cp(["dve", "pool"][kc % 2], xb[:, kc, 0:n], X[:, kc, c0:c0 + n], [X], [xb])
                        S.dma("sp", cs[:, 0:n], cos_d[:, c0:c0 + n], crec, w=[cs.b])
                        S.dma("sp", sn[:, 0:n], sin_d[:, c0:c0 + n], crec2, w=[sn.b])
                        for h in range(8):
                            ps = PS[h % 2]
                            for kc in range(NKC):
                                mm(ps[:, 0:n], w0v[:, kc, h * 128:(h + 1) * 128], xb[:, kc, 0:n], [R0, xb], [ps], start=(kc == 0),
                                   stop=(kc == NKC - 1), sig=(kc == NKC - 1))
                            cp("act", kf[:, 0:n], ps[:, 0:n], [ps], [kf])
                            ps2 = PS[2 + h % 2]
                            mm(ps2[:, 0:n], PERM, kf[:, 0:n], [cf, kf], [ps2])
                            tt("pool", t1[:, 0:n], kf[:, 0:n], cs[:, 0:n], ALU.mult, [kf, cs], [t1])
                            tt("dve", t2[:, 0:n], ps2[:, 0:n], sn[:, 0:n], ALU.mult, [ps2, sn], [t2])
                            ks_ = kst[h % 2]
                            tt("dve", ks_[:, 0:n], t1[:, 0:n], t2[:, 0:n], ALU.add, [t1, t2], [ks_])
                            cp("act", KTv[:, h, c0:c0 + n], ks_[:, 0:n], [ks_], [KT])
                            out_dma(kTo[:, h, c0:c0 + n], ks_[:, 0:n], [ks_])
                        nblk = max(1, n // 128)
                        for blk in range(nblk):
                            rows = min(128, n)
                            kt_i = 0 if ti == 0 else 1 + (ti - 1) * 4 + blk
                            v_ = vst[vi % 2]
                            vi += 1
                            for half in range(2):
                                ps = PS[4 + half]
                                for kc in range(NKC):
                                    mm(ps[0:rows, :], xb[:, kc, blk * 128:blk * 128 + rows], w1v[:, kc, half * 512:(half + 1) * 512],
                                       [xb, R1], [ps], start=(kc == 0), stop=(kc == NKC - 1), sig=(kc == NKC - 1))
                                cp("act" if half else "dve", v_[0:rows, half * 512:(half + 1) * 512], ps[0:rows, :], [ps], [v_])
                            cp("pool", VT[0:rows, kt_i, :], v_[0:rows, :], [v_], [VT])
                            out_dma(vro[c0 + blk * 128:c0 + blk * 128 + rows, :], v_[0:rows, :], [v_])
                    S.barrier()


                if int(os.environ.get('K_STOP', '9')) == 2:
                    return
                if with_sample:
                    with ExitStack() as st:
                        NSQ = 16
                        wc = sb(st, "wc", [128, NKC, 3, 256], BF16)
                        S.dma("pool", wc[:], wqkv_c.rearrange("(k p) j d -> p k j d", p=128), wrecs[2], w=[wc.b])
                        xbs = sb(st, "xbs", [128, NKC, NSQ], BF16)
                        cS = sb(st, "cS", [128, NSQ])
                        sS = sb(st, "sS", [128, NSQ])
                        S.dma("sp", cS[:], cos_d[:, 16:16 + NSQ], crec, w=[cS.b])
                        S.dma("sp", sS[:], sin_d[:, 16:16 + NSQ], crec2, w=[sS.b])
                        cp("dve", xbs[:], X[:, :, 16:16 + NSQ], [X], [xbs])
                        f3 = [[sb(st, f"f3_{e}_{j}", [128, NSQ]) for j in range(3)] for e in range(2)]
                        ta = sb(st, "ta", [128, NSQ])
                        tb = sb(st, "tb", [128, NSQ])
                        for e in range(2):
                            for j in range(3):
                                ps = PS[4 + j % 2]
                                for kc in range(NKC):
                                    mm(ps[:, 0:NSQ], wc[:, kc, j, e * 128:(e + 1) * 128], xbs[:, kc, :], [wc, xbs], [ps], start=(kc == 0),
                                       stop=(kc == NKC - 1), sig=(kc == NKC - 1))
                                cp("act", f3[e][j][:], ps[:, 0:NSQ], [ps], [f3[e][j]])
                                if j < 2:
                                    sc_ = 0.125 if j == 0 else 1.0
                                    mm(PS[6][:, 0:NSQ], PERM, f3[e][j][:], [cf, f3[e][j]], [PS[6]])
                                    stt(ta[:], f3[e][j][:], sc_, cS[:], ALU.mult, ALU.mult, [f3[e][j], cS], [ta])
                                    stt(tb[:], PS[6][:, 0:NSQ], sc_, sS[:], ALU.mult, ALU.mult, [PS[6], sS], [tb])
                                    tt("dve", f3[e][j][:], ta[:], tb[:], ALU.add, [ta, tb], [f3[e][j]])
                        Qb = sb(st, "Qb", [128, NSQ, 2, 2], BF16)
                        S.op("dve", lambda e_: e_.memset(Qb[:], 0.0), w=[Qb.b])
                        for e in range(2):
                            cp("dve", Qb[0:64, :, e, 0], f3[e][0][0:64, :], [f3[e][0]], [Qb])
                            cp("dve", Qb[64:128, :, e, 1], f3[e][0][64:128, :], [f3[e][0]], [Qb])
                        NPT = NSQ * NPAGES
                        idx = sb(st, "idx", [128, NPT], I32)
                        io = sb(st, "io", [128, 1])
                        ptrec = S.rec("pt")
                        S.dma("sp", idx[:], pt_d.to_broadcast([128, NPT]), ptrec, w=[idx.b])
                        S.op("pool", lambda e_: e_.iota(io[:], pattern=[[0, 1]], base=0, channel_multiplier=1,
                                                        allow_small_or_imprecise_dtypes=True), w=[io.b])
                        ts("pool", idx[:], idx[:], 128.0, io[:, 0:1], ALU.mult, ALU.add, [idx, io], [idx])
                        NSL = 12
                        pg = [sb(st, f"pg{i}", [128, 512], BF16) for i in range(NSL)]
                        prec = [S.rec(f"pg{i}") for i in range(NSL)]
                        OTs = sb(st, "OTs", [128, NSQ, 2, 2])
                        Ls = sb(st, "Ls", [128, NSQ, 4])
                        Pb = [sb(st, f"Pb{i}", [128, 32], BF16) for i in range(2)]
                        PAcc = [PS[0], PS[2]]
                        gi_ = 0
                        pcnt = 0
                        for s_ in range(NSQ):
                            for g in range(8):
                                slots = []
                                for j in range(8):
                                    pj = s_ * 64 + g * 8 + j
                                    sl = pcnt % NSL
                                    pcnt += 1
                                    slots.append(sl)
                                    S._deps("pool", [idx.b], [pg[sl].b])
                                    ins = nc.gpsimd.indirect_dma_start(
                                        out=pg[sl][:], out_offset=None, in_=cache_d,
                                        in_offset=bass.IndirectOffsetOnAxis(ap=idx[:, pj:pj + 1], axis=0))
                                    rec = prec[sl]
                                    rec.val += 16
                                    ins.then_inc(rec.sem, 16)
                                    pg[sl].b.w = ("d", rec, rec.val)
                                    pg[sl].b.r = {}
                                    idx.b.r["d%d" % id(rec)] = ("d", rec, rec.val)
                                pS = PS[4 + gi_ % 2]
                                for j in range(8):
                                    for e in range(2):
                                        c_ = (j * 2 + e) * 2
                                        mm(pS[:, c_:c_ + 2], pg[slots[j]][:, e * 256:e * 256 + 128], Qb[:, s_, e, :], [pg[slots[j]], Qb], [pS],
                                           sig=(j == 7 and e == 1))
                                pb = Pb[gi_ % 2]
                                act(pb[:], pS[:, 0:32], AF.Exp, [pS], [pb])
                                for j in range(8):
                                    for e in range(2):
                                        c_ = (j * 2 + e) * 2
                                        mm(PAcc[e][:, 0:2], pg[slots[j]][:, e * 256 + 128:e * 256 + 256], pb[:, c_:c_ + 2],
                                           [pg[slots[j]], pb], [PAcc[e]], start=(g == 0 and j == 0), stop=(g == 7 and j == 7),
                                           sig=(j == 7 and e == 1))
                                mm(PS[1][:, 0:32], ONESB, pb[:], [cb, pb], [PS[1]], start=(g == 0), stop=(g == 7))
                                gi_ += 1
                            for e in range(2):
                                cp("act", OTs[:, s_, e, :], PAcc[e][:, 0:2], [PAcc[e]], [OTs])
                            S.op("dve", lambda e_, s_=s_: e_.tensor_reduce(out=Ls[:, s_, :],
                                                                          in_=PS[1][:, 0:32].rearrange("p (j q) -> p q j", q=4),
                                                                          axis=mybir.AxisListType.X, op=ALU.add), r=[PS[1].b], w=[Ls.b])
                        pself = sb(st, "pself", [128, 2, NSQ])
                        num = sb(st, "num", [128, 2, NSQ])
                        den = sb(st, "den", [128, 2, NSQ])
                        sqs = sb(st, "sqs", [128, NSQ], BF16)
                        ogs = sb(st, "ogs", [128, 2, NSQ])
                        for e in range(2):
                            qs, ks, vs = f3[e]
                            tt("dve", ta[:], qs[:], ks[:], ALU.mult, [qs, ks], [ta])
                            for m in range(2):
                                mm(PS[6][:, m * NSQ:(m + 1) * NSQ], cf[:, 7 + m, :], ta[:], [cf, ta], [PS[6]], sig=(m == 1))
                            act(pself[:], PS[6][:, 0:2 * NSQ].rearrange("p (m s) -> p m s", m=2), AF.Exp, [PS[6]], [pself])
                            for m in range(2):
                                tt("dve", num[:, m, :], pself[:, m, :], vs[:], ALU.mult, [pself, vs], [num])
                                tt("dve", num[:, m, :], num[:, m, :], OTs[:, :, e, m], ALU.add, [num, OTs], [num])
                                tt("dve", den[:, m, :], pself[:, m, :], Ls[:, :, e * 2 + m], ALU.add, [pself, Ls], [den])
                            act(den[:], den[:], AF.Ln, [den], [den])
                            act(den[:], den[:], AF.Exp, [den], [den], scale=-1.0)
                            tt("dve", num[:], num[:], den[:], ALU.mult, [num, den], [num])
                            stt(ta[:], num[:, 1, :], neglam[:, 0:1], num[:, 0, :], ALU.mult, ALU.add, [num, neglam], [ta])
                            act(sqs[:], ta[:], AF.Square, [ta], [sqs])
                            mm(PS[6][:, 0:NSQ], O128, sqs[:], [cb, sqs], [PS[6]])
                            act(tb[:], PS[6][:, 0:NSQ], AF.Ln, [PS[6]], [tb], bias=epsb[:, 0:1])
                            act(tb[:], tb[:], AF.Exp, [tb], [tb], scale=-0.5)
                            stt(ogs[:, e, :], ta[:], gsub[:, 0:1], tb[:], ALU.mult, ALU.mult, [ta, gsub, tb], [ogs])
                        out_dma(ogs_o, ogs[:], [ogs])
                        S.barrier()

                with ExitStack() as st:
                    S.dma("pool", w0v, wd[:, :, 0:1024], wrecs[0], w=[R0.b])
                    S.dma("pool", w1v, diff_w_out.rearrange("(k p) n -> p k n", p=128), wrecs[1], w=[R1.b])
                    xb = sb(st, "xbB", [128, NKC, 512], BF16)
                    cs = sb(st, "cosB", [128, 512])
                    sn = sb(st, "sinB", [128, 512])
                    qf = sb(st, "qf", [128, 512])
                    t1 = sb(st, "t1B", [128, 512])
                    t2 = sb(st, "t2B", [128, 512])
                    QT = sb(st, "QT", [128, 8, 512], BF16)
                    og = xb
                    pt = [sb(st, f"pt{i}", [128, 512], BF16) for i in range(3)]
                    rl = [cs, sn]
                    sqb = sb(st, "sqB", [128, 512], BF16)
                    pti = 0
                    for ti, (c0, n) in enumerate(TILES):
                        nq = 16 if ti == 0 else n
                        for kc in range(NKC):
                            cp(["dve", "pool"][kc % 2], xb[:, kc, 0:n], X[:, kc, c0:c0 + n], [X], [xb])
                        S.dma("sp", cs[:, 0:n], cos_d[:, c0:c0 + n], crec, w=[cs.b])
                        S.dma("sp", sn[:, 0:n], sin_d[:, c0:c0 + n], crec2, w=[sn.b])
                        for h in range(8):
                            ps = PS[4 + h % 2]
                            for kc in range(NKC):
                                mm(ps[:, 0:nq], w0v[:, kc, h * 128:(h + 1) * 128], xb[:, kc, 0:nq], [R0, xb], [ps], start=(kc == 0),
                                   stop=(kc == NKC - 1), sig=(kc == NKC - 1))
                            cp("act", qf[:, 0:nq], ps[:, 0:nq], [ps], [qf])
                            mm(PS[6][:, 0:nq], PERM, qf[:, 0:nq], [cf, qf], [PS[6]])
                            stt(t1[:, 0:nq], qf[:, 0:nq], 0.125, cs[:, 0:nq], ALU.mult, ALU.mult, [qf, cs], [t1])
                            stt(t2[:, 0:nq], PS[6][:, 0:nq], 0.125, sn[:, 0:nq], ALU.mult, ALU.mult, [PS[6], sn], [t2])
                            tt("pool", QT[:, h, 0:nq], t1[:, 0:nq], t2[:, 0:nq], ALU.add, [t1, t2], [QT])
                        if ti == 0:
                            kl = [(0, 16, 0, 0, True)]
                        else:
                            i = ti - 1
                            kl = [(0, 16, 0, 0, False)]
                            for r in (3, 2, 1, 0):
                                j = 4 * i + r
                                kl.append((C0P + 128 * j, 128, 1 + j, 128 * r, True))
                            for j in range(4 * i):
                                kl.append((C0P + 128 * j, 128, 1 + j, 0, False))
                        for h in range(8):
                            units = [(m, ki) for m in range(2) for ki in range(len(kl))]
                            pend = None
                            for u in units + [None]:
                                cur = None
                                if u is not None:
                                    m, ki = u
                                    kc0, nk, vti, qlo, diag = kl[ki]
                                    w_ = nq - qlo
                                    pS = PS[4 + pti % 2]
                                    p_ = pt[pti % 3]
                                    pti += 1
                                    mm(pS[0:nk, 0:w_], KTv[m * 64:(m + 1) * 64, h, kc0:kc0 + nk], QT[m * 64:(m + 1) * 64, h, qlo:nq],
                                       [KT, QT], [pS])
                                    act(p_[0:nk, 0:w_], pS[0:nk, 0:w_], AF.Exp, [pS], [p_])
                                    if diag:
                                        dw = min(128, w_)
                                        tt("dve", p_[0:nk, 0:dw], p_[0:nk, 0:dw], TRIB[0:nk, 0:dw], ALU.mult, [p_, cb], [p_])
                                    cur = (m, ki, p_)
                                if pend is not None:
                                    m, ki, p_ = pend
                                    kc0, nk, vti, qlo, diag = kl[ki]
                                    w_ = nq - qlo
                                    first, last = (ki == 0), (ki == len(kl) - 1)
                                    pO, pL = PS[m], PS[2 + m]
                                    mm(pO[:, qlo:nq], VT[0:nk, vti, h * 128:(h + 1) * 128], p_[0:nk, 0:w_], [VT, p_], [pO],
                                       start=first, stop=last)
                                    mm(pL[:, qlo:nq], ONESB[0:nk, :], p_[0:nk, 0:w_], [cb, p_], [pL], start=first, stop=last)
                                    if last:
                                        act(rl[m][:, 0:nq], pL[:, 0:nq], AF.Ln, [pL], [rl[m]])
                                        act(rl[m][:, 0:nq], rl[m][:, 0:nq], AF.Exp, [rl[m]], [rl[m]], scale=-1.0)
                                pend = cur
                            tt("dve", t1[:, 0:nq], PS[0][:, 0:nq], rl[0][:, 0:nq], ALU.mult, [PS[0], rl[0]], [t1])
                            tt("dve", t2[:, 0:nq], PS[1][:, 0:nq], rl[1][:, 0:nq], ALU.mult, [PS[1], rl[1]], [t2])
                            stt(t1[:, 0:nq], t2[:, 0:nq], neglam[:, 0:1], t1[:, 0:nq], ALU.mult, ALU.add, [t2, neglam, t1], [t1])
                            act(sqb[:, 0:nq], t1[:, 0:nq], AF.Square, [t1], [sqb])
                            mm(PS[6][:, 0:nq], O128, sqb[:, 0:nq], [cb, sqb], [PS[6]])
                            act(qf[:, 0:nq], PS[6][:, 0:nq], AF.Ln, [PS[6]], [qf], bias=epsb[:, 0:1])
                            act(qf[:, 0:nq], qf[:, 0:nq], AF.Exp, [qf], [qf], scale=-0.5)
                            stt(og[:, h, 0:nq], t1[:, 0:nq], gsub[:, 0:1], qf[:, 0:nq], ALU.mult, ALU.mult, [t1, gsub, qf], [og])
                        if ti == 0:
                            sample_og(og)
                        for dc in range(NKC):
                            ps = PS[4 + dc % 2]
                            for h in range(8):
                                mm(ps[:, 0:n], w1v[:, h, dc * 128:(dc + 1) * 128], og[:, h, 0:n], [R1, og], [ps], start=(h == 0), stop=(h == 7),
                                   sig=(h == 7))
                            stt(X[:, dc, c0:c0 + n], X[:, dc, c0:c0 + n], ALPHA, ps[:, 0:n], ALU.mult, ALU.add, [X, ps], [X])
                        layer_norm(c0, n, 32, 40, xb, QT, [qf, t1, t2, rl[0]], PS[4], PS[5])
                    S.barrier()

        def sample_og(og):
            S.op("dve", lambda e: e.memset(og[:, :, 16:48], 0.0), w=[og.b])


        def tail_prog():
            with ExitStack() as st:
                ogf = sb(st, "ogf", [128, 8, 32])
                ogb = sb(st, "ogb", [128, 8, 32], BF16)
                trec = S.rec("tl")
                S.dma("sp", ogf[:], og_all_d, trec, w=[ogf.b])
                cp("dve", ogb[:], ogf[:], [ogf], [ogb])
                w1v = wview(8192, 8, 1024)
                R1 = T(None, "r1")
                S.dma("pool", w1v, diff_w_out.rearrange("(k p) n -> p k n", p=128), wrecs[1], w=[R1.b])
                for dc in range(NKC):
                    ps = PS[4 + dc % 2]
                    for h in range(8):
                        mm(ps[:, 0:32], w1v[:, h, dc * 128:(dc + 1) * 128], ogb[:, h, :], [R1, ogb], [ps], start=(h == 0), stop=(h == 7),
                           sig=(h == 7))
                    stt(X[:, dc, 0:32], X[:, dc, 0:32], ALPHA, ps[:, 0:32], ALU.mult, ALU.add, [X, ps], [X])
                zb = sb(st, "zbt", [128, 8, 32], BF16)
                zq = sb(st, "zqt", [128, 8, 32], BF16)
                tmp = [sb(st, f"tt{i}", [128, 32]) for i in range(4)]
                layer_norm(0, 32, 32, 40, zb, zq, tmp, PS[4], PS[5])
                S.barrier()

        if tail:
            tail_prog()
            mlp(1)
            out_dma(yT, X[:], [X])
            S.barrier()
            return nc
        KSTOP = int(os.environ.get('K_STOP', '9'))
        gla()
        if KSTOP >= 1:
            mlp(0)
        out_dma(xmid_o, X[:, :, 16:48], [X])
        if KSTOP >= 2:
            diff()
        if KSTOP >= 4:
            mlp(1)
        out_dma(yT, X[:], [X])
        S.barrier()
    return nc


def _consts():
    cf = np.zeros((128, 10, 128), np.float32)
    cf[0:64, 7, :] = 1.0
    cf[64:128, 8, :] = 1.0
    cf[:, 6, :] = 1.0 / 128
    cf[:, 0, :] = np.eye(128)
    cf[:, 1, :] = 1.0
    cf[:, 2, :] = np.triu(np.ones((128, 128)))
    for p in range(128):
        d = p % 64
        if d < 8:
            cf[p + 8, 3, p] = -1.0
        elif d < 16:
            cf[p - 8, 3, p] = 1.0
    cf[:, 4, :] = 1.0 / 1024
    cf[:, 5, :] = 1.0 / 256
    sm = np.ones((128, GT + C0P), np.float32)
    sm[:, 0:GT:64] = 0.0
    sm[:, GT] = 0.0
    sm[:, GT + 16:] = 0.0
    pos = np.zeros(NT, np.float32)
    pos[0:16] = np.arange(16)
    pos[16:48] = 8192
    pos[48:] = 16 + np.arange(SEQ)
    inv = (np.float32(500000.0) ** (-(np.arange(0, 16, 2, dtype=np.float32)) / np.float32(16))).astype(np.float32)
    rc = np.ones((128, NT), np.float32)
    rs = np.zeros((128, NT), np.float32)
    for p in range(128):
        d = p % 64
        if d < 16:
            ang = (pos * inv[d % 8]).astype(np.float32)
            rc[p] = np.cos(ang)
            rs[p] = np.sin(ang)
    return cf, sm, rc, rs


def kernel(x_prompt, x_sample, state_gla, cache_k, cache_v, page_table, meta_tokens,
           gla_w_in, gla_w_gate, gla_b_gate, gla_norm, gla_w_out,
           diff_w_in, diff_lambda, diff_norm, diff_w_out,
           mlp_w1, mlp_w2, ln_mix_g, ln_mix_b, ln_mlp_g, ln_mlp_b, _ncores=NCORES, _dev=False, _npool=None):
    f = lambda a: np.ascontiguousarray(np.asarray(a, dtype=np.float32))
    x_prompt, x_sample, state_gla, meta_tokens = f(x_prompt), f(x_sample), f(state_gla), f(meta_tokens)
    ncores = _ncores
    npool = 8 if _dev else (_npool or NPOOL)
    cf, sm, rc, rs = _consts()
    pp = np.zeros((128, 80), np.float32)
    lnp = [ln_mix_g, ln_mix_b, ln_mlp_g, ln_mlp_b]
    for layer in range(2):
        for j in range(4):
            pp[:, (layer * 4 + j) * 8:(layer * 4 + j + 1) * 8] = f(lnp[j])[layer].reshape(8, 128).T
    pp[:, 64:68] = f(gla_b_gate)[0].reshape(4, 128).T
    pp[:, 68:70] = f(gla_norm)[0].reshape(2, 128).T
    pp[:, 70] = f(diff_norm)[0]
    tail_shared = {
        "gla_w_gate": f(gla_w_gate)[0], "diff_w_out": f(diff_w_out)[0], "mlp_w1": f(mlp_w1), "mlp_w2": f(mlp_w2),
        "pp": pp, "cf": cf, "smask": sm,
    }
    shared = dict(tail_shared)
    shared.update({
        "gla_w_in": f(gla_w_in)[0], "gla_w_out": f(gla_w_out)[0], "diff_w_in": f(diff_w_in)[0],
        "lam": f(diff_lambda)[0].reshape(1, 256), "rcos": rc, "rsin": rs,
        "ptab": np.ascontiguousarray(np.asarray(page_table, dtype=np.int32).reshape(1, -1)),
        "state_all": np.ascontiguousarray(state_gla[0]),
    })
    dwi = shared["diff_w_in"]
    pt_full = np.asarray(page_table, dtype=np.int32)
    shared.pop("ptab")
    shared.pop("state_all")
    in_maps = []
    perms = []
    cache_pair = {}
    for c in range(ncores):
        pair, half = c // 2, c % 2
        own = list(range(half * 16, half * 16 + 16))
        perm = own + [i for i in range(32) if i not in own]
        perms.append(perm)
        xt = np.zeros((D, NT), np.float32)
        xt[:, 0:16] = meta_tokens.T
        xt[:, 16:48] = x_sample[perm, 0, :].T
        xt[:, C0P:] = x_prompt[c].T
        m = dict(shared)
        m["xT"] = np.ascontiguousarray(xt.reshape(8, 128, NT).transpose(1, 0, 2))
        m["state_all"] = np.ascontiguousarray(state_gla[0][perm])
        m["ptab"] = np.ascontiguousarray(pt_full[own].reshape(1, -1))
        h0 = 2 * pair
        m["wqkv_c"] = np.ascontiguousarray(
            np.stack([dwi[:, j * 1024 + h0 * 128:j * 1024 + (h0 + 2) * 128] for j in range(3)], axis=1))
        if _dev:
            m["cache"] = np.zeros((npool * 128, 512), np.float32)
        else:
            if pair not in cache_pair:
                parts = []
                for e in range(2):
                    kc_ = np.asarray(cache_k)[0, :npool, :, h0 + e, :]
                    vc_ = np.asarray(cache_v)[0, :npool, :, h0 + e, :]
                    parts += [kc_.transpose(0, 2, 1), vc_]
                cache_pair = {pair: np.ascontiguousarray(
                    np.concatenate(parts, axis=2).reshape(npool * 128, 512).astype(np.float32))}
            m["cache"] = cache_pair[pair]
        in_maps.append(m)
    nc = build(ncores, npool, not _dev)
    res = run_bass_kernel_spmd(nc, in_maps, core_ids=list(range(ncores))).results
    B = ncores
    y_prompt = np.zeros((B, SEQ, D), np.float32)
    gla_sp = np.zeros((1, B, 4, 128, 256), np.float32)
    k_p = np.zeros((1, B, SEQ + 16, 8, 128), np.float32)
    v_p = np.zeros((1, B, SEQ + 16, 8, 128), np.float32)
    for c in range(ncores):
        r = res[c]
        yt = r["yT"].transpose(1, 0, 2).reshape(D, NT)
        y_prompt[c] = yt[:, C0P:].T
        gla_sp[0, c] = r["gla_sp"]
        kt_ = r["kT_rows"]
        kcols = np.concatenate([kt_[:, :, 0:16], kt_[:, :, C0P:]], axis=2)
        k_p[0, c] = kcols.transpose(2, 1, 0)
        vr = r["v_rows"]
        v_p[0, c] = np.concatenate([vr[0:16], vr[C0P:]], axis=0).reshape(SEQ + 16, 8, 128)
    r0 = res[0]
    gla_ss = np.ascontiguousarray(r0["gla_ss"]).reshape(1, 32, 4, 128, 256)
    k_s = np.ascontiguousarray(r0["kT_rows"][:, :, 16:48].transpose(2, 1, 0)).reshape(1, 32, 1, 8, 128)
    v_s = np.ascontiguousarray(r0["v_rows"][16:48]).reshape(1, 32, 1, 8, 128)
    og_all = np.zeros((128, 8, 32), np.float32)
    for c in range(ncores):
        pair, half = c // 2, c % 2
        og_all[:, 2 * pair:2 * pair + 2, half * 16:half * 16 + 16] = res[c]["ogs"]
    m2 = dict(tail_shared)
    m2["xT"] = np.ascontiguousarray(r0["xs_mid"])
    m2["og_all"] = og_all
    nc2 = build(1, npool, False, tail=True)
    res2 = run_bass_kernel_spmd(nc2, [m2], core_ids=[0]).results[0]
    y_sample = np.ascontiguousarray(res2["yT"].transpose(1, 0, 2).reshape(D, 32).T).reshape(32, 1, D)
    return (y_prompt, y_sample, gla_sp, gla_ss, k_p, v_p, k_s, v_s)
```

```python
import math
import os
from contextlib import ExitStack
import numpy as np
import concourse.bass as bass
import concourse.mybir as mybir
from concourse.bass_utils import run_bass_kernel_spmd

F32 = mybir.dt.float32
BF16 = mybir.dt.bfloat16
I32 = mybir.dt.int32
AF = mybir.ActivationFunctionType
ALU = mybir.AluOpType

D = 1024
NKC = 8
NMETA = 16
NSLOT = 32
C0P = NMETA + NSLOT
SEQ = 2048
NT = C0P + SEQ
NCORES = 8
DEPTH = 2
ALPHA = (2 * DEPTH) ** 0.25
EPS = 1e-5
PAGE = 128
NPAGES = 64
NPOOL = 2560
EPOCH = 4000
TILES = [(0, C0P)] + [(C0P + 512 * i, 512) for i in range(SEQ // 512)]
GT = 256
GTILES = [(0, C0P)] + [(C0P + GT * i, GT) for i in range(SEQ // GT)]
LAM_INIT = 0.8 - 0.6 * math.exp(-0.3 * 1)
DEBUG = False


class Buf:
    __slots__ = ("name", "w", "r")

    def __init__(self, name):
        self.name = name
        self.w = None
        self.r = {}


class SemRec:
    def __init__(self, sem):
        self.sem = sem
        self.val = 0


class Sched:
    def __init__(self, nc, stack):
        self.nc = nc
        self.stack = stack
        self.eng = {"pe": nc.tensor, "act": nc.scalar, "dve": nc.vector, "pool": nc.gpsimd, "sp": nc.sync}
        self.cnt = {k: 0 for k in self.eng}
        self.sems = {k: [] for k in self.eng}
        self.waited = {k: {} for k in self.eng}
        self.recs = []
        self.nsem = 0

    def new_sem(self, name):
        self.nsem += 1
        return self.stack.enter_context(self.nc.semaphore(name))

    def rec(self, name):
        r = SemRec(self.new_sem("d_" + name))
        self.recs.append(r)
        return r

    def esem(self, e, i):
        ep = (i - 1) // EPOCH
        while len(self.sems[e]) <= ep:
            self.sems[e].append(self.new_sem(f"e_{e}_{len(self.sems[e])}"))
        return self.sems[e][ep], (i - 1) % EPOCH + 1

    def _wait(self, e, dep):
        if dep[0] == "e":
            _, e2, i = dep
            if e2 == e and e == "pe":
                return
            if self.waited[e].get(e2, 0) >= i:
                return
            sem, val = self.esem(e2, i)
            self.eng[e].wait_ge(sem, val)
            self.waited[e][e2] = i
        else:
            _, rec, val = dep
            if self.waited[e].get(id(rec), 0) >= val:
                return
            self.eng[e].wait_ge(rec.sem, val)
            self.waited[e][id(rec)] = val

    def _deps(self, e, r, w):
        for b in r:
            if b.w is not None:
                self._wait(e, b.w)
        for b in w:
            if b.w is not None:
                self._wait(e, b.w)
            for k, d in b.r.items():
                if k != e:
                    self._wait(e, d)

    def op(self, e, fn, r=(), w=(), sig=True):
        self._deps(e, r, w)
        ins = fn(self.eng[e])
        nid = self.cnt[e] + 1
        if sig:
            self.cnt[e] = nid
            sem, _ = self.esem(e, nid)
            ins.then_inc(sem, 1)
        me = ("e", e, nid)
        for b in r:
            b.r[e] = me
        for b in w:
            b.w = me
            b.r = {}
        return ins

    def dma(self, q, out, in_, rec, r=(), w=(), **kw):
        self._deps(q, r, w)
        ins = self.eng[q].dma_start(out=out, in_=in_, **kw)
        rec.val += 16
        ins.then_inc(rec.sem, 16)
        me = ("d", rec, rec.val)
        for b in r:
            b.r["d%d" % id(rec)] = me
        for b in w:
            b.w = me
            b.r = {}
        return ins

    def barrier(self):
        for e in self.eng:
            for e2 in self.eng:
                if e2 != e and self.cnt[e2] > 0:
                    self._wait(e, ("e", e2, self.cnt[e2]))
            for rec in self.recs:
                if rec.val > 0:
                    self._wait(e, ("d", rec, rec.val))


class T:
    def __init__(self, t, name):
        self.t = t
        self.b = Buf(name)

    def __getitem__(self, k):
        return self.t[k]


def build(ncores, npool, with_sample, tail=False):
    nc = bass.Bass("TRN2", target_bir_lowering=False)
    dram = {}
    ntot = 32 if tail else NT
    tiles = [(0, 32)] if tail else TILES

    def din(name, shape, dt=F32):
        dram[name] = nc.dram_tensor(name, list(shape), dt, kind="ExternalInput").ap()
        return dram[name]

    def dout(name, shape, dt=F32):
        dram[name] = nc.dram_tensor(name, list(shape), dt, kind="ExternalOutput").ap()
        return dram[name]

    xT = din("xT", [128, NKC, ntot])
    gla_w_gate = din("gla_w_gate", [16, 512])
    diff_w_out = din("diff_w_out", [D, D])
    mlp_w1 = din("mlp_w1", [2, D, 4 * D])
    mlp_w2 = din("mlp_w2", [2, 4 * D, D])
    pp_d = din("pp", [128, 80])
    cf_d = din("cf", [128, 10, 128])
    smask_d = din("smask", [128, GT + C0P])
    yT = dout("yT", [128, NKC, ntot])
    og_all_d = din("og_all", [128, 8, 32]) if tail else None
    if not tail:
        state_own = din("state_all", [32, 4, 128, 256])
        gla_w_in = din("gla_w_in", [D, 3088])
        gla_w_out = din("gla_w_out", [D, D])
        diff_w_in = din("diff_w_in", [D, 3072])
        lam_d = din("lam", [1, 256])
        cos_d = din("rcos", [128, NT])
        sin_d = din("rsin", [128, NT])
        wqkv_c = din("wqkv_c", [D, 3, 256])
        cache_d = din("cache", [npool * 128, 512])
        pt_d = din("ptab", [1, 16 * NPAGES], I32)
        gsp = dout("gla_sp", [4, 128, 256])
        gss = dout("gla_ss", [32, 4, 128, 256])
        xmid_o = dout("xs_mid", [128, NKC, 32])
        ogs_o = dout("ogs", [128, 2, 16])
        kTo = dout("kT_rows", [128, 8, NT])
        vro = dout("v_rows", [NT, D])
    dbg = [dout(f"dbg{i}", [128, NKC, NT]) for i in range(2)] if (DEBUG and not tail) else []

    with ExitStack() as top:
        S = Sched(nc, top)
        uid = [0]

        def sb(st, name, shape, dt=F32):
            uid[0] += 1
            return T(st.enter_context(nc.sbuf_tensor("s%d_" % uid[0] + name, list(shape), dt)), name)

        def pst(st, name, shape=(128, 512), dt=F32):
            uid[0] += 1
            return T(st.enter_context(nc.psum_tensor("p%d_" % uid[0] + name, list(shape), dt)), name)

        def bufs(ts):
            return [t.b for t in ts]

        def mm(out, lhsT, rhs, r, w, start=True, stop=True, sig=True):
            return S.op("pe", lambda e: e.matmul(out, lhsT=lhsT, rhs=rhs, start=start, stop=stop),
                        r=bufs(r), w=bufs(w), sig=sig)

        def act(out, in_, func, r, w, bias=None, scale=None):
            kw = {}
            if bias is not None:
                kw["bias"] = bias
            if scale is not None:
                kw["scale"] = scale
            return S.op("act", lambda e: e.activation(out=out, in_=in_, func=func, **kw), r=bufs(r), w=bufs(w))

        def tt(eng, out, a, b, op, r, w):
            return S.op(eng, lambda e: e.tensor_tensor(out=out, in0=a, in1=b, op=op), r=bufs(r), w=bufs(w))

        def stt(out, a, scalar, b, op0, op1, r, w):
            return S.op("dve", lambda e: e.scalar_tensor_tensor(out=out, in0=a, scalar=scalar, in1=b, op0=op0, op1=op1),
                        r=bufs(r), w=bufs(w))

        def ts(eng, out, a, s1, s2, op0, op1, r, w):
            return S.op(eng, lambda e: e.tensor_scalar(out=out, in0=a, scalar1=s1, scalar2=s2, op0=op0, op1=op1),
                        r=bufs(r), w=bufs(w))

        def cp(eng, out, in_, r, w):
            if eng == "act":
                return S.op("act", lambda e: e.copy(out=out, in_=in_), r=bufs(r), w=bufs(w))
            return S.op(eng, lambda e: e.tensor_copy(out=out, in_=in_), r=bufs(r), w=bufs(w))

        X = sb(top, "X", [128, NKC, ntot])
        WB = sb(top, "WB", [128, 33280], BF16)
        pp = sb(top, "pp", [128, 80])
        cf = sb(top, "cf", [128, 10, 128])
        cb = sb(top, "cb", [128, 10, 128], BF16)
        smask = sb(top, "smask", [128, GT + C0P])
        wgate = sb(top, "wgate", [16, 512])
        negb = sb(top, "negb", [128, 4])
        epsb = sb(top, "epsb", [128, 1])
        r_ld = S.rec("ld")
        S.dma("sp", X[:], xT, r_ld, w=[X.b])
        S.dma("sp", pp[:], pp_d, r_ld, w=[pp.b])
        S.dma("sp", cf[:], cf_d, r_ld, w=[cf.b])
        S.dma("sp", smask[:], smask_d, r_ld, w=[smask.b])
        S.dma("sp", wgate[:], gla_w_gate, r_ld, w=[wgate.b])
        for t_ in (X, pp, cf, smask, wgate):
            t_.b.w = ("d", r_ld, r_ld.val)
        cp("dve", cb[:], cf[:], [cf], [cb])
        ts("dve", negb[:], pp[:, 64:68], -1.0, None, ALU.mult, ALU.bypass, [pp], [negb])
        S.op("dve", lambda e: e.memset(epsb[:], EPS), w=[epsb.b])
        IDENT = cb[:, 0, :]
        ONESB = cb[:, 1, :]
        TRIB = cb[:, 2, :]
        PERM = cf[:, 3, :]
        O1024 = cb[:, 4, :]
        O256 = cb[:, 5, :]
        O128 = cb[:, 6, :]

        PS = [pst(top, f"ps{i}") for i in range(7)]
        PSB = pst(top, "psb", (128, 1024), BF16)
        wrecs = [S.rec(f"w{i}") for i in range(6)]
        orec = {}

        def out_dma(out, in_, r):
            key = r[0].b.name
            if key not in orec:
                orec[key] = S.rec("o_" + key)
            S.dma("sp", out, in_, orec[key], r=bufs(r))

        def wload(dst_ap, src_ap, buf, i):
            S.dma("pool", dst_ap, src_ap, wrecs[i], w=[buf])

        def wview(lo, k, n):
            return WB[:, lo:lo + k * n].rearrange("p (k n) -> p k n", k=k)

        def layer_norm(c0, n, gcol, bcol, zb, zsq, tmp, psA, psB):
            m2, var, n1, n2 = tmp
            xs = X[:, :, c0:c0 + n]
            cp("pool", zb[:, :, 0:n], xs, [X], [zb])
            act(zsq[:, :, 0:n], xs, AF.Square, [X], [zsq])
            for kc in range(NKC):
                mm(psA[:, 0:n], O1024, zb[:, kc, 0:n], [cb, zb], [psA], start=(kc == 0), stop=(kc == NKC - 1), sig=(kc == NKC - 1))
            for kc in range(NKC):
                mm(psB[:, 0:n], O1024, zsq[:, kc, 0:n], [cb, zsq], [psB], start=(kc == 0), stop=(kc == NKC - 1), sig=(kc == NKC - 1))
            act(m2[:, 0:n], psA[:, 0:n], AF.Square, [psA], [m2])
            tt("dve", var[:, 0:n], psB[:, 0:n], m2[:, 0:n], ALU.subtract, [psB, m2], [var])
            act(var[:, 0:n], var[:, 0:n], AF.Ln, [var], [var], bias=epsb[:, 0:1])
            act(var[:, 0:n], var[:, 0:n], AF.Exp, [var], [var], scale=-0.5)
            cp("act", m2[:, 0:n], psA[:, 0:n], [psA], [m2])
            for kc in range(NKC):
                tt("dve", n1[:, 0:n], X[:, kc, c0:c0 + n], m2[:, 0:n], ALU.subtract, [X, m2], [n1])
                tt("pool", n2[:, 0:n], n1[:, 0:n], var[:, 0:n], ALU.mult, [n1, var], [n2])
                act(X[:, kc, c0:c0 + n], n2[:, 0:n], AF.Identity, [n2, pp], [X],
                    scale=pp[:, gcol + kc:gcol + kc + 1], bias=pp[:, bcol + kc:bcol + kc + 1])

        def mlp(layer):
            with ExitStack() as st:
                Xb = sb(st, "Xb", [128, NKC, ntot], BF16)
                hT = [sb(st, f"hT{i}", [128, 8, 512], BF16) for i in range(2)]
                rl = [sb(st, f"rl{i}", [128, 512]) for i in range(2)]
                tmp = [sb(st, f"lt{i}", [128, 512]) for i in range(4)]
                wb = [Buf("w1a"), Buf("w2a"), Buf("w1b"), Buf("w2b")]
                for (c0, n) in tiles:
                    for kc in range(NKC):
                        eng = ["dve", "act", "pool"][kc % 3]
                        cp(eng, Xb[:, kc, c0:c0 + n], X[:, kc, c0:c0 + n], [X], [Xb])
                w1d = mlp_w1[layer].rearrange("(k p) n -> p k n", p=128)
                w2d = mlp_w2[layer].rearrange("(k p) n -> p k n", p=128)
                hi = 0
                for q in range(4):
                    par = q % 2
                    w1v = wview(par * 16384, 8, 1024)
                    w2v = wview(par * 16384 + 8192, 8, 1024)
                    b1, b2 = wb[par * 2], wb[par * 2 + 1]
                    wload(w1v, w1d[:, :, q * 1024:(q + 1) * 1024], b1, par * 2)
                    wload(w2v, w2d[:, q * 8:(q + 1) * 8, :], b2, par * 2 + 1)
                    W1 = T(None, "w1")
                    W1.b = b1
                    W2 = T(None, "w2")
                    W2.b = b2
                    for (c0, n) in tiles:
                        h = hT[hi % 2]
                        hi += 1
                        for fc in range(8):
                            ps = PS[fc % 2]
                            for kc in range(NKC):
                                mm(ps[:, 0:n], w1v[:, kc, fc * 128:(fc + 1) * 128], Xb[:, kc, c0:c0 + n], [W1, Xb], [ps],
                                   start=(kc == 0), stop=(kc == NKC - 1), sig=(kc == NKC - 1))
                            r_ = rl[fc % 2]
                            act(r_[:, 0:n], ps[:, 0:n], AF.Relu, [ps], [r_])
                            tt("pool" if fc % 2 else "dve", h[:, fc, 0:n], r_[:, 0:n], r_[:, 0:n], ALU.mult, [r_], [h])
                        for dc in range(NKC):
                            ps = PS[2 + dc % 2]
                            for fc in range(8):
                                mm(ps[:, 0:n], w2v[:, fc, dc * 128:(dc + 1) * 128], h[:, fc, 0:n], [W2, h], [ps],
                                   start=(fc == 0), stop=(fc == 7), sig=(fc == 7))
                            stt(X[:, dc, c0:c0 + n], X[:, dc, c0:c0 + n], ALPHA if q == 0 else 1.0, ps[:, 0:n],
                                ALU.mult, ALU.add, [X, ps], [X])
                gi = (layer * 4 + 2) * 8
                for (c0, n) in tiles:
                    layer_norm(c0, n, gi, gi + 8, hT[0], hT[1], tmp, PS[4], PS[5])
                S.barrier()

        def gla():
            with ExitStack() as st:
                wq = wview(0, 8, 512)
                wk = wview(4096, 8, 512)
                wv = wview(8192, 8, 1024)
                wr = wview(16384, 8, 1024)
                wg = wview(24576, 8, 16)
                wo = wview(24704, 8, 1024)
                Wq, Wk, Wv, Wr, Wg, Wo = [T(None, n_) for n_ in ("gwq", "gwk", "gwv", "gwr", "gwg", "gwo")]
                wd = gla_w_in.rearrange("(k p) n -> p k n", p=128)
                S.dma("pool", wq, wd[:, :, 0:512], wrecs[0], w=[Wq.b])
                S.dma("pool", wk, wd[:, :, 512:1024], wrecs[1], w=[Wk.b])
                S.dma("pool", wg, wd[:, :, 3072:3088], wrecs[4], w=[Wg.b])
                S.dma("pool", wv, wd[:, :, 1024:2048], wrecs[2], w=[Wv.b])
                S.dma("pool", wr, wd[:, :, 2048:3072], wrecs[3], w=[Wr.b])
                S.dma("pool", wo, gla_w_out.rearrange("(k p) n -> p k n", p=128), wrecs[5], w=[Wo.b])
                xb = sb(st, "xb", [128, NKC, GT], BF16)
                glow = sb(st, "glow", [16, GT])
                e1 = sb(st, "e1", [128, GT])
                bc = sb(st, "bc", [128, GT])
                enb = sb(st, "enb", [128, GT])
                eb = [sb(st, f"eb{h}", [128, GT]) for h in range(4)]
                qt = [sb(st, f"qt{h}", [128, GT], BF16) for h in range(4)]
                kt = [sb(st, f"kt{h}", [128, GT], BF16) for h in range(4)]
                kh = sb(st, "kh", [128, GT], BF16)
                khtok = [sb(st, f"khtok{i}", [64, 4, 128], BF16) for i in range(2)]
                vtok = [sb(st, f"vtok{i}", [64, 1024], BF16) for i in range(2)]
                AT = [sb(st, f"AT{i}", [64, 4, 64], BF16) for i in range(2)]
                OT = sb(st, "OT", [128, 8, GT])
                sq = sb(st, "sq", [128, 2, GT], BF16)
                rstd = sb(st, "rstd", [128, GT])
                sr = sb(st, "sr", [128, GT])
                tq = sb(st, "tq", [128, GT])
                og = sb(st, "og", [128, 8, GT], BF16)
                Sst = sb(st, "Sst", [128, 4, 256])
                Sbf = sb(st, "Sbf", [128, 4, 256], BF16)
                S0s = [sb(st, f"S0_{i}", [128, 4, 256]) for i in range(2)]
                S0bfs = [sb(st, f"S0bf_{i}", [128, 4, 256], BF16) for i in range(2)]
                srecs = [S.rec("st0"), S.rec("st1")]
                S.op("dve", lambda e: e.memset(Sst[:], 0.0), w=[Sst.b])
                S.op("pool", lambda e: e.memset(Sbf[:], 0.0), w=[Sbf.b])
                cctr = [0]

                def chunkA(cl, L):
                    i2 = cctr[0] % 2
                    cctr[0] += 1
                    vt, kk, at = vtok[i2], khtok[i2], AT[i2]
                    for half in range(2):
                        ps = PS[half]
                        for kc in range(NKC):
                            mm(ps[0:L, :], xb[:, kc, cl:cl + L], wv[:, kc, half * 512:(half + 1) * 512], [xb, Wv], [ps],
                               start=(kc == 0), stop=(kc == NKC - 1), sig=(kc == NKC - 1))
                        cp("act", vt[0:L, half * 512:(half + 1) * 512], ps[0:L, :], [ps], [vt])
                    for h in range(4):
                        S.op("pe", lambda e, h=h: e.transpose(PSB[0:L, h * 128:(h + 1) * 128], khT4[h][:, cl:cl + L], IDENT),
                             r=[khT4[h].b, cb.b], w=[PSB.b], sig=(h == 3))
                    cp("dve", kk[0:L, :, :], PSB[0:L, 0:512].rearrange("p (h d) -> p h d", h=4), [PSB], [kk])
                    psS = PS[2]
                    for h in range(4):
                        mm(psS[0:L, h * 64:h * 64 + L], kt[h][:, cl:cl + L], qt[h][:, cl:cl + L], [kt[h], qt[h]], [psS], sig=(h == 3))
                    tt("dve", at[0:L, :, 0:L], psS[0:L, 0:256].rearrange("p (h i) -> p h i", h=4)[:, :, 0:L],
                       TRIB[0:L, 0:L].unsqueeze(1).to_broadcast([L, 4, L]), ALU.mult, [psS, cb], [at])
                    return (cl, L, vt, kk, at)

                def chunkB(ctx, St, Sb, keep_bf=True):
                    cl, L, vt, kk, at = ctx
                    psO = PS[3]
                    for h in range(4):
                        for ec in range(2):
                            j = h * 2 + ec
                            mm(psO[:, j * 64:j * 64 + L], Sb[:, h, ec * 128:(ec + 1) * 128], qt[h][:, cl:cl + L], [Sb, qt[h]], [psO],
                               start=True, stop=False, sig=False)
                            mm(psO[:, j * 64:j * 64 + L], vt[0:L, h * 256 + ec * 128:h * 256 + (ec + 1) * 128], at[0:L, h, 0:L],
                               [vt, at], [psO], start=False, stop=True, sig=(j == 7))
                    cp("act", OT[:, :, cl:cl + L], psO[:, :].rearrange("p (j i) -> p j i", j=8)[:, :, 0:L], [psO], [OT])
                    for h in range(4):
                        ps = PS[4 + h // 2]
                        mm(ps[:, (h % 2) * 256:(h % 2 + 1) * 256], kk[0:L, h, :], vt[0:L, h * 256:(h + 1) * 256], [kk, vt], [ps],
                           sig=(h % 2 == 1))
                    for h in range(4):
                        ps = PS[4 + h // 2]
                        stt(St[:, h, :], St[:, h, :], eb[h][:, cl + L - 1:cl + L], ps[:, (h % 2) * 256:(h % 2 + 1) * 256],
                            ALU.mult, ALU.add, [St, eb[h], ps], [St])
                    if keep_bf:
                        cp("pool", Sb[:], St[:], [St], [Sb])

                khT4 = [sb(st, f"khT{h}", [128, GT], BF16) for h in range(4)]

                for ti, (c0, n) in enumerate(GTILES):
                    thin = (ti == 0)
                    for kc in range(NKC):
                        cp(["dve", "pool"][kc % 2], xb[:, kc, 0:n], X[:, kc, c0:c0 + n], [X], [xb])
                    for kc in range(NKC):
                        mm(PS[6][0:16, 0:n], wg[:, kc, :], xb[:, kc, 0:n], [Wg, xb], [PS[6]], start=(kc == 0), stop=(kc == NKC - 1),
                           sig=(kc == NKC - 1))
                    cp("act", glow[:, 0:n], PS[6][0:16, 0:n], [PS[6]], [glow])
                    mk = smask[:, GT:GT + n] if thin else smask[:, 0:n]
                    for h in range(4):
                        mm(PS[6][:, 0:n], wgate[:, h * 128:(h + 1) * 128], glow[:, 0:n], [wgate, glow], [PS[6]])
                        act(e1[:, 0:n], PS[6][:, 0:n], AF.Exp, [PS[6], negb], [e1], scale=-1.0, bias=negb[:, h:h + 1])
                        act(e1[:, 0:n], e1[:, 0:n], AF.Ln, [e1], [e1], bias=1.0)
                        S.op("dve", lambda e: e.tensor_tensor_scan(out=bc[:, 0:n], data0=mk, data1=e1[:, 0:n], initial=0.0,
                                                                    op0=ALU.mult, op1=ALU.add), r=[smask.b, e1.b], w=[bc.b])
                        act(eb[h][:, 0:n], bc[:, 0:n], AF.Exp, [bc], [eb[h]], scale=-1.0 / 16.0)
                        act(enb[:, 0:n], bc[:, 0:n], AF.Exp, [bc], [enb], scale=1.0 / 16.0)
                        for kc in range(NKC):
                            mm(PS[0][:, 0:n], wq[:, kc, h * 128:(h + 1) * 128], xb[:, kc, 0:n], [Wq, xb], [PS[0]], start=(kc == 0),
                               stop=(kc == NKC - 1), sig=(kc == NKC - 1))
                        stt(qt[h][:, 0:n], PS[0][:, 0:n], 128.0 ** -0.5, eb[h][:, 0:n], ALU.mult, ALU.mult, [PS[0], eb[h]], [qt[h]])
                        for kc in range(NKC):
                            mm(PS[1][:, 0:n], wk[:, kc, h * 128:(h + 1) * 128], xb[:, kc, 0:n], [Wk, xb], [PS[1]], start=(kc == 0),
                               stop=(kc == NKC - 1), sig=(kc == NKC - 1))
                        tt("dve", kt[h][:, 0:n], PS[1][:, 0:n], enb[:, 0:n], ALU.mult, [PS[1], enb], [kt[h]])
                        if thin:
                            tt("pool", khT4[h][:, 0:16], kt[h][:, 0:16], eb[h][:, 15:16].to_broadcast([128, 16]), ALU.mult,
                               [kt[h], eb[h]], [khT4[h]])
                            tt("pool", khT4[h][:, 16:n], kt[h][:, 16:n], eb[h][:, 16:n], ALU.mult, [kt[h], eb[h]], [khT4[h]])
                        else:
                            nch = n // 64
                            tt("pool", khT4[h][:, 0:n].rearrange("p (c l) -> p c l", c=nch),
                               kt[h][:, 0:n].rearrange("p (c l) -> p c l", c=nch),
                               eb[h][:, 0:n].rearrange("p (c l) -> p c l", c=nch)[:, :, 63:64].to_broadcast([128, nch, 64]),
                               ALU.mult, [kt[h], eb[h]], [khT4[h]])
                    if thin:
                        S.op("dve", lambda e: e.memset(OT[:, :, 0:n], 0.0), w=[OT.b])
                        pend = (chunkA(0, 16), Sst, Sbf, None)
                        for i in range(32):
                            S0, S0bf = S0s[i % 2], S0bfs[i % 2]
                            S.dma("sp", S0[:], state_own[i].rearrange("h p e -> p h e"), srecs[i % 2], w=[S0.b])
                            cp("pool", S0bf[:], S0[:], [S0], [S0bf])
                            ctx = chunkA(16 + i, 1)
                            chunkB(pend[0], pend[1], pend[2], keep_bf=(pend[3] is None))
                            if pend[3] is not None:
                                out_dma(gss[pend[3]].rearrange("h p e -> p h e"), pend[1][:], [pend[1]])
                            pend = (ctx, S0, S0bf, i)
                        chunkB(pend[0], pend[1], pend[2], keep_bf=False)
                        out_dma(gss[pend[3]].rearrange("h p e -> p h e"), pend[1][:], [pend[1]])
                    else:
                        pend = None
                        for ci in range(n // 64):
                            ctx = chunkA(ci * 64, 64)
                            if pend is not None:
                                chunkB(pend, Sst, Sbf)
                            pend = ctx
                        chunkB(pend, Sst, Sbf)
                    for h in range(4):
                        act(sq[:, :, 0:n], OT[:, 2 * h:2 * h + 2, 0:n], AF.Square, [OT], [sq])
                        for ec in range(2):
                            mm(PS[6][:, 0:n], O256, sq[:, ec, 0:n], [cb, sq], [PS[6]], start=(ec == 0), stop=(ec == 1), sig=(ec == 1))
                        act(rstd[:, 0:n], PS[6][:, 0:n], AF.Ln, [PS[6]], [rstd], bias=epsb[:, 0:1])
                        act(rstd[:, 0:n], rstd[:, 0:n], AF.Exp, [rstd], [rstd], scale=-0.5)
                        for ec in range(2):
                            j = 2 * h + ec
                            ps = PS[ec]
                            for kc in range(NKC):
                                mm(ps[:, 0:n], wr[:, kc, j * 128:(j + 1) * 128], xb[:, kc, 0:n], [Wr, xb], [ps], start=(kc == 0),
                                   stop=(kc == NKC - 1), sig=(kc == NKC - 1))
                            act(sr[:, 0:n], ps[:, 0:n], AF.Silu, [ps], [sr])
                            stt(tq[:, 0:n], OT[:, j, 0:n], pp[:, 68 + ec:69 + ec], rstd[:, 0:n], ALU.mult, ALU.mult, [OT, pp, rstd], [tq])
                            tt("pool", og[:, j, 0:n], tq[:, 0:n], sr[:, 0:n], ALU.mult, [tq, sr], [og])
                    for dc in range(NKC):
                        ps = PS[dc % 2]
                        for j in range(8):
                            mm(ps[:, 0:n], wo[:, j, dc * 128:(dc + 1) * 128], og[:, j, 0:n], [Wo, og], [ps], start=(j == 0), stop=(j == 7),
                               sig=(j == 7))
                        stt(X[:, dc, c0:c0 + n], X[:, dc, c0:c0 + n], ALPHA, ps[:, 0:n], ALU.mult, ALU.add, [X, ps], [X])
                    layer_norm(c0, n, 0, 8, xb, og, [e1, bc, sr, tq], PS[2], PS[3])
                out_dma(gsp.rearrange("h p e -> p h e"), Sst[:], [Sst])
                S.barrier()


        def diff():
            with ExitStack() as st2:
                VT = sb(st2, "VT", [128, 17, 1024], BF16)
                KTv = WB[:, 16384:16384 + 8 * NT].rearrange("p (h n) -> p h n", h=8)
                KT = T(None, "KT")
                R0, R1 = T(None, "wr0"), T(None, "wr1")
                w0v = wview(0, 8, 1024)
                w1v = wview(8192, 8, 1024)
                neglam = sb(st2, "neglam", [128, 1])
                gsub = sb(st2, "gsub", [128, 1])
                wd = diff_w_in.rearrange("(k p) n -> p k n", p=128)
                crec = S.rec("cs")
                crec2 = S.rec("sn")
                with ExitStack() as st:
                    lp = sb(st, "lp", [1, 256])
                    pr = sb(st, "pr", [1, 128])
                    s2 = sb(st, "s2", [1, 2])
                    l1 = sb(st, "l1", [1, 1])
                    S.dma("sp", lp[:], lam_d, crec, w=[lp.b])
                    tt("dve", pr[:].rearrange("p (a d) -> p a d", a=2), lp[:].rearrange("p (a b d) -> p a b d", a=2, b=2)[:, :, 0, :],
                       lp[:].rearrange("p (a b d) -> p a b d", a=2, b=2)[:, :, 1, :], ALU.mult, [lp], [pr])
                    S.op("dve", lambda e: e.tensor_reduce(out=s2[:], in_=pr[:].rearrange("p (a d) -> p a d", a=2),
                                                          axis=mybir.AxisListType.X, op=ALU.add), r=[pr.b], w=[s2.b])
                    act(s2[:], s2[:], AF.Exp, [s2], [s2])
                    tt("dve", l1[:], s2[:, 1:2], s2[:, 0:1], ALU.subtract, [s2], [l1])
                    ts("dve", l1[:], l1[:], -LAM_INIT, None, ALU.add, ALU.bypass, [l1], [l1])
                    mm(PS[6][:, 0:1], cf[0:1, 1, :], l1[0:1, 0:1], [cf, l1], [PS[6]])
                    cp("act", neglam[:], PS[6][:, 0:1], [PS[6]], [neglam])
                    ts("dve", gsub[:], pp[:, 70:71], 1.0 - LAM_INIT, None, ALU.mult, ALU.bypass, [pp], [gsub])
                    S.barrier()

                with ExitStack() as st:
                    S.dma("pool", w0v, wd[:, :, 1024:2048], wrecs[0], w=[R0.b])
                    S.dma("pool", w1v, wd[:, :, 2048:3072], wrecs[1], w=[R1.b])
                    xb = sb(st, "xbA", [128, NKC, 512], BF16)
                    cs = sb(st, "cosA", [128, 512])
                    sn = sb(st, "sinA", [128, 512])
                    kf = sb(st, "kf", [128, 512])
                    t1 = sb(st, "t1A", [128, 512])
                    t2 = sb(st, "t2A", [128, 512])
                    kst = [sb(st, f"kst{i}", [128, 512]) for i in range(2)]
                    vst = [sb(st, f"vst{i}", [128, 1024]) for i in range(2)]
                    vi = 0
                    order = list(range(1, len(TILES))) + [0]
                    for ti in order:
                        c0, n = TILES[ti]
                        for kc in range(NKC):
                            cp(["dve", "pool"][kc % 2], xb[:, kc, 0:n], X[:, kc, c0:c0 + n], [X], [xb])
                        S.dma("sp", cs[:, 0:n], cos_d[:, c0:c0 + n], crec, w=[cs.b])
                        S.dma("sp", sn[:, 0:n], sin_d[:, c0:c0 + n], crec2, w=[sn.b])
                        for h in range(8):
                            ps = PS[h % 2]
                            for kc in range(NKC):
                                mm(ps[:, 0:n], w0v[:, kc, h * 128:(h + 1) * 128], xb[:, kc, 0:n], [R0, xb], [ps], start=(kc == 0),
                                   stop=(kc == NKC - 1), sig=(kc == NKC - 1))
                            cp("act", kf[:, 0:n], ps[:, 0:n], [ps], [kf])
                            ps2 = PS[2 + h % 2]
                            mm(ps2[:, 0:n], PERM, kf[:, 0:n], [cf, kf], [ps2])
                            tt("pool", t1[:, 0:n], kf[:, 0:n], cs[:, 0:n], ALU.mult, [kf, cs], [t1])
                            tt("dve", t2[:, 0:n], ps2[:, 0:n], sn[:, 0:n], ALU.mult, [ps2, sn], [t2])
                            ks_ = kst[h % 2]
                            tt("dve", ks_[:, 0:n], t1[:, 0:n], t2[:, 0:n], ALU.add, [t1, t2], [ks_])
                            cp("act", KTv[:, h, c0:c0 + n], ks_[:, 0:n], [ks_], [KT])
                            out_dma(kTo[:, h, c0:c0 + n], ks_[:, 0:n], [ks_])
                        nblk = max(1, n // 128)
                        for blk in range(nblk):
                            rows = min(128, n)
                            kt_i = 0 if ti == 0 else 1 + (ti - 1) * 4 + blk
                            v_ = vst[vi % 2]
                            vi += 1
                            for half in range(2):
                                ps = PS[4 + half]
                                for kc in range(NKC):
                                    mm(ps[0:rows, :], xb[:, kc, blk * 128:blk * 128 + rows], w1v[:, kc, half * 512:(half + 1) * 512],
                                       [xb, R1], [ps], start=(kc == 0), stop=(kc == NKC - 1), sig=(kc == NKC - 1))
                                cp("act" if half else "dve", v_[0:rows, half * 512:(half + 1) * 512], ps[0:rows, :], [ps], [v_])
                            cp("pool", VT[0:rows, kt_i, :], v_[0:rows, :], [v_], [VT])
                            out_dma(vro[c0 + blk * 128:c0 + blk * 128 + rows, :], v_[0:rows, :], [v_])
                    S.barrier()


                if int(os.environ.get('K_STOP', '9')) == 2:
                    return
                if with_sample:
                    with ExitStack() as st:
                        NSQ = 16
                        wc = sb(st, "wc", [128, NKC, 3, 256], BF16)
                        S.dma("pool", wc[:], wqkv_c.rearrange("(k p) j d -> p k j d", p=128), wrecs[2], w=[wc.b])
                        xbs = sb(st, "xbs", [128, NKC, NSQ], BF16)
                        cS = sb(st, "cS", [128, NSQ])
                        sS = sb(st, "sS", [128, NSQ])
                        S.dma("sp", cS[:], cos_d[:, 16:16 + NSQ], crec, w=[cS.b])
                        S.dma("sp", sS[:], sin_d[:, 16:16 + NSQ], crec2, w=[sS.b])
                        cp("dve", xbs[:], X[:, :, 16:16 + NSQ], [X], [xbs])
                        f3 = [[sb(st, f"f3_{e}_{j}", [128, NSQ]) for j in range(3)] for e in range(2)]
                        ta = sb(st, "ta", [128, NSQ])
                        tb = sb(st, "tb", [128, NSQ])
                        for e in range(2):
                            for j in range(3):
                                ps = PS[4 + j % 2]
                                for kc in range(NKC):
                                    mm(ps[:, 0:NSQ], wc[:, kc, j, e * 128:(e + 1) * 128], xbs[:, kc, :], [wc, xbs], [ps], start=(kc == 0),
                                       stop=(kc == NKC - 1), sig=(kc == NKC - 1))
                                cp("act", f3[e][j][:], ps[:, 0:NSQ], [ps], [f3[e][j]])
                                if j < 2:
                                    sc_ = 0.125 if j == 0 else 1.0
                                    mm(PS[6][:, 0:NSQ], PERM, f3[e][j][:], [cf, f3[e][j]], [PS[6]])
                                    stt(ta[:], f3[e][j][:], sc_, cS[:], ALU.mult, ALU.mult, [f3[e][j], cS], [ta])
                                    stt(tb[:], PS[6][:, 0:NSQ], sc_, sS[:], ALU.mult, ALU.mult, [PS[6], sS], [tb])
                                    tt("dve", f3[e][j][:], ta[:], tb[:], ALU.add, [ta, tb], [f3[e][j]])
                        Qb = sb(st, "Qb", [128, NSQ, 2, 2], BF16)
                        S.op("dve", lambda e_: e_.memset(Qb[:], 0.0), w=[Qb.b])
                        for e in range(2):
                            cp("dve", Qb[0:64, :, e, 0], f3[e][0][0:64, :], [f3[e][0]], [Qb])
                            cp("dve", Qb[64:128, :, e, 1], f3[e][0][64:128, :], [f3[e][0]], [Qb])
                        NPT = NSQ * NPAGES
                        idx = sb(st, "idx", [128, NPT], I32)
                        io = sb(st, "io", [128, 1])
                        ptrec = S.rec("pt")
                        S.dma("sp", idx[:], pt_d.to_broadcast([128, NPT]), ptrec, w=[idx.b])
                        S.op("pool", lambda e_: e_.iota(io[:], pattern=[[0, 1]], base=0, channel_multiplier=1,
                                                        allow_small_or_imprecise_dtypes=True), w=[io.b])
                        ts("pool", idx[:], idx[:], 128.0, io[:, 0:1], ALU.mult, ALU.add, [idx, io], [idx])
                        NSL = 12
                        pg = [sb(st, f"pg{i}", [128, 512], BF16) for i in range(NSL)]
                        prec = [S.rec(f"pg{i}") for i in range(NSL)]
                        OTs = sb(st, "OTs", [128, NSQ, 2, 2])
                        Ls = sb(st, "Ls", [128, NSQ, 4])
                        Pb = [sb(st, f"Pb{i}", [128, 32], BF16) for i in range(2)]
                        PAcc = [PS[0], PS[2]]
                        gi_ = 0
                        pcnt = 0
                        for s_ in range(NSQ):
                            for g in range(8):
                                slots = []
                                for j in range(8):
                                    pj = s_ * 64 + g * 8 + j
                                    sl = pcnt % NSL
                                    pcnt += 1
                                    slots.append(sl)
                                    S._deps("pool", [idx.b], [pg[sl].b])
                                    ins = nc.gpsimd.indirect_dma_start(
                                        out=pg[sl][:], out_offset=None, in_=cache_d,
                                        in_offset=bass.IndirectOffsetOnAxis(ap=idx[:, pj:pj + 1], axis=0))
                                    rec = prec[sl]
                                    rec.val += 16
                                    ins.then_inc(rec.sem, 16)
                                    pg[sl].b.w = ("d", rec, rec.val)
                                    pg[sl].b.r = {}
                                    idx.b.r["d%d" % id(rec)] = ("d", rec, rec.val)
                                pS = PS[4 + gi_ % 2]
                                for j in range(8):
                                    for e in range(2):
                                        c_ = (j * 2 + e) * 2
                                        mm(pS[:, c_:c_ + 2], pg[slots[j]][:, e * 256:e * 256 + 128], Qb[:, s_, e, :], [pg[slots[j]], Qb], [pS],
                                           sig=(j == 7 and e == 1))
                                pb = Pb[gi_ % 2]
                                act(pb[:], pS[:, 0:32], AF.Exp, [pS], [pb])
                                for j in range(8):
                                    for e in range(2):
                                        c_ = (j * 2 + e) * 2
                                        mm(PAcc[e][:, 0:2], pg[slots[j]][:, e * 256 + 128:e * 256 + 256], pb[:, c_:c_ + 2],
                                           [pg[slots[j]], pb], [PAcc[e]], start=(g == 0 and j == 0), stop=(g == 7 and j == 7),
                                           sig=(j == 7 and e == 1))
                                mm(PS[1][:, 0:32], ONESB, pb[:], [cb, pb], [PS[1]], start=(g == 0), stop=(g == 7))
                                gi_ += 1
                            for e in range(2):
                                cp("act", OTs[:, s_, e, :], PAcc[e][:, 0:2], [PAcc[e]], [OTs])
                            S.op("dve", lambda e_, s_=s_: e_.tensor_reduce(out=Ls[:, s_, :],
                                                                          in_=PS[1][:, 0:32].rearrange("p (j q) -> p q j", q=4),
                                                                          axis=mybir.AxisListType.X, op=ALU.add), r=[PS[1].b], w=[Ls.b])
                        pself = sb(st, "pself", [128, 2, NSQ])
                        num = sb(st, "num", [128, 2, NSQ])
                        den = sb(st, "den", [128, 2, NSQ])
                        sqs = sb(st, "sqs", [128, NSQ], BF16)
                        ogs = sb(st, "ogs", [128, 2, NSQ])
                        for e in range(2):
                            qs, ks, vs = f3[e]
                            tt("dve", ta[:], qs[:], ks[:], ALU.mult, [qs, ks], [ta])
                            for m in range(2):
                                mm(PS[6][:, m * NSQ:(m + 1) * NSQ], cf[:, 7 + m, :], ta[:], [cf, ta], [PS[6]], sig=(m == 1))
                            act(pself[:], PS[6][:, 0:2 * NSQ].rearrange("p (m s) -> p m s", m=2), AF.Exp, [PS[6]], [pself])
                            for m in range(2):
                                tt("dve", num[:, m, :], pself[:, m, :], vs[:], ALU.mult, [pself, vs], [num])
                                tt("dve", num[:, m, :], num[:, m, :], OTs[:, :, e, m], ALU.add, [num, OTs], [num])
                                tt("dve", den[:, m, :], pself[:, m, :], Ls[:, :, e * 2 + m], ALU.add, [pself, Ls], [den])
                            act(den[:], den[:], AF.Ln, [den], [den])
                            act(den[:], den[:], AF.Exp, [den], [den], scale=-1.0)
                            tt("dve", num[:], num[:], den[:], ALU.mult, [num, den], [num])
                            stt(ta[:], num[:, 1, :], neglam[:, 0:1], num[:, 0, :], ALU.mult, ALU.add, [num, neglam], [ta])
                            act(sqs[:], ta[:], AF.Square, [ta], [sqs])
                            mm(PS[6][:, 0:NSQ], O128, sqs[:], [cb, sqs], [PS[6]])
                            act(tb[:], PS[6][:, 0:NSQ], AF.Ln, [PS[6]], [tb], bias=epsb[:, 0:1])
                            act(tb[:], tb[:], AF.Exp, [tb], [tb], scale=-0.5)
                            stt(ogs[:, e, :], ta[:], gsub[:, 0:1], tb[:], ALU.mult, ALU.mult, [ta, gsub, tb], [ogs])
                        out_dma(ogs_o, ogs[:], [ogs])
                        S.barrier()

                with ExitStack() as st:
                    S.dma("pool", w0v, wd[:, :, 0:1024], wrecs[0], w=[R0.b])
                    S.dma("pool", w1v, diff_w_out.rearrange("(k p) n -> p k n", p=128), wrecs[1], w=[R1.b])
                    xb = sb(st, "xbB", [128, NKC, 512], BF16)
                    cs = sb(st, "cosB", [128, 512])
                    sn = sb(st, "sinB", [128, 512])
                    qf = sb(st, "qf", [128, 512])
                    t1 = sb(st, "t1B", [128, 512])
                    t2 = sb(st, "t2B", [128, 512])
                    QT = sb(st, "QT", [128, 8, 512], BF16)
                    og = xb
                    pt = [sb(st, f"pt{i}", [128, 512], BF16) for i in range(3)]
                    rl = [cs, sn]
                    sqb = sb(st, "sqB", [128, 512], BF16)
                    pti = 0
                    for ti, (c0, n) in enumerate(TILES):
                        nq = 16 if ti == 0 else n
                        for kc in range(NKC):
                            cp(["dve", "pool"][kc % 2], xb[:, kc, 0:n], X[:, kc, c0:c0 + n], [X], [xb])
                        S.dma("sp", cs[:, 0:n], cos_d[:, c0:c0 + n], crec, w=[cs.b])
                        S.dma("sp", sn[:, 0:n], sin_d[:, c0:c0 + n], crec2, w=[sn.b])
                        for h in range(8):
                            ps = PS[4 + h % 2]
                            for kc in range(NKC):
                                mm(ps[:, 0:nq], w0v[:, kc, h * 128:(h + 1) * 128], xb[:, kc, 0:nq], [R0, xb], [ps], start=(kc == 0),
                                   stop=(kc == NKC - 1), sig=(kc == NKC - 1))
                            cp("act", qf[:, 0:nq], ps[:, 0:nq], [ps], [qf])
                            mm(PS[6][:, 0:nq], PERM, qf[:, 0:nq], [cf, qf], [PS[6]])
                            stt(t1[:, 0:nq], qf[:, 0:nq], 0.125, cs[:, 0:nq], ALU.mult, ALU.mult, [qf, cs], [t1])
                            stt(t2[:, 0:nq], PS[6][:, 0:nq], 0.125, sn[:, 0:nq], ALU.mult, ALU.mult, [PS[6], sn], [t2])
                            tt("pool", QT[:, h, 0:nq], t1[:, 0:nq], t2[:, 0:nq], ALU.add, [t1, t2], [QT])
                        if ti == 0:
                            kl = [(0, 16, 0, 0, True)]
                        else:
                            i = ti - 1
                            kl = [(0, 16, 0, 0, False)]
                            for r in (3, 2, 1, 0):
                                j = 4 * i + r
                                kl.append((C0P + 128 * j, 128, 1 + j, 128 * r, True))
                            for j in range(4 * i):
                                kl.append((C0P + 128 * j, 128, 1 + j, 0, False))
                        for h in range(8):
                            units = [(m, ki) for m in range(2) for ki in range(len(kl))]
                            pend = None
                            for u in units + [None]:
                                cur = None
                                if u is not None:
                                    m, ki = u
                                    kc0, nk, vti, qlo, diag = kl[ki]
                                    w_ = nq - qlo
                                    pS = PS[4 + pti % 2]
                                    p_ = pt[pti % 3]
                                    pti += 1
                                    mm(pS[0:nk, 0:w_], KTv[m * 64:(m + 1) * 64, h, kc0:kc0 + nk], QT[m * 64:(m + 1) * 64, h, qlo:nq],
                                       [KT, QT], [pS])
                                    act(p_[0:nk, 0:w_], pS[0:nk, 0:w_], AF.Exp, [pS], [p_])
                                    if diag:
                                        dw = min(128, w_)
                                        tt("dve", p_[0:nk, 0:dw], p_[0:nk, 0:dw], TRIB[0:nk, 0:dw], ALU.mult, [p_, cb], [p_])
                                    cur = (m, ki, p_)
                                if pend is not None:
                                    m, ki, p_ = pend
                                    kc0, nk, vti, qlo, diag = kl[ki]
                                    w_ = nq - qlo
                                    first, last = (ki == 0), (ki == len(kl) - 1)
                                    pO, pL = PS[m], PS[2 + m]
                                    mm(pO[:, qlo:nq], VT[0:nk, vti, h * 128:(h + 1) * 128], p_[0:nk, 0:w_], [VT, p_], [pO],
                                       start=first, stop=last)
                                    mm(pL[:, qlo:nq], ONESB[0:nk, :], p_[0:nk, 0:w_], [cb, p_], [pL], start=first, stop=last)
                                    if last:
                                        act(rl[m][:, 0:nq], pL[:, 0:nq], AF.Ln, [pL], [rl[m]])
                                        act(rl[m][:, 0:nq], rl[m][:, 0:nq], AF.Exp, [rl[m]], [rl[m]], scale=-1.0)
                                pend = cur
                            tt("dve", t1[:, 0:nq], PS[0][:, 0:nq], rl[0][:, 0:nq], ALU.mult, [PS[0], rl[0]], [t1])
                            tt("dve", t2[:, 0:nq], PS[1][:, 0:nq], rl[1][:, 0:nq], ALU.mult, [PS[1], rl[1]], [t2])
                            stt(t1[:, 0:nq], t2[:, 0:nq], neglam[:, 0:1], t1[:, 0:nq], ALU.mult, ALU.add, [t2, neglam, t1], [t1])
                            act(sqb[:, 0:nq], t1[:, 0:nq], AF.Square, [t1], [sqb])
                            mm(PS[6][:, 0:nq], O128, sqb[:, 0:nq], [cb, sqb], [PS[6]])
                            act(qf[:, 0:nq], PS[6][:, 0:nq], AF.Ln, [PS[6]], [qf], bias=epsb[:, 0:1])
                            act(qf[:, 0:nq], qf[:, 0:nq], AF.Exp, [qf], [qf], scale=-0.5)
                            stt(og[:, h, 0:nq], t1[:, 0:nq], gsub[:, 0:1], qf[:, 0:nq], ALU.mult, ALU.mult, [t1, gsub, qf], [og])
                        if ti == 0:
                            sample_og(og)
                        for dc in range(NKC):
                            ps = PS[4 + dc % 2]
                            for h in range(8):
                                mm(ps[:, 0:n], w1v[:, h, dc * 128:(dc + 1) * 128], og[:, h, 0:n], [R1, og], [ps], start=(h == 0), stop=(h == 7),
                                   sig=(h == 7))
                            stt(X[:, dc, c0:c0 + n], X[:, dc, c0:c0 + n], ALPHA, ps[:, 0:n], ALU.mult, ALU.add, [X, ps], [X])
                        layer_norm(c0, n, 32, 40, xb, QT, [qf, t1, t2, rl[0]], PS[4], PS[5])
                    S.barrier()

        def sample_og(og):
            S.op("dve", lambda e: e.memset(og[:, :, 16:48], 0.0), w=[og.b])


        def tail_prog():
            with ExitStack() as st:
                ogf = sb(st, "ogf", [128, 8, 32])
                ogb = sb(st, "ogb", [128, 8, 32], BF16)
                trec = S.rec("tl")
                S.dma("sp", ogf[:], og_all_d, trec, w=[ogf.b])
                cp("dve", ogb[:], ogf[:], [ogf], [ogb])
                w1v = wview(8192, 8, 1024)
                R1 = T(None, "r1")
                S.dma("pool", w1v, diff_w_out.rearrange("(k p) n -> p k n", p=128), wrecs[1], w=[R1.b])
                for dc in range(NKC):
                    ps = PS[4 + dc % 2]
                    for h in range(8):
                        mm(ps[:, 0:32], w1v[:, h, dc * 128:(dc + 1) * 128], ogb[:, h, :], [R1, ogb], [ps], start=(h == 0), stop=(h == 7),
                           sig=(h == 7))
                    stt(X[:, dc, 0:32], X[:, dc, 0:32], ALPHA, ps[:, 0:32], ALU.mult, ALU.add, [X, ps], [X])
                zb = sb(st, "zbt", [128, 8, 32], BF16)
                zq = sb(st, "zqt", [128, 8, 32], BF16)
                tmp = [sb(st, f"tt{i}", [128, 32]) for i in range(4)]
                layer_norm(0, 32, 32, 40, zb, zq, tmp, PS[4], PS[5])
                S.barrier()

        if tail:
            tail_prog()
            mlp(1)
            out_dma(yT, X[:], [X])
            S.barrier()
            return nc
        KSTOP = int(os.environ.get('K_STOP', '9'))
        gla()
        if KSTOP >= 1:
            mlp(0)
        out_dma(xmid_o, X[:, :, 16:48], [X])
        if KSTOP >= 2:
            diff()
        if KSTOP >= 4:
            mlp(1)
        out_dma(yT, X[:], [X])
        S.barrier()
    return nc


def _consts():
    cf = np.zeros((128, 10, 128), np.float32)
    cf[0:64, 7, :] = 1.0
    cf[64:128, 8, :] = 1.0
    cf[:, 6, :] = 1.0 / 128
    cf[:, 0, :] = np.eye(128)
    cf[:, 1, :] = 1.0
    cf[:, 2, :] = np.triu(np.ones((128, 128)))
    for p in range(128):
        d = p % 64
        if d < 8:
            cf[p + 8, 3, p] = -1.0
        elif d < 16:
            cf[p - 8, 3, p] = 1.0
    cf[:, 4, :] = 1.0 / 1024
    cf[:, 5, :] = 1.0 / 256
    sm = np.ones((128, GT + C0P), np.float32)
    sm[:, 0:GT:64] = 0.0
    sm[:, GT] = 0.0
    sm[:, GT + 16:] = 0.0
    pos = np.zeros(NT, np.float32)
    pos[0:16] = np.arange(16)
    pos[16:48] = 8192
    pos[48:] = 16 + np.arange(SEQ)
    inv = (np.float32(500000.0) ** (-(np.arange(0, 16, 2, dtype=np.float32)) / np.float32(16))).astype(np.float32)
    rc = np.ones((128, NT), np.float32)
    rs = np.zeros((128, NT), np.float32)
    for p in range(128):
        d = p % 64
        if d < 16:
            ang = (pos * inv[d % 8]).astype(np.float32)
            rc[p] = np.cos(ang)
            rs[p] = np.sin(ang)
    return cf, sm, rc, rs


def kernel(x_prompt, x_sample, state_gla, cache_k, cache_v, page_table, meta_tokens,
           gla_w_in, gla_w_gate, gla_b_gate, gla_norm, gla_w_out,
           diff_w_in, diff_lambda, diff_norm, diff_w_out,
           mlp_w1, mlp_w2, ln_mix_g, ln_mix_b, ln_mlp_g, ln_mlp_b, _ncores=NCORES, _dev=False, _npool=None):
    f = lambda a: np.ascontiguousarray(np.asarray(a, dtype=np.float32))
    x_prompt, x_sample, state_gla, meta_tokens = f(x_prompt), f(x_sample), f(state_gla), f(meta_tokens)
    ncores = _ncores
    npool = 8 if _dev else (_npool or NPOOL)
    cf, sm, rc, rs = _consts()
    pp = np.zeros((128, 80), np.float32)
    lnp = [ln_mix_g, ln_mix_b, ln_mlp_g, ln_mlp_b]
    for layer in range(2):
        for j in range(4):
            pp[:, (layer * 4 + j) * 8:(layer * 4 + j + 1) * 8] = f(lnp[j])[layer].reshape(8, 128).T
    pp[:, 64:68] = f(gla_b_gate)[0].reshape(4, 128).T
    pp[:, 68:70] = f(gla_norm)[0].reshape(2, 128).T
    pp[:, 70] = f(diff_norm)[0]
    tail_shared = {
        "gla_w_gate": f(gla_w_gate)[0], "diff_w_out": f(diff_w_out)[0], "mlp_w1": f(mlp_w1), "mlp_w2": f(mlp_w2),
        "pp": pp, "cf": cf, "smask": sm,
    }
    shared = dict(tail_shared)
    shared.update({
        "gla_w_in": f(gla_w_in)[0], "gla_w_out": f(gla_w_out)[0], "diff_w_in": f(diff_w_in)[0],
        "lam": f(diff_lambda)[0].reshape(1, 256), "rcos": rc, "rsin": rs,
        "ptab": np.ascontiguousarray(np.asarray(page_table, dtype=np.int32).reshape(1, -1)),
        "state_all": np.ascontiguousarray(state_gla[0]),
    })
    dwi = shared["diff_w_in"]
    pt_full = np.asarray(page_table, dtype=np.int32)
    shared.pop("ptab")
    shared.pop("state_all")
    in_maps = []
    perms = []
    cache_pair = {}
    for c in range(ncores):
        pair, half = c // 2, c % 2
        own = list(range(half * 16, half * 16 + 16))
        perm = own + [i for i in range(32) if i not in own]
        perms.append(perm)
        xt = np.zeros((D, NT), np.float32)
        xt[:, 0:16] = meta_tokens.T
        xt[:, 16:48] = x_sample[perm, 0, :].T
        xt[:, C0P:] = x_prompt[c].T
        m = dict(shared)
        m["xT"] = np.ascontiguousarray(xt.reshape(8, 128, NT).transpose(1, 0, 2))
        m["state_all"] = np.ascontiguousarray(state_gla[0][perm])
        m["ptab"] = np.ascontiguousarray(pt_full[own].reshape(1, -1))
        h0 = 2 * pair
        m["wqkv_c"] = np.ascontiguousarray(
            np.stack([dwi[:, j * 1024 + h0 * 128:j * 1024 + (h0 + 2) * 128] for j in range(3)], axis=1))
        if _dev:
            m["cache"] = np.zeros((npool * 128, 512), np.float32)
        else:
            if pair not in cache_pair:
                parts = []
                for e in range(2):
                    kc_ = np.asarray(cache_k)[0, :npool, :, h0 + e, :]
                    vc_ = np.asarray(cache_v)[0, :npool, :, h0 + e, :]
                    parts += [kc_.transpose(0, 2, 1), vc_]
                cache_pair = {pair: np.ascontiguousarray(
                    np.concatenate(parts, axis=2).reshape(npool * 128, 512).astype(np.float32))}
            m["cache"] = cache_pair[pair]
        in_maps.append(m)
    nc = build(ncores, npool, not _dev)
    res = run_bass_kernel_spmd(nc, in_maps, core_ids=list(range(ncores))).results
    B = ncores
    y_prompt = np.zeros((B, SEQ, D), np.float32)
    gla_sp = np.zeros((1, B, 4, 128, 256), np.float32)
    k_p = np.zeros((1, B, SEQ + 16, 8, 128), np.float32)
    v_p = np.zeros((1, B, SEQ + 16, 8, 128), np.float32)
    for c in range(ncores):
        r = res[c]
        yt = r["yT"].transpose(1, 0, 2).reshape(D, NT)
        y_prompt[c] = yt[:, C0P:].T
        gla_sp[0, c] = r["gla_sp"]
        kt_ = r["kT_rows"]
        kcols = np.concatenate([kt_[:, :, 0:16], kt_[:, :, C0P:]], axis=2)
        k_p[0, c] = kcols.transpose(2, 1, 0)
        vr = r["v_rows"]
        v_p[0, c] = np.concatenate([vr[0:16], vr[C0P:]], axis=0).reshape(SEQ + 16, 8, 128)
    r0 = res[0]
    gla_ss = np.ascontiguousarray(r0["gla_ss"]).reshape(1, 32, 4, 128, 256)
    k_s = np.ascontiguousarray(r0["kT_rows"][:, :, 16:48].transpose(2, 1, 0)).reshape(1, 32, 1, 8, 128)
    v_s = np.ascontiguousarray(r0["v_rows"][16:48]).reshape(1, 32, 1, 8, 128)
    og_all = np.zeros((128, 8, 32), np.float32)
    for c in range(ncores):
        pair, half = c // 2, c % 2
        og_all[:, 2 * pair:2 * pair + 2, half * 16:half * 16 + 16] = res[c]["ogs"]
    m2 = dict(tail_shared)
    m2["xT"] = np.ascontiguousarray(r0["xs_mid"])
    m2["og_all"] = og_all
    nc2 = build(1, npool, False, tail=True)
    res2 = run_bass_kernel_spmd(nc2, [m2], core_ids=[0]).results[0]
    y_sample = np.ascontiguousarray(res2["yT"].transpose(1, 0, 2).reshape(D, 32).T).reshape(32, 1, D)
    return (y_prompt, y_sample, gla_sp, gla_ss, k_p, v_p, k_s, v_s)
```
